# Optimizing a Trainium2 kernel written in Bass

```python
import math
import jax, jax.numpy as jnp
from jax import lax
import numpy as np

D_MODEL = 1024
BATCH = 32
SEQ = 256
DEPTH = 2
DEC_BATCH = 2
DEC_SEQ = 1024
PAST_LEN = 512

GRID_W = 64
NA_HEADS = 8
HEAD_DIM = 64
NA_WIDTH = NA_HEADS * HEAD_DIM
WIN_R = 8
WIN_C = 16
S5_WIDTH = D_MODEL - NA_WIDTH
S5_GROUP = 16
S5_GROUPS = S5_WIDTH // S5_GROUP
S5_STATE = 64
FNET_GROUPS = 4
FNET_GROUP_WIDTH = D_MODEL // FNET_GROUPS
D_FF = -(-8 * D_MODEL // (3 * 256)) * 256
N_EVEN = (DEPTH + 1) // 2
N_ODD = DEPTH // 2
Q_BLOCK = 128
EPS = 1e-6

kernel_name = 'hybrid_natten_s5_fnet_diffusion_step'


def _rmsnorm(x, g):
    xf = x.astype(jnp.float32)
    y = xf * lax.rsqrt(jnp.mean(xf * xf, axis=-1, keepdims=True) + EPS)
    return (y * g.astype(jnp.float32)).astype(x.dtype)


def _modulation(cond, w_mod, b_mod):
    m = jax.nn.silu(cond) @ w_mod + b_mod
    return [t[:, None, :] for t in jnp.split(m, 6, axis=-1)]


def _modulate(h, shift, scale):
    return h * (1 + scale) + shift


def _swiglu(h, wg, wu, wd):
    return (jax.nn.silu(h @ wg) * (h @ wu)) @ wd


def _even_proj(h, w_in):
    B, L, _ = h.shape
    z = h @ w_in
    q, k, v, u = jnp.split(z, [NA_WIDTH, 2 * NA_WIDTH, 3 * NA_WIDTH], axis=-1)
    shp = (B, L, NA_HEADS, HEAD_DIM)
    return q.reshape(shp), k.reshape(shp), v.reshape(shp), u


def _ctx_attention(q, k, v):
    B, L, H, Dh = q.shape
    nb = L // Q_BLOCK
    scale = Dh ** -0.5
    qb = q.reshape(B, nb, Q_BLOCK, H, Dh).transpose(1, 0, 2, 3, 4)

    def block(qi):
        s = jnp.einsum('bqhd,bkhd->bhqk', qi, k).astype(jnp.float32) * scale
        p = jax.nn.softmax(s, axis=-1).astype(v.dtype)
        return jnp.einsum('bhqk,bkhd->bqhd', p, v)

    o = lax.map(block, qb)
    return o.transpose(1, 0, 2, 3, 4).reshape(B, L, H, Dh)


def _latent_na_attention(q, k, v, ck, cv, rpb):
    B, T, H, Dh = q.shape
    rows = T // GRID_W
    kr = min(WIN_R, rows)
    kc = WIN_C
    scale = Dh ** -0.5
    qg = q.reshape(B, rows, GRID_W, H, Dh).transpose(1, 0, 2, 3, 4)
    kg = k.reshape(B, rows, GRID_W, H, Dh)
    vg = v.reshape(B, rows, GRID_W, H, Dh)
    cols = np.arange(GRID_W)
    col_start = np.clip(cols - kc // 2, 0, GRID_W - kc)
    col_idx = col_start[:, None] + np.arange(kc)[None, :]
    dc_idx = col_idx - cols[:, None] + (WIN_C - 1)

    def row_block(args):
        r, qr = args
        rs = jnp.clip(r - kr // 2, 0, rows - kr)
        kband = lax.dynamic_slice_in_dim(kg, rs, kr, axis=1)
        vband = lax.dynamic_slice_in_dim(vg, rs, kr, axis=1)
        kwin = kband[:, :, col_idx]
        vwin = vband[:, :, col_idx]
        dr_idx = rs + jnp.arange(kr) - r + (WIN_R - 1)
        bias = rpb[:, dr_idx][:, :, dc_idx]
        bias = bias.transpose(0, 2, 1, 3).astype(jnp.float32)
        s_loc = jnp.einsum('bqhd,biqjhd->bhqij', qr, kwin).astype(jnp.float32) * scale + bias[None]
        s_ctx = jnp.einsum('bqhd,blhd->bhql', qr, ck).astype(jnp.float32) * scale
        s = jnp.concatenate([s_loc.reshape(B, H, GRID_W, kr * kc), s_ctx], axis=-1)
        p = jax.nn.softmax(s, axis=-1).astype(v.dtype)
        p_loc = p[..., :kr * kc].reshape(B, H, GRID_W, kr, kc)
        p_ctx = p[..., kr * kc:]
        return (jnp.einsum('bhqij,biqjhd->bqhd', p_loc, vwin)
                + jnp.einsum('bhql,blhd->bqhd', p_ctx, cv))

    o = lax.map(row_block, (jnp.arange(rows), qg))
    return o.transpose(1, 0, 2, 3, 4).reshape(B, T, H, Dh)


def _scan_combine(e1, e2):
    a1r, a1i, b1r, b1i = e1
    a2r, a2i, b2r, b2i = e2
    return (a2r * a1r - a2i * a1i,
            a2r * a1i + a2i * a1r,
            a2r * b1r - a2i * b1i + b2r,
            a2r * b1i + a2i * b1r + b2i)


def _s5_direction(u, lam_re, lam_im, log_step, b_re, b_im, c_re, c_im, s0_re, s0_im, reverse):
    f32 = jnp.float32
    lr = jnp.minimum(lam_re.astype(f32), -1e-4)
    li = lam_im.astype(f32)
    dt = jnp.exp(log_step.astype(f32))[:, None]
    mag = jnp.exp(lr * dt)
    ar = mag * jnp.cos(li * dt)
    ai = mag * jnp.sin(li * dt)
    den = lr * lr + li * li
    nr = ar - 1.0
    ni = ai
    cr = (nr * lr + ni * li) / den
    ci = (ni * lr - nr * li) / den
    br_, bi_ = b_re.astype(f32), b_im.astype(f32)
    bbr = cr[..., None] * br_ - ci[..., None] * bi_
    bbi = cr[..., None] * bi_ + ci[..., None] * br_
    bur = jnp.einsum('blgh,gph->blgp', u, bbr)
    bui = jnp.einsum('blgh,gph->blgp', u, bbi)
    L = u.shape[1]
    first = L - 1 if reverse else 0
    last = 0 if reverse else L - 1
    s0r = s0_re.astype(f32)
    s0i = s0_im.astype(f32)
    bur = bur.at[:, first].add(ar * s0r - ai * s0i)
    bui = bui.at[:, first].add(ar * s0i + ai * s0r)
    Ar = jnp.broadcast_to(ar, bur.shape)
    Ai = jnp.broadcast_to(ai, bur.shape)
    _, _, xr, xi = lax.associative_scan(_scan_combine, (Ar, Ai, bur, bui), axis=1, reverse=reverse)
    y = (jnp.einsum('blgp,ghp->blgh', xr, c_re.astype(f32))
         - jnp.einsum('blgp,ghp->blgh', xi, c_im.astype(f32)))
    return y, xr[:, last], xi[:, last]


def _s5_mixer(u, s0, lam_re, lam_im, log_step, b_re, b_im, c_re, c_im, d, glu_w, glu_b):
    B, L, _ = u.shape
    uf = u.astype(jnp.float32).reshape(B, L, S5_GROUPS, S5_GROUP)
    yf, fr, fi = _s5_direction(uf, lam_re[0], lam_im[0], log_step[0], b_re[0], b_im[0],
                               c_re[0], c_im[0], s0[:, 0, 0], s0[:, 0, 1], False)
    yb, br, bi = _s5_direction(uf, lam_re[1], lam_im[1], log_step[1], b_re[1], b_im[1],
                               c_re[1], c_im[1], s0[:, 1, 0], s0[:, 1, 1], True)
    y = (yf + yb).reshape(B, L, S5_WIDTH) + d.astype(jnp.float32) * uf.reshape(B, L, S5_WIDTH)
    y = jax.nn.gelu(y)
    y = y * jax.nn.sigmoid(y @ glu_w.astype(jnp.float32) + glu_b.astype(jnp.float32))
    state = jnp.stack([jnp.stack([fr, fi], axis=1), jnp.stack([br, bi], axis=1)], axis=1)
    return y.astype(u.dtype), state


def _fourier_mixer(h, w_in, w_out):
    B, L, _ = h.shape
    z = (h @ w_in).astype(jnp.float32).reshape(B, L, FNET_GROUPS, FNET_GROUP_WIDTH)
    f = jnp.fft.fft2(z, axes=(1, 3), norm='ortho').real
    return f.reshape(B, L, D_MODEL).astype(h.dtype) @ w_out


def setup_inputs(seed: int = 0) -> dict:
    key = jax.random.key(seed)
    ks = iter(jax.random.split(key, 32))
    f32 = jnp.float32

    def nrm(shape, scale):
        return jax.random.normal(next(ks), shape, f32) * scale

    dS = D_MODEL ** -0.5
    inp = {}
    inp['x_prompt'] = nrm((BATCH, SEQ, D_MODEL), 1.0)
    inp['x_sample'] = nrm((DEC_BATCH, DEC_SEQ, D_MODEL), 1.0)
    inp['cache_na_k'] = nrm((DEC_BATCH, N_EVEN, PAST_LEN, NA_HEADS, HEAD_DIM), 1.0)
    inp['cache_na_v'] = nrm((DEC_BATCH, N_EVEN, PAST_LEN, NA_HEADS, HEAD_DIM), 1.0)
    inp['state_s5'] = nrm((DEC_BATCH, N_EVEN, 2, 2, S5_GROUPS, S5_STATE), 0.5)
    inp['c'] = nrm((DEC_BATCH, D_MODEL), 1.0)
    inp['c_ctx'] = nrm((D_MODEL,), 1.0)
    inp['w_mod'] = nrm((DEPTH, D_MODEL, 6 * D_MODEL), 0.5 * dS)
    inp['b_mod'] = nrm((DEPTH, 6 * D_MODEL), 0.01)
    inp['norm_mix_g'] = 1.0 + nrm((DEPTH, D_MODEL), 0.01)
    inp['norm_ffn_g'] = 1.0 + nrm((DEPTH, D_MODEL), 0.01)
    inp['w_in_even'] = nrm((N_EVEN, D_MODEL, 3 * NA_WIDTH + S5_WIDTH), dS)
    inp['na_rpb'] = nrm((N_EVEN, NA_HEADS, 2 * WIN_R - 1, 2 * WIN_C - 1), 0.1)
    inp['s5_lam_re'] = -0.5 + nrm((N_EVEN, 2, S5_GROUPS, S5_STATE), 0.01)
    inp['s5_lam_im'] = jnp.pi * jnp.arange(S5_STATE, dtype=f32) + nrm((N_EVEN, 2, S5_GROUPS, S5_STATE), 0.01)
    inp['s5_log_step'] = jax.random.uniform(next(ks), (N_EVEN, 2, S5_GROUPS), f32,
                                            minval=math.log(0.001), maxval=math.log(0.1))
    inp['s5_b_re'] = nrm((N_EVEN, 2, S5_GROUPS, S5_STATE, S5_GROUP), (2 * S5_GROUP) ** -0.5)
    inp['s5_b_im'] = nrm((N_EVEN, 2, S5_GROUPS, S5_STATE, S5_GROUP), (2 * S5_GROUP) ** -0.5)
    inp['s5_c_re'] = nrm((N_EVEN, 2, S5_GROUPS, S5_GROUP, S5_STATE), (2 * S5_STATE) ** -0.5)
    inp['s5_c_im'] = nrm((N_EVEN, 2, S5_GROUPS, S5_GROUP, S5_STATE), (2 * S5_STATE) ** -0.5)
    inp['s5_d'] = nrm((N_EVEN, S5_WIDTH), 1.0)
    inp['s5_glu_w'] = nrm((N_EVEN, S5_WIDTH, S5_WIDTH), S5_WIDTH ** -0.5)
    inp['s5_glu_b'] = nrm((N_EVEN, S5_WIDTH), 0.01)
    inp['w_out_even'] = nrm((N_EVEN, D_MODEL, D_MODEL), dS)
    inp['w_in_odd'] = nrm((N_ODD, D_MODEL, D_MODEL), dS)
    inp['w_out_odd'] = nrm((N_ODD, D_MODEL, D_MODEL), dS)
    inp['ffn_w_gate'] = nrm((DEPTH, D_MODEL, D_FF), dS)
    inp['ffn_w_up'] = nrm((DEPTH, D_MODEL, D_FF), dS)
    inp['ffn_w_down'] = nrm((DEPTH, D_FF, D_MODEL), D_FF ** -0.5)
    inp['final_norm_g'] = 1.0 + nrm((D_MODEL,), 0.01)
    return inp


def reference(x_prompt, x_sample, cache_na_k, cache_na_v, state_s5, c, c_ctx,
              w_mod, b_mod, norm_mix_g, norm_ffn_g, w_in_even, na_rpb,
              s5_lam_re, s5_lam_im, s5_log_step, s5_b_re, s5_b_im, s5_c_re, s5_c_im,
              s5_d, s5_glu_w, s5_glu_b, w_out_even, w_in_odd, w_out_odd,
              ffn_w_gate, ffn_w_up, ffn_w_down, final_norm_g):
    xp = x_prompt
    xs = x_sample
    Bp, Lp, _ = xp.shape
    Bs, Ls, _ = xs.shape
    new_k, new_v, new_s = [], [], []
    for layer in range(DEPTH):
        mp = _modulation(c_ctx[None], w_mod[layer], b_mod[layer])
        ms = _modulation(c, w_mod[layer], b_mod[layer])
        hp = _modulate(_rmsnorm(xp, norm_mix_g[layer]), mp[0], mp[1])
        hs = _modulate(_rmsnorm(xs, norm_mix_g[layer]), ms[0], ms[1])
        if layer % 2 == 0:
            e = layer // 2
            s5p = (s5_lam_re[e], s5_lam_im[e], s5_log_step[e], s5_b_re[e], s5_b_im[e],
                   s5_c_re[e], s5_c_im[e], s5_d[e], s5_glu_w[e], s5_glu_b[e])
            qp, kp, vp, up = _even_proj(hp, w_in_even[e])
            ap = _ctx_attention(qp, kp, vp)
            s0 = jnp.zeros((Bp, 2, 2, S5_GROUPS, S5_STATE), jnp.float32)
            yp, stp = _s5_mixer(up, s0, *s5p)
            op = jnp.concatenate([ap.reshape(Bp, Lp, NA_WIDTH), yp], axis=-1) @ w_out_even[e]
            new_k.append(kp)
            new_v.append(vp)
            new_s.append(stp)
            qs, ks_, vs, us = _even_proj(hs, w_in_even[e])
            as_ = _latent_na_attention(qs, ks_, vs, cache_na_k[:, e], cache_na_v[:, e], na_rpb[e])
            ys, _ = _s5_mixer(us, state_s5[:, e].astype(jnp.float32), *s5p)
            os_ = jnp.concatenate([as_.reshape(Bs, Ls, NA_WIDTH), ys], axis=-1) @ w_out_even[e]
        else:
            o = layer // 2
            op = _fourier_mixer(hp, w_in_odd[o], w_out_odd[o])
            os_ = _fourier_mixer(hs, w_in_odd[o], w_out_odd[o])
        xp = xp + mp[2] * op
        xs = xs + ms[2] * os_
        hp = _modulate(_rmsnorm(xp, norm_ffn_g[layer]), mp[3], mp[4])
        hs = _modulate(_rmsnorm(xs, norm_ffn_g[layer]), ms[3], ms[4])
        xp = xp + mp[5] * _swiglu(hp, ffn_w_gate[layer], ffn_w_up[layer], ffn_w_down[layer])
        xs = xs + ms[5] * _swiglu(hs, ffn_w_gate[layer], ffn_w_up[layer], ffn_w_down[layer])
    y_prompt = _rmsnorm(xp, final_norm_g)
    y_sample = _rmsnorm(xs, final_norm_g)
    new_cache_na_k = jnp.stack(new_k, axis=1)
    new_cache_na_v = jnp.stack(new_v, axis=1)
    new_state_s5 = jnp.stack(new_s, axis=1)
    return (y_prompt, y_sample, new_cache_na_k, new_cache_na_v, new_state_s5)
```

```python
import numpy as np
import ml_dtypes
from contextlib import ExitStack
import concourse.bass as bass
import concourse.mybir as mybir
from concourse.bass_utils import run_bass_kernel_spmd

F32 = mybir.dt.float32
BF16 = mybir.dt.bfloat16
I32 = mybir.dt.int32
AF = mybir.ActivationFunctionType
ALU = mybir.AluOpType

_DTSIZE = {F32: 4, BF16: 2, I32: 4}

NT = 1280
TB = [(0, 512), (512, 512), (1024, 256)]
CR = [(0, 1024, 0), (1024, 256, 1)]
D = 1024
DFF = 2816
NEG = -30000.0

CFG = dict(attn=True, s5=True, fnet=True, ffn=True, dbg=False, att=9)


def _region(ap):
    t = ap.tensor
    name = t.name
    space = str(ap.space).upper()
    pat = ap.ap
    esz = _DTSIZE[ap.dtype]
    off = int(ap.offset)
    if 'DRAM' in space or 'HBM' in space:
        lo = hi = off
        for st, cn in pat:
            if st >= 0:
                hi += st * (cn - 1)
            else:
                lo += st * (cn - 1)
        return ('D:' + name, 0, 1, lo * esz, (hi + 1) * esz)
    pstep, pcnt = pat[0]
    p0 = off // pstep if pstep else 0
    foff = off - p0 * pstep if pstep else off
    lo = hi = foff
    for st, cn in pat[1:]:
        if st >= 0:
            hi += st * (cn - 1)
        else:
            lo += st * (cn - 1)
    if 'PSUM' in space:
        b0 = (lo * esz) // 2048
        b1 = ((hi + 1) * esz - 1) // 2048
        return ('P:' + name, 0, 128, b0 * 2048, (b1 + 1) * 2048)
    return (name, p0, p0 + pcnt, lo * esz, (hi + 1) * esz)


class Op:
    __slots__ = ('eng', 'fn', 'reads', 'writes', 'dma', 'deps', 'signal', 'sem', 'val', 'idx')


class Sched:
    ENGS = ('pe', 'act', 'dve', 'pool', 'sp')
    NDSEM = 12

    def __init__(self, nc):
        self.nc = nc
        self.ops = []
        self.track_dram = set()

    def add(self, eng, fn, reads=(), writes=(), dma=False):
        op = Op()
        op.eng = eng
        op.fn = fn
        op.reads = [_region(a) for a in reads]
        op.writes = [_region(a) for a in writes]
        op.dma = dma
        op.idx = len(self.ops)
        self.ops.append(op)
        return op

    def matmul(self, out, lhsT, rhs, start=True, stop=True, **kw):
        rd = [lhsT, rhs] + ([] if start else [out])
        return self.add('pe', lambda e: e.matmul(out, lhsT=lhsT, rhs=rhs, start=start, stop=stop, **kw), rd, [out])

    def transpose(self, out, in_, ident):
        return self.add('pe', lambda e: e.transpose(out, in_, ident), [in_, ident], [out])

    def act(self, out, in_, func, bias=None, scale=None, accum_out=None):
        rd = [in_]
        kw = {}
        if bias is not None:
            kw['bias'] = bias
            if not isinstance(bias, (int, float)):
                rd.append(bias)
        if scale is not None:
            kw['scale'] = scale
            if not isinstance(scale, (int, float)):
                rd.append(scale)
        wr = [out]
        if accum_out is not None:
            kw['accum_out'] = accum_out
            wr.append(accum_out)
        return self.add('act', lambda e: e.activation(out=out, in_=in_, func=func, **kw), rd, wr)

    def tt(self, eng, out, in0, in1, op):
        return self.add(eng, lambda e: e.tensor_tensor(out=out, in0=in0, in1=in1, op=op), [in0, in1], [out])

    def ts(self, eng, out, in0, s1, op0, s2=None, op1=None):
        rd = [in0]
        if not isinstance(s1, (int, float)):
            rd.append(s1)
        if s2 is not None and not isinstance(s2, (int, float)):
            rd.append(s2)
        if op1 is None:
            return self.add(eng, lambda e: e.tensor_scalar(out=out, in0=in0, scalar1=s1, scalar2=None, op0=op0),
                            rd, [out])
        return self.add(eng, lambda e: e.tensor_scalar(out=out, in0=in0, scalar1=s1, scalar2=s2, op0=op0, op1=op1),
                        rd, [out])

    def stt(self, out, in0, scalar, in1, op0, op1):
        rd = [in0, in1]
        if not isinstance(scalar, (int, float)):
            rd.append(scalar)
        return self.add('dve', lambda e: e.scalar_tensor_tensor(out=out, in0=in0, scalar=scalar, in1=in1,
                                                                  op0=op0, op1=op1), rd, [out])

    def copy(self, eng, out, in_):
        if eng == 'act':
            return self.add('act', lambda e: e.copy(out=out, in_=in_), [in_], [out])
        return self.add(eng, lambda e: e.tensor_copy(out=out, in_=in_), [in_], [out])

    def recip(self, out, in_):
        return self.add('dve', lambda e: e.reciprocal(out=out, in_=in_), [in_], [out])

    def memset(self, eng, out, val):
        return self.add(eng, lambda e: e.memset(out, val), [], [out])

    def dma(self, eng, out, in_, **kw):
        return self.add(eng, lambda e: e.dma_start(out=out, in_=in_, **kw), [in_], [out], dma=True)

    def _analyze(self):
        live = {}
        ndma = {e: 0 for e in self.ENGS}
        dma_ops = {e: [] for e in self.ENGS}
        ops = self.ops
        for op in ops:
            deps = {}
            for regs, is_write in ((op.reads, False), (op.writes, True)):
                for (nm, p0, p1, b0, b1) in regs:
                    if nm[0:2] == 'D:' and nm not in self.track_dram:
                        continue
                    lst = live.get(nm)
                    if not lst:
                        continue
                    psum = nm[0:2] == 'P:'
                    for (q0, q1, c0, c1, j, w) in lst:
                        if q0 < p1 and p0 < q1 and c0 < b1 and b0 < c1 and j != op.idx:
                            if is_write:
                                kind = 'WAW' if w else 'WAR'
                            else:
                                if not w:
                                    if psum and ops[j].eng != op.eng:
                                        kind = 'RAR'
                                    else:
                                        continue
                                else:
                                    kind = 'RAW'
                            if j not in deps or kind == 'RAW':
                                deps[j] = kind
            for (nm, p0, p1, b0, b1) in op.writes:
                if nm[0:2] == 'D:' and nm not in self.track_dram:
                    continue
                lst = live.setdefault(nm, [])
                lst[:] = [t for t in lst if not (p0 <= t[0] and t[1] <= p1 and b0 <= t[2] and t[3] <= b1)]
                lst.append((p0, p1, b0, b1, op.idx, True))
            for (nm, p0, p1, b0, b1) in op.reads:
                if nm[0:2] == 'D:' and nm not in self.track_dram:
                    continue
                live.setdefault(nm, []).append((p0, p1, b0, b1, op.idx, False))
            if op.dma:
                n = ndma[op.eng]
                if n >= self.NDSEM:
                    deps.setdefault(dma_ops[op.eng][n - self.NDSEM].idx, 'SEM')
                dma_ops[op.eng].append(op)
                ndma[op.eng] = n + 1
            need = []
            best = {}
            for j, kind in deps.items():
                p = ops[j]
                if p.dma:
                    need.append(j)
                    continue
                if p.eng == op.eng and not op.dma:
                    if op.eng == 'pe':
                        continue
                if p.eng not in best or best[p.eng] < j:
                    best[p.eng] = j
            need.extend(best.values())
            op.deps = need
            op.signal = False
        for op in ops:
            for j in op.deps:
                ops[j].signal = True
        self.dma_ops = dma_ops

    def emit(self, es):
        nc = self.nc
        self._analyze()
        csem = {e: es.enter_context(nc.semaphore('c_' + e)) for e in ('pe', 'act', 'dve', 'pool')}
        dsem = {e: [es.enter_context(nc.semaphore('d_%s%d' % (e, i))) for i in range(self.NDSEM)]
                for e in ('sp', 'pool', 'act')}
        cnt = {e: 0 for e in self.ENGS}
        nd = {e: 0 for e in self.ENGS}
        for op in self.ops:
            if op.dma:
                n = nd[op.eng]
                nd[op.eng] = n + 1
                op.sem = dsem[op.eng][n % self.NDSEM]
                op.val = 16 * (n // self.NDSEM + 1)
                op.signal = True
            elif op.signal:
                cnt[op.eng] += 1
                op.sem = csem[op.eng]
                op.val = cnt[op.eng]
        self.stats = {e: (len([o for o in self.ops if o.eng == e]), cnt[e], nd[e]) for e in self.ENGS}
        block = es.enter_context(nc.Block())
        per = {e: [op for op in self.ops if op.eng == e] for e in self.ENGS}
        ops = self.ops
        dma_ops = self.dma_ops

        def run(engname, e):
            waited = {}
            for op in per[engname]:
                for j in op.deps:
                    p = ops[j]
                    key = id(p.sem)
                    if waited.get(key, 0) < p.val:
                        e.wait_ge(p.sem, p.val)
                        waited[key] = p.val
                inst = op.fn(e)
                if op.signal:
                    inst.then_inc(op.sem, 16 if op.dma else 1)
            for op in dma_ops[engname]:
                key = id(op.sem)
                if waited.get(key, 0) < op.val:
                    e.wait_ge(op.sem, op.val)
                    waited[key] = op.val

        @block.tensor
        def _(e):
            run('pe', e)

        @block.scalar
        def _(e):
            run('act', e)

        @block.vector
        def _(e):
            run('dve', e)

        @block.gpsimd
        def _(e):
            run('pool', e)

        @block.sync
        def _(e):
            run('sp', e)


def _ap(base, off, pat):
    return bass.AP(base.tensor, int(base.offset) + off, [list(x) for x in pat])


class Prog:
    def __init__(self, cfg):
        self.cfg = cfg
        self.nc = bass.Bass("TRN2", target_bir_lowering=False)
        self.es = ExitStack()
        self.S = Sched(self.nc)
        self.ins = {}
        self.outs = {}
        self.wslot_i = 0
        self.bank_i = 0

    def din(self, name, shape, dt=F32):
        a = self.nc.dram_tensor(name, list(shape), dt, kind="ExternalInput").ap()
        self.ins[name] = a
        return a

    def dout(self, name, shape, dt=F32):
        a = self.nc.dram_tensor(name, list(shape), dt, kind="ExternalOutput").ap()
        self.outs[name] = a
        return a

    def sb(self, name, shape, dt):
        return self.es.enter_context(self.nc.sbuf_tensor(name, list(shape), dt))

    def wload(self, src3):
        kc, mw = src3.shape[1], src3.shape[2]
        assert kc * mw <= self.WELEMS, (kc, mw)
        slot = self.wslots[self.wslot_i % getattr(self, 'NW_eff', self.NW)]
        self.wslot_i += 1
        v = slot[:, 0:kc * mw].rearrange("p (k m) -> p k m", m=mw)
        self.S.dma('pool', v, src3)
        return v

    def bank(self, lo=0, hi=4):
        b = lo + (self.bank_i % (hi - lo))
        self.bank_i += 1
        return b


def build_program(cfg):
    P = Prog(cfg)
    nc, S, es = P.nc, P.S, P.es
    din, dout, sb = P.din, P.dout, P.sb

    xin = din("xin", [NT, D])
    condT_d = din("condT", [128, 8, 2])
    bmodT_d = din("bmodT", [128, 2, 48])
    gvec_d = din("gvec", [128, 5, 8])
    ident_d = din("ident", [128, 128])
    w_mod = din("w_mod", [2, D, 6 * D])
    wg_d = din("ffn_w_gate", [2, D, DFF])
    wu_d = din("ffn_w_up", [2, D, DFF])
    wd_d = din("ffn_w_down", [2, DFF, D])
    y_d = dout("y", [NT, D])
    w_in_odd = din("w_in_odd", [D, D])
    w_in_even = din("w_in_even", [D, 2048])
    w_out_even = din("w_out_even", [D, D])
    ctxkT_d = din("ctxkT", [512, 512])
    ctxv_d = din("ctxv", [512, 512])
    ctxbias_d = din("ctxbias", [128, 1])
    maskt_d = din("maskt", [128, 40, 128], BF16)
    rpbH_d = din("rpbH", [8, 19, 128])
    sA_lam_d = din("sA_lam", [128, 2, 4, 2, 64])
    sA_ls_d = din("sA_ls", [128, 4, 2])
    sA_b_d = din("sA_b", [128, 2, 4, 2, 64])
    parA_d = din("parA", [128, 2])
    sB_lam_d = din("sB_lam", [128, 2, 32])
    sB_ls_d = din("sB_ls", [128, 32])
    sB_c_d = din("sB_c", [128, 2, 32, 16])
    sB_b_d = din("sB_b", [128, 2, 32, 16])
    s0B_d = din("s0B", [128, 64])
    carry_d = din("carry", [128, 1])
    sdT_d = din("sdT", [128, 4])
    glubT_d = din("glubT", [128, 4])
    glu_w_d = din("s5_glu_w", [512, 512])
    so_d = dout("so", [5, 128, 64])
    ko_d = dout("ko", [NT, 512])
    vo_d = dout("vo", [NT, 512])
    w_out_odd = din("w_out_odd", [D, D])
    cs256_d = din("cs256", [256, 512], BF16)
    dftL_d = din("dftL", [2, 1024, 1024], BF16)
    dft4_d = din("dft4", [2, 256, 256], BF16)

    xT = sb("xT", [128, 8, NT], F32)
    rstd = sb("rstd", [128, NT], F32)
    ntmp = sb("ntmp", [128, 2, 512], F32)
    ident_bf = sb("ident_bf", [128, 128], BF16)
    ctxbias = sb("ctxbias_s", [128, 1], F32)
    sbm = sb("sbm", [128, 1280], F32)
    ident = sb("ident_s", [128, 128], F32)
    ones_bf = sb("ones_bf", [128, 128], BF16)
    condT = sb("condT_s", [128, 8, 2], F32)
    scT = sb("scT", [128, 8, 2], BF16)
    bmodT = sb("bmodT_s", [128, 2, 48], F32)
    gvec = sb("gvec_s", [128, 5, 8], F32)
    modT = sb("modT", [128, 2, 48, 2], F32)
    gs = sb("gs", [128, 4, 8, 2], F32)
    P.NW = 5
    P.WELEMS = 4096
    wsl = sb("wsl", [128, P.NW * P.WELEMS + 768], BF16)
    P.wslots = [wsl[:, i * P.WELEMS:(i + 1) * P.WELEMS] for i in range(P.NW)]
    ARENA_BYTES = 111104
    arena = sb("arena", [128, ARENA_BYTES // 2], BF16)
    PS = es.enter_context(nc.psum_tensor("PS", [128, 8, 512], F32))
    PSM = PS[:, 7, 0:256].rearrange("p (l c) -> p l c", l=2)

    def aview(off_bytes, shape, dt):
        esz = _DTSIZE[dt]
        n = int(np.prod(shape[1:]))
        assert off_bytes % 4 == 0 and off_bytes + n * esz <= ARENA_BYTES, (off_bytes, shape)
        a = arena[:, off_bytes // 2: off_bytes // 2 + n * esz // 2]
        if dt != BF16:
            a = a.bitcast(dt)
        if len(shape) == 2:
            return a
        names = " ".join("d%d" % i for i in range(len(shape) - 1))
        kw = {"d%d" % i: shape[i + 1] for i in range(len(shape) - 1)}
        return a.rearrange("p (%s) -> p %s" % (names, names), **kw)

    S.dma('sp', ident[:], ident_d)
    S.dma('sp', condT[:], condT_d)
    S.dma('sp', bmodT[:], bmodT_d)
    S.dma('sp', gvec[:], gvec_d)
    S.memset('dve', ones_bf[:], 1.0)
    S.copy('dve', ident_bf[:], ident[:])
    S.dma('sp', ctxbias[:], ctxbias_d)
    S.act(scT[:], condT[:], AF.Silu)

    def modulation_gen(layer):
        for u in range(12):
            W = P.wload(w_mod[layer][:, u * 512:(u + 1) * 512].rearrange("(k p) m -> p k m", p=128))
            for mt in range(4):
                col = (u * 4 + mt) * 2
                for kc in range(8):
                    S.matmul(PSM[:, layer, col:col + 2], W[:, kc, mt * 128:(mt + 1) * 128], scT[:, kc, :],
                             start=(kc == 0), stop=(kc == 7))
            yield u
        src = PSM[:, layer, 0:96].rearrange("p (m c) -> p m c", c=2)
        b_ = bmodT[:, layer, :]
        bb = _ap(b_, 0, [b_.ap[0], [1, 48], [0, 2]])
        S.tt('dve', modT[:, layer, :, :], src, bb, ALU.add)
        for which in range(2):
            sc = modT[:, layer, (1 + 3 * which) * 8:(2 + 3 * which) * 8, :]
            g_ = gvec[:, layer * 2 + which, :]
            gb = _ap(g_, 0, [g_.ap[0], [1, 8], [0, 2]])
            S.stt(gs[:, layer * 2 + which, :, :], sc, 1.0, gb, ALU.add, ALU.mult)

    def modulation(layer):
        for _ in modulation_gen(layer):
            pass

    def mod_vec(layer, j, c, ci):
        return modT[:, layer, j * 8 + c, ci:ci + 1]

    hT = aview(0, [128, 8, NT], BF16)
    sq = aview(20480, [128, 8, NT], BF16)

    def sumsq_rstd_tb(s, n):
        S.act(sq[:, :, s:s + n], xT[:, :, s:s + n], AF.Square)
        b = P.bank(0, 4)
        for c in range(8):
            S.matmul(PS[:, b, 0:n], ones_bf[:], sq[:, c, s:s + n], start=(c == 0), stop=(c == 7))
        S.act(rstd[:, s:s + n], PS[:, b, 0:n], AF.Sqrt, bias=epsb[:, 0:1], scale=1.0 / D)
        S.recip(rstd[:, s:s + n], rstd[:, s:s + n])

    def sumsq_rstd():
        for (s, n) in TB:
            sumsq_rstd_tb(s, n)

    def norm_mod(layer, which):
        k = 0
        for ti, (s, n) in enumerate(TB):
            sumsq_rstd_tb(s, n)
        for ti, (s, n) in enumerate(TB):
            ci = 0 if ti < 2 else 1
            for c in range(8):
                t = ntmp[:, k % 2, 0:n]
                k += 1
                S.tt('dve', t, xT[:, c, s:s + n], rstd[:, s:s + n], ALU.mult)
                S.act(hT[:, c, s:s + n], t, AF.Identity,
                      bias=mod_vec(layer, 3 * which, c, ci), scale=gs[:, layer * 2 + which, c, ci:ci + 1])

    epsb = sb("epsb", [128, 1], F32)
    S.memset('dve', epsb[:], 1e-6)

    aT = aview(20480, [128, 22, NT], BF16)
    sgt = [aview(76800, [128, 512], BF16), aview(77824, [128, 512], BF16)]

    def ffn(layer):
        groups = [(i * 512, 512) for i in range(5)] + [(2560, 256)]
        k = 0
        for (m0, mw) in groups:
            Wg = P.wload(wg_d[layer][:, m0:m0 + mw].rearrange("(k p) m -> p k m", p=128))
            Wu = P.wload(wu_d[layer][:, m0:m0 + mw].rearrange("(k p) m -> p k m", p=128))
            for mt in range(mw // 128):
                j = m0 // 128 + mt
                for (s, n) in TB:
                    bg = P.bank(0, 6)
                    bu = P.bank(0, 6)
                    for kc in range(8):
                        S.matmul(PS[:, bg, 0:n], Wg[:, kc, mt * 128:(mt + 1) * 128], hT[:, kc, s:s + n],
                                 start=(kc == 0), stop=(kc == 7))
                    for kc in range(8):
                        S.matmul(PS[:, bu, 0:n], Wu[:, kc, mt * 128:(mt + 1) * 128], hT[:, kc, s:s + n],
                                 start=(kc == 0), stop=(kc == 7))
                    t = sgt[k % 2][:, 0:n]
                    k += 1
                    S.act(t, PS[:, bg, 0:n], AF.Silu)
                    S.tt('dve', aT[:, j, s:s + n], t, PS[:, bu, 0:n], ALU.mult)
        for mg in range(4):
            Wd = [P.wload(wd_d[layer][kh * 1408:(kh + 1) * 1408, mg * 256:(mg + 1) * 256]
                          .rearrange("(k p) m -> p k m", p=128)) for kh in range(2)]
            for mt in range(2):
                m = mg * 2 + mt
                for ti, (s, n) in enumerate(TB):
                    b = P.bank(0, 6)
                    for kk in range(22):
                        S.matmul(PS[:, b, 0:n], Wd[kk // 11][:, kk % 11, mt * 128:(mt + 1) * 128],
                                 aT[:, kk, s:s + n], start=(kk == 0), stop=(kk == 21))
                    ci = 0 if ti < 2 else 1
                    S.stt(xT[:, m, s:s + n], PS[:, b, 0:n], mod_vec(layer, 5, m, ci), xT[:, m, s:s + n],
                          ALU.mult, ALU.add)

    def evac(i, out, in_):
        S.copy('act' if i % 2 == 0 else 'dve', out, in_)

    def fnet_mixer(layer):
        zT = aview(20480, [128, 8, NT], BF16)
        ZCS = aview(40960, [128, 10, 4, 512], BF16)
        fT = aview(0, [128, 8, NT], BF16)
        cs256 = aview(81920, [128, 2, 512], BF16)
        dft4 = aview(83968, [128, 2, 2, 256], BF16)
        S.dma('sp', cs256, cs256_d.rearrange("(c p) m -> p c m", p=128))
        S.dma('sp', dft4, dft4_d.rearrange("a (t p) k -> p a t k", p=128))
        norm_mod(layer, 0)
        k = 0
        for u in range(2):
            W = P.wload(w_in_odd[:, u * 512:(u + 1) * 512].rearrange("(k p) m -> p k m", p=128))
            for mt in range(4):
                m = u * 4 + mt
                for (s_, n) in TB:
                    b = P.bank(0, 6)
                    for kc in range(8):
                        S.matmul(PS[:, b, 0:n], W[:, kc, mt * 128:(mt + 1) * 128], hT[:, kc, s_:s_ + n],
                                 start=(kc == 0), stop=(kc == 7))
                    evac(k, zT[:, m, s_:s_ + n], PS[:, b, 0:n]); k += 1
        for tt in range(10):
            for gq in range(4):
                b = P.bank(0, 6)
                for cc in range(2):
                    S.matmul(PS[:, b, :], zT[:, 2 * gq + cc, tt * 128:(tt + 1) * 128], cs256[:, cc, :],
                             start=(cc == 0), stop=(cc == 1))
                evac(k, ZCS[:, tt, gq, :], PS[:, b, :]); k += 1
        for kb in range(2):
            CL = P.wload(dftL_d[0][:, kb * 512:(kb + 1) * 512].rearrange("(t p) k -> p t k", p=128))
            SL = P.wload(dftL_d[1][:, kb * 512:(kb + 1) * 512].rearrange("(t p) k -> p t k", p=128))
            for m in range(8):
                gq, half = m // 2, m % 2
                b = P.bank(0, 6)
                for tt in range(8):
                    S.matmul(PS[:, b, :], ZCS[:, tt, gq, half * 128:(half + 1) * 128], CL[:, tt, :],
                             start=(tt == 0), stop=False)
                for tt in range(8):
                    S.matmul(PS[:, b, :], ZCS[:, tt, gq, 256 + half * 128:256 + (half + 1) * 128], SL[:, tt, :],
                             start=False, stop=(tt == 7))
                evac(k, fT[:, m, kb * 512:(kb + 1) * 512], PS[:, b, :]); k += 1
        for m in range(8):
            gq, half = m // 2, m % 2
            b = P.bank(0, 6)
            i = 0
            for a in range(2):
                for t in range(2):
                    S.matmul(PS[:, b, 0:256], ZCS[:, 8 + t, gq, a * 256 + half * 128:a * 256 + (half + 1) * 128],
                             dft4[:, a, t, :], start=(i == 0), stop=(i == 3))
                    i += 1
            evac(k, fT[:, m, 1024:1280], PS[:, b, 0:256]); k += 1
        for u in range(2):
            W = P.wload(w_out_odd[:, u * 512:(u + 1) * 512].rearrange("(k p) m -> p k m", p=128))
            for mt in range(4):
                m = u * 4 + mt
                for ti, (s_, n) in enumerate(TB):
                    b = P.bank(0, 6)
                    for kc in range(8):
                        S.matmul(PS[:, b, 0:n], W[:, kc, mt * 128:(mt + 1) * 128], fT[:, kc, s_:s_ + n],
                                 start=(kc == 0), stop=(kc == 7))
                    ci = 0 if ti < 2 else 1
                    S.stt(xT[:, m, s_:s_ + n], PS[:, b, 0:n], mod_vec(layer, 2, m, ci), xT[:, m, s_:s_ + n],
                          ALU.mult, ALU.add)

    KB = [[0, 1, 2, 3], [0, 1, 2, 3], [0, 1, 2, 3, 4], [1, 2, 3, 4, 5], [2, 3, 4, 5, 6], [3, 4, 5, 6, 7],
          [4, 5, 6, 7], [4, 5, 6, 7]]
    qT = aview(40960, [128, 4, NT], BF16)
    kT = aview(51200, [128, 4, NT], BF16)
    uTp = aview(61440, [128, 4, 8, 160], BF16)
    vtok = aview(71680, [128, 10, 512], BF16)
    catT = aview(81920, [128, 8, NT], BF16)
    kvst = [aview(102400, [128, 512], F32), aview(104448, [128, 512], F32)]
    rec = aview(106496, [128, 2, 256], F32)
    rstage = [aview(108544, [128, 9, 128], BF16)]

    def pp(n, nb=None):
        for attr, cnt in (('s5prepA', n), ('s5prepB', n if nb is None else nb)):
            for _ in range(cnt):
                g_ = getattr(P, attr, None)
                if g_ is None:
                    break
                try:
                    next(g_)
                except StopIteration:
                    setattr(P, attr, None)

    def projections():
        norm_mod(0, 0)

        def wl(i):
            return P.wload(w_in_even[:, i * 512:(i + 1) * 512].rearrange("(k p) m -> p k m", p=128))
        k = 0
        Wq = wl(0)
        Wk = wl(1)
        for mt in range(4):
            for (s_, n) in TB:
                b = P.bank(0, 6)
                for kc in range(8):
                    S.matmul(PS[:, b, 0:n], Wq[:, kc, mt * 128:(mt + 1) * 128], hT[:, kc, s_:s_ + n],
                             start=(kc == 0), stop=(kc == 7))
                S.act(qT[:, mt, s_:s_ + n], PS[:, b, 0:n], AF.Identity, scale=0.125)
            pp(1)
        Wv = wl(2)
        for mt in range(4):
            for (s_, n) in TB:
                b = P.bank(0, 6)
                for kc in range(8):
                    S.matmul(PS[:, b, 0:n], Wk[:, kc, mt * 128:(mt + 1) * 128], hT[:, kc, s_:s_ + n],
                             start=(kc == 0), stop=(kc == 7))
                S.copy('act', kT[:, mt, s_:s_ + n], PS[:, b, 0:n])
            pp(1)
        for tt in range(10):
            b = P.bank(0, 6)
            for kc in range(8):
                S.matmul(PS[:, b, :], hT[:, kc, tt * 128:(tt + 1) * 128], Wk[:, kc, :],
                         start=(kc == 0), stop=(kc == 7))
            st = kvst[tt % 2]
            S.copy('act', st, PS[:, b, :])
            S.dma('sp', ko_d[tt * 128:(tt + 1) * 128, :], st)
            pp(1)
        Wu = wl(3)
        for tt in range(10):
            b = P.bank(0, 6)
            for kc in range(8):
                S.matmul(PS[:, b, :], hT[:, kc, tt * 128:(tt + 1) * 128], Wv[:, kc, :],
                         start=(kc == 0), stop=(kc == 7))
            st = kvst[tt % 2]
            S.copy('act', st, PS[:, b, :])
            S.copy('act', vtok[:, tt, :], PS[:, b, :])
            S.dma('sp', vo_d[tt * 128:(tt + 1) * 128, :], st)
            pp(1)
        for mt in range(4):
            for (s_, n) in TB:
                b = P.bank(0, 6)
                for kc in range(8):
                    S.matmul(PS[:, b, 0:n], Wu[:, kc, mt * 128:(mt + 1) * 128], hT[:, kc, s_:s_ + n],
                             start=(kc == 0), stop=(kc == 7))
                pv = PS[:, b, 0:n]
                src = _ap(pv, 0, [pv.ap[0], [1, 8], [8, n // 8]])
                dst = uTp[:, mt, :, s_ // 8:(s_ + n) // 8]
                S.copy('act', dst, src)
            pp(1)

    def attention():
        ctxkT = aview(0, [128, 4, 512], BF16)
        ctxv = aview(4096, [128, 4, 512], BF16)
        E = aview(8192, [128, 2, 10, 256], BF16)
        maskt = aview(18432, [128, 40, 128], BF16)
        rpbT = aview(28672, [128, 2, 9, 128], BF16)
        stgs = [rstage[0], aview(103680, [128, 9, 128], BF16)]
        rec2 = aview(106496, [128, 2, 256], F32)
        S.dma('pool', ctxkT, ctxkT_d.rearrange("(c p) k -> p c k", p=128))
        S.dma('pool', ctxv, ctxv_d.rearrange("(j p) f -> p j f", p=128))
        S.dma('sp', maskt, maskt_d)
        UN = [[0, 1, 2, 3], [0, 1, 2, 3, 4, 5], [2, 3, 4, 5, 6, 7], [4, 5, 6, 7]]
        ti0 = [0, 8, 20, 32]

        def build_bias(h):
            stg = stgs[h % 2]
            for krl in range(2):
                src = bass.AP(rpbH_d.tensor, h * 19 * 128 + krl * 128, [[1, 64], [256, 9], [128, 2], [1, 64]])
                dst = stg[krl * 64:(krl + 1) * 64, :, :].rearrange("p a (r c) -> p a r c", c=64)
                S.dma('pool', dst, src)

        its = []
        for j_ in range(4):
            for (hh, mp_) in ((0, 0), (0, 1), (1, 0), (1, 1), (0, 2), (0, 3), (1, 2), (1, 3)):
                its.append((2 * j_ + hh, mp_))

        def build_bm(it):
            h, mp = its[it]
            if mp == 0:
                a_ = stgs[h % 2][:, :, :]
                rev = _ap(a_, 127, [a_.ap[0], [128, 9], [-1, 128]])
                S.copy('act', rpbT[:, h % 2, :, :], rev)
            if mp == 3 and h + 2 < 8:
                build_bias(h + 2)

        def sreg(slot):
            b = slot // 2
            return PS[:, b, (slot % 2) * 256:(slot % 2) * 256 + 256]

        def scores(it):
            h, mp = its[it]
            hp, hc = h % 2, h // 2
            pr = slice(64 * hp, 64 * hp + 64)
            Ju = len(UN[mp])
            qsl = qT[pr, hc, mp * 256:(mp + 1) * 256]
            for jn, n in enumerate(UN[mp]):
                S.matmul(sreg(jn), kT[pr, hc, n * 128:(n + 1) * 128], qsl, start=(jn % 2 == 0), stop=False,
                         skip_group_check=True)
            for j in range(4):
                S.matmul(sreg(6 + j), ctxkT[pr, hc, j * 128:(j + 1) * 128], qsl, start=(j % 2 == 0), stop=True,
                         skip_group_check=True)
            rp = rpbT[:, h % 2, :, :]
            for jn, n in enumerate(UN[mp]):
                t0_ = ti0[mp] + 2 * jn
                S.matmul(sreg(jn), ident_bf[:], maskt[:, t0_:t0_ + 2, :].rearrange("p a c -> p (a c)"),
                         start=False, stop=False, skip_group_check=True)
                d0 = n - 2 * mp + 4
                rhs = _ap(rp, d0 * 128, [rp.ap[0], [-128, 2], [1, 128]])
                S.matmul(sreg(jn), ident_bf[:], rhs, start=False, stop=True, skip_group_check=True)

        def exps(it):
            h, mp = its[it]
            Ju = len(UN[mp])
            Et = E[:, it % 2, :, :].rearrange("p a c -> p (a c)")
            Sl = PS[:, 0:3, :].rearrange("p b c -> p (b c)")
            Sc = PS[:, 3:5, :].rearrange("p b c -> p (b c)")
            S.act(Et[:, 0:Ju * 256], Sl[:, 0:Ju * 256], AF.Exp)
            S.act(Et[:, 1536:2560], Sc, AF.Exp, bias=ctxbias[:, 0:1])

        def pv(it):
            h, mp = its[it]
            hp, hc = h % 2, h // 2
            pr = slice(64 * hp, 64 * hp + 64)
            Ju = len(UN[mp])
            bnk = 5 + mp % 2
            num = PS[pr, bnk, 0:256]
            den = PS[pr, bnk, 256:512]
            tot = Ju + 4
            for j in range(tot):
                lv = vtok[:, UN[mp][j], h * 64:(h + 1) * 64] if j < Ju else ctxv[:, j - Ju, h * 64:(h + 1) * 64]
                ev = E[:, it % 2, j if j < Ju else 6 + j - Ju, :]
                S.matmul(num, lv, ev, start=(j == 0), stop=(j == tot - 1))
            for j in range(tot):
                ev = E[:, it % 2, j if j < Ju else 6 + j - Ju, :]
                S.matmul(den, ones_bf[:, 0:64], ev, start=(j == 0), stop=(j == tot - 1))
            if hp == 1:
                rc = rec2[:, mp % 2, :]
                S.recip(rc, PS[:, bnk, 256:512])
                S.tt('dve', catT[:, hc, mp * 256:(mp + 1) * 256], PS[:, bnk, 0:256], rc, ALU.mult)

        def pump():
            if getattr(P, 'mod1_gen', None) is not None:
                try:
                    next(P.mod1_gen)
                except StopIteration:
                    P.mod1_gen = None

        def pump_prep(n):
            for _ in range(n):
                if getattr(P, 's5prep', None) is None:
                    return
                try:
                    next(P.s5prep)
                except StopIteration:
                    P.s5prep = None

        NI = len(its)
        build_bias(0)
        build_bias(1)
        build_bm(0)
        build_bm(1)
        scores(0)
        exps(0)
        for it in range(1, NI + 1):
            if it + 1 < NI:
                build_bm(it + 1)
            if it < NI:
                scores(it)
                exps(it)
            pv(it - 1)
            for _ in range(4):
                if getattr(P, 'scan_gen', None) is not None:
                    try:
                        next(P.scan_gen)
                    except StopIteration:
                        P.scan_gen = None
        it = 0
        for h in range(8):
            hp, hc = h % 2, h // 2
            pr = slice(64 * hp, 64 * hp + 64)
            Sreg = PS[:, it % 2, :]
            Et = E[:, it % 2, 0:2, :]
            for kb in range(2):
                S.matmul(Sreg[:, kb * 256:(kb + 1) * 256], kT[pr, hc, 1024 + kb * 128:1024 + (kb + 1) * 128],
                         qT[pr, hc, 1024:1280], start=(kb == 0), stop=True, skip_group_check=True)
            S.act(Et.rearrange("p a c -> p (a c)"), Sreg, AF.Exp)
            bnk = 5 + hc % 2
            num = PS[pr, bnk, 0:256]
            den = PS[pr, bnk, 256:512]
            for kb in range(2):
                S.matmul(num, vtok[:, 8 + kb, h * 64:(h + 1) * 64], Et[:, kb, :], start=(kb == 0), stop=(kb == 1))
            for kb in range(2):
                S.matmul(den, ones_bf[:, 0:64], Et[:, kb, :], start=(kb == 0), stop=(kb == 1))
            if hp == 1:
                rc = rec2[:, hc % 2, :]
                S.recip(rc, PS[:, bnk, 256:512])
                S.tt('dve', catT[:, hc, 1024:1280], PS[:, bnk, 0:256], rc, ALU.mult)
            it += 1

    def out_proj_even():
        for u in range(2):
            W = P.wload(w_out_even[:, u * 512:(u + 1) * 512].rearrange("(k p) m -> p k m", p=128))
            for mt in range(4):
                m = u * 4 + mt
                for ti, (s_, n) in enumerate(TB):
                    b = P.bank(0, 6)
                    for kc in range(8):
                        S.matmul(PS[:, b, 0:n], W[:, kc, mt * 128:(mt + 1) * 128], catT[:, kc, s_:s_ + n],
                                 start=(kc == 0), stop=(kc == 7))
                    ci = 0 if ti < 2 else 1
                    S.stt(xT[:, m, s_:s_ + n], PS[:, b, 0:n], mod_vec(0, 2, m, ci), xT[:, m, s_:s_ + n],
                          ALU.mult, ALU.add)

    TWO_PI = 6.283185

    def disc(eng, lam, ls_b, T, F):
        ar, ai, cr, ci, lr, t0, t1, t2, t3, mag = T[:10]
        ti = T[10].bitcast(I32)
        S.ts(eng, lr, lam[:, 0], -1e-4, ALU.min)
        S.tt(eng, t0, lr, ls_b, ALU.mult)
        S.act(mag, t0, AF.Exp)
        S.tt(eng, t0, lam[:, 1], ls_b, ALU.mult)
        S.ts(eng, t0, t0, 1.0 / (2 * np.pi), ALU.mult)
        S.copy(eng, ti, t0)
        S.copy(eng, t1, ti)
        S.tt(eng, t1, t0, t1, ALU.subtract)
        S.act(t2, t1, AF.Sin, scale=TWO_PI)
        S.tt(eng, ai, mag, t2, ALU.mult)
        S.ts(eng, t0, t0, 0.25, ALU.add)
        S.copy(eng, ti, t0)
        S.copy(eng, t1, ti)
        S.tt(eng, t1, t0, t1, ALU.subtract)
        S.act(t2, t1, AF.Sin, scale=TWO_PI)
        S.tt(eng, ar, mag, t2, ALU.mult)
        li = lam[:, 1]
        S.ts(eng, t0, ar, -1.0, ALU.add)
        S.tt(eng, t1, lr, lr, ALU.mult)
        S.tt(eng, t2, li, li, ALU.mult)
        S.tt(eng, t1, t1, t2, ALU.add)
        S.recip(t1, t1)
        S.tt(eng, t2, t0, lr, ALU.mult)
        S.tt(eng, t3, ai, li, ALU.mult)
        S.tt(eng, t2, t2, t3, ALU.add)
        S.tt(eng, cr, t2, t1, ALU.mult)
        S.tt(eng, t2, ai, lr, ALU.mult)
        S.tt(eng, t3, t0, li, ALU.mult)
        S.tt(eng, t2, t2, t3, ALU.subtract)
        S.tt(eng, ci, t2, t1, ALU.mult)
        return ar, ai, cr, ci

    def bc(ap2, shape, pat):
        return _ap(ap2, 0, [ap2.ap[0]] + pat)

    def tview(t2d, off_bytes, shape, dt):
        esz0 = _DTSIZE[t2d.dtype]
        esz = _DTSIZE[dt]
        n = int(np.prod(shape[1:]))
        a = t2d[:, off_bytes // esz0:(off_bytes + n * esz) // esz0]
        if dt != t2d.dtype:
            a = a.bitcast(dt)
        if len(shape) == 2:
            return a
        names = " ".join("d%d" % i for i in range(len(shape) - 1))
        kw = {"d%d" % i: shape[i + 1] for i in range(len(shape) - 1)}
        return a.rearrange("p (%s) -> p %s" % (names, names), **kw)

    lsA = sbm[:, 0:8]
    parA = sbm[:, 8:10]
    carry = sbm[:, 10:11]
    sdT = sbm[:, 12:16]
    glubT = sbm[:, 16:20]
    dtA = sbm[:, 20:28]
    lamB = sbm[:, 32:96].rearrange("p (a f) -> p a f", a=2)
    lsB = sbm[:, 96:128]
    dtB = sbm[:, 128:160]
    Fin = sbm[:, 160:480].rearrange("p (i t) -> p i t", i=5)
    PB = sbm[:, 480:1056].rearrange("p (a e f) -> p a e f", a=2, e=9)
    lam_rr = sbm[:, 1056:1120].rearrange("p (f r) -> p f r", r=2)
    lam_is = sbm[:, 1120:1184].rearrange("p (f r) -> p f r", r=2)
    crB = sbm[:, 1184:1216]
    ciB = sbm[:, 1216:1248]
    WinD = nc.dram_tensor("WinD", [4, 128, 4096], BF16, kind="Internal").ap()
    WoutD = nc.dram_tensor("WoutD", [4, 128, 4096], BF16, kind="Internal").ap()
    ToepD = nc.dram_tensor("ToepD", [4, 128, 2048], BF16, kind="Internal").ap()
    for nm_ in ("WinD", "WoutD", "ToepD"):
        S.track_dram.add('D:' + nm_)
    rs2 = rstd[:, :]
    nt2 = ntmp[:, :, :].rearrange("p a b -> p (a b)")

    def s5_small():
        eng = 'dve'
        S.dma('sp', lsA, sA_ls_d.rearrange("p q d -> p (q d)"))
        S.dma('sp', parA, parA_d)
        S.dma('sp', carry, carry_d)
        S.dma('sp', sdT, sdT_d)
        S.dma('sp', glubT, glubT_d)
        S.act(dtA, lsA, AF.Exp)
        S.dma('sp', lamB, sB_lam_d)
        S.dma('sp', lsB, sB_ls_d)
        S.act(dtB, lsB, AF.Exp)
        TB_ = [tview(rs2, 1536 + 128 * i, [128, 32], F32) for i in range(11)]
        arB, aiB, crB_, ciB_ = disc(eng, lamB, dtB, TB_, 32)
        S.copy(eng, crB, crB_)
        S.copy(eng, ciB, ciB_)
        S.memset(eng, PB[:, 0, 0, :], 1.0)
        S.memset(eng, PB[:, 1, 0, :], 0.0)
        S.copy(eng, PB[:, 0, 1, :], arB)
        S.copy(eng, PB[:, 1, 1, :], aiB)
        u1, u2 = TB_[4], TB_[5]
        for e in range(2, 9):
            S.tt(eng, u1, PB[:, 0, e - 1, :], arB, ALU.mult)
            S.tt(eng, u2, PB[:, 1, e - 1, :], aiB, ALU.mult)
            S.tt(eng, PB[:, 0, e, :], u1, u2, ALU.subtract)
            S.tt(eng, u1, PB[:, 0, e - 1, :], aiB, ALU.mult)
            S.tt(eng, u2, PB[:, 1, e - 1, :], arB, ALU.mult)
            S.tt(eng, PB[:, 1, e, :], u1, u2, ALU.add)
        S.copy(eng, lam_rr[:, :, 0], PB[:, 0, 8, :])
        S.copy(eng, lam_rr[:, :, 1], PB[:, 0, 8, :])
        S.copy(eng, lam_is[:, :, 0], PB[:, 1, 8, :])
        S.ts(eng, lam_is[:, :, 1], PB[:, 1, 8, :], -1.0, ALU.mult)

    def s5_prep_batched():
        eng = 'dve'
        Win = aview(8192, [128, 4, 2, 8, 2, 128], BF16)
        T_ = [aview(40960 + 2048 * i, [128, 4, 2, 64], F32) for i in range(11)]
        lamA = aview(63488, [128, 2, 4, 2, 64], F32)
        bA = aview(67584, [128, 2, 4, 2, 64], F32)
        S.dma('sp', lamA, sA_lam_d)
        S.dma('sp', bA, sA_b_d)
        dt_b = _ap(dtA, 0, [dtA.ap[0], [2, 4], [1, 2], [0, 64]])
        arA, aiA, crA, ciA = disc(eng, lamA, dt_b, T_, 512)
        bre = bA[:, 0]
        bim = bA[:, 1]
        bbr, bbi, ua, ub = T_[4], T_[5], T_[6], T_[7]
        S.tt(eng, ua, crA, bre, ALU.mult)
        S.tt(eng, ub, ciA, bim, ALU.mult)
        S.tt(eng, bbr, ua, ub, ALU.subtract)
        S.tt(eng, ua, crA, bim, ALU.mult)
        S.tt(eng, ub, ciA, bre, ALU.mult)
        S.tt(eng, bbi, ua, ub, ALU.add)
        Wr = [bbr, T_[8]]
        Wi = [bbi, T_[9]]
        for e in range(8):
            cr_, ci_ = Wr[e % 2], Wi[e % 2]
            if e > 0:
                pr_, pi_ = Wr[(e - 1) % 2], Wi[(e - 1) % 2]
                S.tt(eng, ua, pr_, arA, ALU.mult)
                S.tt(eng, ub, pi_, aiA, ALU.mult)
                S.tt(eng, cr_, ua, ub, ALU.subtract)
                S.tt(eng, ua, pr_, aiA, ALU.mult)
                S.tt(eng, ub, pi_, arA, ALU.mult)
                S.tt(eng, ci_, ua, ub, ALU.add)
            for gp in range(2):
                S.act(Win[:, :, 0, e, :, gp * 64:(gp + 1) * 64], cr_, AF.Copy, scale=parA[:, gp:gp + 1])
                S.act(Win[:, :, 1, e, :, gp * 64:(gp + 1) * 64], ci_, AF.Copy, scale=parA[:, gp:gp + 1])
        for q in range(4):
            S.dma('sp', WinD[q], Win[:, q].rearrange("p a b c d -> p (a b c d)"))
        Toep = aview(8192, [128, 4, 16, 128], BF16)
        Wo = aview(40960, [128, 32, 2, 8, 32], BF16)
        Wo0 = aview(73728, [128, 32, 2, 32], BF16)
        BbS = aview(77824, [128, 32, 2, 32], BF16)
        cB = aview(81920, [128, 2, 32, 16], F32)
        bB = aview(86016, [128, 2, 32, 16], F32)
        w1b = [aview(90112, [128, 32, 16], F32), aview(92160, [128, 32, 16], F32)]
        w2 = aview(94208, [128, 32, 16], F32)
        Dd = aview(96256, [128, 4, 128], F32)
        S.dma('sp', cB, sB_c_d)
        S.dma('sp', bB, sB_b_d)
        cr_b = _ap(crB, 0, [crB.ap[0], [1, 32], [0, 16]])
        ci_b = _ap(ciB, 0, [ciB.ap[0], [1, 32], [0, 16]])
        S.memset('pool', BbS[:, :, :, :].rearrange("p a b c -> p (a b c)"), 0.0)
        S.memset('pool', Wo[:, :, :, :, :].rearrange("p a b c d -> p (a b c d)"), 0.0)
        S.memset('pool', Wo0[:, :, :, :].rearrange("p a b c -> p (a b c)"), 0.0)
        for ri in range(2):
            wx = w1b[ri]
            if ri == 0:
                S.tt(eng, wx, cr_b, bB[:, 0], ALU.mult)
                S.tt(eng, w2, ci_b, bB[:, 1], ALU.mult)
                S.tt(eng, wx, wx, w2, ALU.subtract)
            else:
                S.tt(eng, wx, cr_b, bB[:, 1], ALU.mult)
                S.tt(eng, w2, ci_b, bB[:, 0], ALU.mult)
                S.tt(eng, wx, wx, w2, ALU.add)
            for gp in range(2):
                ps_ = slice(64 * gp, 64 * gp + 64)
                S.copy('act', BbS[ps_, :, ri, 16 * gp:16 * gp + 16], wx[ps_, :, :])
        for e in range(9):
            pr_b = _ap(PB, (0 * 9 + e) * 32, [PB.ap[0], [1, 32], [0, 16]])
            pi_b = _ap(PB, (1 * 9 + e) * 32, [PB.ap[0], [1, 32], [0, 16]])
            for ri in range(2):
                wx = w1b[ri]
                if ri == 0:
                    S.tt(eng, wx, cB[:, 0], pr_b, ALU.mult)
                    S.tt(eng, w2, cB[:, 1], pi_b, ALU.mult)
                    S.tt(eng, wx, wx, w2, ALU.subtract)
                    sgn = 1.0
                else:
                    S.tt(eng, wx, cB[:, 0], pi_b, ALU.mult)
                    S.tt(eng, w2, cB[:, 1], pr_b, ALU.mult)
                    S.tt(eng, wx, wx, w2, ALU.add)
                    sgn = -1.0
                for gp in range(2):
                    ps_ = slice(64 * gp, 64 * gp + 64)
                    if e == 0:
                        dst = Wo0[ps_, :, ri, 16 * gp:16 * gp + 16]
                    else:
                        dst = Wo[ps_, :, ri, e - 1, 16 * gp:16 * gp + 16]
                    S.act(dst, wx[ps_, :, :], AF.Copy, scale=sgn)
        for q in range(4):
            S.dma('sp', WoutD[q], Wo[:, q * 8:(q + 1) * 8].rearrange("p a b c d -> p (a b c d)"))

        def wo(f, ri, tau):
            return Wo0[:, f, ri, :] if tau == 0 else Wo[:, f, ri, tau - 1, :]

        kk = 0
        for q in range(4):
            S.ts(eng, Dd[:, q, :], ident[:], sdT[:, q:q + 1], ALU.mult)
            for bi in range(4):
                b = P.bank(0, 6)
                S.memset(eng, PS[:, b, :], 0.0)
                for s4 in range(4):
                    slot = bi * 4 + s4
                    if slot > 14:
                        continue
                    if slot < 7:
                        combos = [(0, slot + 1)]
                    elif slot < 14:
                        combos = [(1, slot - 6)]
                    else:
                        combos = [(0, 0), (1, 0)]
                    n_ = len(combos) * 2
                    i_ = 0
                    for (d, tau) in combos:
                        for ri in range(2):
                            for pr in range(4):
                                f = (q * 2 + d) * 4 + pr
                                o = PS[32 * pr:32 * pr + 32, b, s4 * 128 + 32 * pr:s4 * 128 + 32 * pr + 32]
                                S.matmul(o, BbS[:, f, ri, :], wo(f, ri, tau),
                                         start=(i_ == 0), stop=(i_ == n_ - 1), tile_position=(0, 32 * pr))
                            i_ += 1
                if bi < 3:
                    evac(kk, Toep[:, q, bi * 4:bi * 4 + 4, :], PS[:, b, :].rearrange("p (s c) -> p s c", c=128)); kk += 1
                else:
                    S.copy('act', Toep[:, q, 12:14, :], PS[:, b, 0:256].rearrange("p (s c) -> p s c", c=128))
                    S.tt(eng, Toep[:, q, 14, :], PS[:, b, 256:384], Dd[:, q, :], ALU.add)
            S.dma('sp', ToepD[q], Toep[:, q].rearrange("p a c -> p (a c)"))

    def s5_defs():
        P.Pst = tview(wsl[:, :], 0, [128, 64, 162], F32)

    def s5_V():
        eng = 'dve'
        Pst = P.Pst
        Winb = [aview(8192 * i, [128, 2, 8, 2, 128], BF16) for i in range(2)]
        BT = 102400
        s0tmp = aview(BT + 1024, [128, 64], F32)
        S.dma('sp', s0tmp, s0B_d)
        S.copy(eng, Pst[:, :, 0], s0tmp)
        S.memset(eng, Pst[:, :, 129], 0.0)
        kk = 0
        for q in range(4):
            Win = Winb[q % 2]
            S.dma('sp', Win[:, :, :, :, :].rearrange("p a b c d -> p (a b c d)"), WinD[q])
            for d in range(2):
                for pr in range(4):
                    for ri in range(2):
                        t = q * 16 + d * 8 + pr * 2 + ri
                        b = P.bank(0, 6)
                        rows = slice(32 * pr, 32 * pr + 32)
                        for j in range(8):
                            e = 7 - j if d == 0 else j
                            S.matmul(PS[:, b, 0:160], Win[rows, ri, e, d, :], uTp[rows, q, j, :],
                                     start=(j == 0), stop=(j == 7), tile_position=(32 * pr, 0))
                        pt = Pst[:, t, :]
                        if d == 0:
                            evac(kk, pt[:, 1:129], PS[:, b, 0:128]); kk += 1
                            evac(kk, pt[:, 130:162], PS[:, b, 128:160]); kk += 1
                        else:
                            evac(kk, _ap(pt, 128, [pt.ap[0], [-1, 128]]), PS[:, b, 0:128]); kk += 1
                            evac(kk, _ap(pt, 161, [pt.ap[0], [-1, 32]]), PS[:, b, 128:160]); kk += 1


    def s5_scan_gen():
        eng = 'dve'
        Pst = P.Pst
        BT = 102400
        sc1 = aview(BT, [128, 32, 2, 2], F32)
        sc2 = aview(BT + 512, [128, 32, 2, 2], F32)
        pa = Pst[:, :, :]
        pstep = pa.ap[0]
        for k in range(128):
            if k % 1 == 0 and k > 0:
                yield
            ncol = 2 if k < 32 else 1
            src = _ap(pa, k, [pstep, [324, 32], [162, 2], [129, ncol]])
            dst = _ap(pa, k + 1, [pstep, [324, 32], [162, 2], [129, ncol]])
            if ncol == 1:
                src2 = _ap(pa, k, [pstep, [0, 2], [324, 32], [162, 2]])
                lam2 = _ap(lam_rr, 0, [lam_rr.ap[0], [64, 2], [2, 32], [1, 2]])
                out2 = _ap(sc1, 0, [sc1.ap[0], [128, 2], [4, 32], [2, 2]])
                S.tt(eng, out2, src2, lam2, ALU.mult)
                a1 = _ap(sc1, 0, [sc1.ap[0], [4, 32], [2, 2], [1, 1]])
                a2s = _ap(sc2, 2, [sc2.ap[0], [4, 32], [-2, 2], [1, 1]])
                S.tt(eng, dst, dst, a1, ALU.add)
                S.tt(eng, dst, dst, a2s, ALU.add)
            else:
                lr_b = _ap(lam_rr, 0, [lam_rr.ap[0], [2, 32], [1, 2], [0, ncol]])
                li_b = _ap(lam_is, 0, [lam_is.ap[0], [2, 32], [1, 2], [0, ncol]])
                a1 = sc1[:, :, :, 0:ncol]
                a2 = sc2[:, :, :, 0:ncol]
                a2s = _ap(sc2, 2, [sc2.ap[0], [4, 32], [-2, 2], [1, ncol]])
                S.tt(eng, a1, src, lr_b, ALU.mult)
                S.tt(eng, a2, src, li_b, ALU.mult)
                S.tt(eng, dst, dst, a1, ALU.add)
                S.tt(eng, dst, dst, a2s, ALU.add)
            if (k + 1) % 32 == 0:
                idx = (k + 1) // 32 - 1
                S.copy(eng, Fin[:, idx, :], Pst[:, :, k + 1])
                if k + 1 < 128:
                    S.ts(eng, Pst[:, :, k + 1], Pst[:, :, k + 1], carry, ALU.mult)
                if k == 31:
                    S.copy(eng, Fin[:, 4, :], Pst[:, :, 161])

    def s5_rest():
        eng = 'dve'
        Pst = P.Pst
        ygT = aview(71680, [128, 4, NT], BF16)
        SinA = aview(92160, [128, 32, 160], BF16)
        SinB = aview(51200, [128, 32, 160], BF16)
        Wob = [aview(8192 * i, [128, 2, 4, 2, 8, 32], BF16) for i in range(2)]
        Tpb = [aview(16384 + 4096 * i, [128, 16, 128], BF16) for i in range(2)]
        BT = 102400
        S.dma('sp', so_d.rearrange("i p t -> p i t"), Fin)

        for q in (2, 3, 0, 1):
            Sq = (SinA if q < 2 else SinB)[:, (q % 2) * 16:(q % 2) * 16 + 16, :]
            for d in range(2):
                pqd = Pst[:, q * 16 + d * 8:q * 16 + d * 8 + 8, :]
                sd_ = Sq[:, d * 8:(d + 1) * 8, :]
                if d == 0:
                    S.copy('act', sd_[:, :, 0:128], pqd[:, :, 0:128])
                    S.copy(eng, sd_[:, :, 128:160], pqd[:, :, 129:161])
                else:
                    S.copy('act', sd_[:, :, 0:128], _ap(pqd, 127, [pqd.ap[0], [162, 8], [-1, 128]]))
                    S.copy(eng, sd_[:, :, 128:160], _ap(pqd, 129 + 31, [pqd.ap[0], [162, 8], [-1, 32]]))
        gluW = P.wload(glu_w_d.rearrange("(k p) m -> p k m", p=128))
        for q in range(4):
            Wout = Wob[q % 2]
            Toep = Tpb[q % 2]
            S.dma('sp', Wout[:, :, :, :, :, :].rearrange("p a b c d e -> p (a b c d e)"), WoutD[q])
            S.dma('sp', Toep[:, :, :].rearrange("p a c -> p (a c)"), ToepD[q])
            Sin = (SinA if q < 2 else SinB)[:, (q % 2) * 16:(q % 2) * 16 + 16, :]
            for i in range(8):
                b = P.bank(0, 6)
                o = PS[:, b, 0:160]
                for j in range(8):
                    slot = (i - j - 1) if j < i else ((7 + j - i - 1) if j > i else 14)
                    S.matmul(o, Toep[:, slot, :], uTp[:, q, j, :], start=(j == 0), stop=False)
                i_ = 0
                for d in range(2):
                    e = i + 1 if d == 0 else 8 - i
                    for ri in range(2):
                        for pr in range(4):
                            i_ += 1
                            S.matmul(PS[32 * pr:32 * pr + 32, b, 0:160], Wout[:, d, pr, ri, e - 1, :],
                                     Sin[:, d * 8 + pr * 2 + ri, :], start=False, stop=(i_ > 12),
                                     tile_position=(0, 32 * pr))
                yq = ygT[:, q, :]
                S.act(_ap(yq, i, [yq.ap[0], [8, 160]]), o, AF.Gelu)
        mark('s5_y')
        sg = [aview(BT + 1024, [128, 512], BF16), aview(BT + 2048, [128, 512], BF16)]
        k2 = 0
        for m in range(4):
            for (s_, n) in TB:
                b = P.bank(0, 6)
                for kc in range(4):
                    S.matmul(PS[:, b, 0:n], gluW[:, kc, m * 128:(m + 1) * 128], ygT[:, kc, s_:s_ + n],
                             start=(kc == 0), stop=(kc == 3))
                t = sg[k2 % 2][:, 0:n]
                k2 += 1
                S.act(t, PS[:, b, 0:n], AF.Sigmoid, bias=glubT[:, m:m + 1])
                S.tt('dve', catT[:, 4 + m, s_:s_ + n], ygT[:, m, s_:s_ + n], t, ALU.mult)

    def mixer_even():
        projections()
        mark('proj')
        P.scan_gen = None
        if cfg['s5']:
            s5_defs()
            s5_V()
            mark('s5_V')
            P.scan_gen = s5_scan_gen()
        if cfg['attn']:
            attention()
        else:
            S.memset('dve', catT[:, 0:4, :], 0.0)
        if P.scan_gen is not None:
            for _ in P.scan_gen:
                pass
            P.scan_gen = None
        mark('attn')
        if cfg['s5']:
            s5_rest()
        else:
            S.memset('dve', catT[:, 4:8, :], 0.0)
        mark('s5')
        out_proj_even()
        mark('wout')

    def input_transposes():
        xstage = [aview(0, [128, D], F32), aview(4096, [128, D], F32),
                  aview(98304, [128, D], F32), aview(102400, [128, D], F32)]
        for tt in range(10):
            st = xstage[tt % 4]
            S.dma('sp', st, xin[tt * 128:(tt + 1) * 128, :])
            for half in range(2):
                b = P.bank(0, 4)
                for c4 in range(4):
                    c = half * 4 + c4
                    S.transpose(PS[:, b, c4 * 128:(c4 + 1) * 128], st[:, c * 128:(c + 1) * 128], ident[:])
                src = PS[:, b, :].rearrange("p (c t) -> p c t", t=128)
                dst = xT[:, half * 4:half * 4 + 4, tt * 128:(tt + 1) * 128]
                S.copy('act' if (tt + half) % 2 == 0 else 'dve', dst, src)


    P.marks = []

    def mark(name):
        P.marks.append((name, sum(1 for o in S.ops if o.eng == 'pe'), sum(1 for o in S.ops if o.eng == 'dve'),
                        sum(1 for o in S.ops if o.eng == 'act')))
    if cfg['s5']:
        s5_small()
    input_transposes()
    mark('xin')
    P.s5prepA = None
    P.s5prepB = None
    g0_ = modulation_gen(0)
    if cfg['s5']:
        for _ in range(12):
            next(g0_)
        P.mod1_gen = modulation_gen(1)
        for _ in range(12):
            next(P.mod1_gen)
        s5_prep_batched()
    for _ in g0_:
        pass
    if getattr(P, 'mod1_gen', None) is not None:
        for _ in P.mod1_gen:
            pass
        P.mod1_gen = 'done'
    mark('mod0')
    for layer in range(2):
        if layer == 1 and cfg['fnet']:
            fnet_mixer(1)
            mark('fnet')
        if layer == 0 and (cfg['attn'] or cfg['s5']):
            mixer_even()
        if layer == 0:
            g_ = getattr(P, 'mod1_gen', None)
            if g_ is None:
                g_ = modulation_gen(1)
            if g_ != 'done':
                for _ in g_:
                    pass
            mark('mod1')
        if cfg['ffn']:
            norm_mod(layer, 1)
            ffn(layer)
            mark('ffn%d' % layer)

    sumsq_rstd()
    for c in range(8):
        S.stt(xT[:, c, :], xT[:, c, :], gvec[:, 4, c:c + 1], rstd[:, :], ALU.mult, ALU.mult)
    ystage = [aview(0, [128, D], F32), aview(4096, [128, D], F32)]
    for tt in range(10):
        st = ystage[tt % 2]
        for half in range(2):
            b = P.bank(0, 4)
            for c4 in range(4):
                c = half * 4 + c4
                S.transpose(PS[:, b, c4 * 128:(c4 + 1) * 128], xT[:, c, tt * 128:(tt + 1) * 128], ident[:])
            S.copy('act' if (tt + half) % 2 == 0 else 'dve', st[:, half * 512:(half + 1) * 512], PS[:, b, :])
        S.dma('sp', y_d[tt * 128:(tt + 1) * 128, :], st)

    S.emit(es)
    P.es.close()
    return P


def _core_tokens(c, x_prompt, x_sample):
    if c < 2:
        return np.concatenate([x_sample[c], x_prompt[c]], 0)
    s = 2 + 5 * (c - 2)
    return x_prompt[s:s + 5].reshape(NT, D)


def _fm(v, nch):
    return np.ascontiguousarray(v.reshape(nch, 128).T)


def make_in_maps(inp, cores):
    f32 = np.float32
    shared = {}
    shared['bmodT'] = np.ascontiguousarray(np.stack([_fm(inp['b_mod'][l], 48) for l in range(2)], 1)).astype(f32)
    shared['gvec'] = np.ascontiguousarray(np.stack([
        _fm(inp['norm_mix_g'][0], 8), _fm(inp['norm_ffn_g'][0], 8),
        _fm(inp['norm_mix_g'][1], 8), _fm(inp['norm_ffn_g'][1], 8),
        _fm(inp['final_norm_g'], 8)], 1)).astype(f32)
    shared['ident'] = np.eye(128, dtype=f32)
    for k in ('w_mod', 'ffn_w_gate', 'ffn_w_up', 'ffn_w_down'):
        shared[k] = np.ascontiguousarray(inp[k])
    shared['w_in_odd'] = np.ascontiguousarray(inp['w_in_odd'][0])
    shared['w_out_odd'] = np.ascontiguousarray(inp['w_out_odd'][0])
    bf = ml_dtypes.bfloat16
    ang = 2 * np.pi * np.outer(np.arange(256), np.arange(256)) / 256.0
    shared['cs256'] = np.concatenate([np.cos(ang) / 16.0, -np.sin(ang) / 16.0], 1).astype(bf)
    shared['dft4'] = np.stack([np.cos(ang) / 16.0, np.sin(ang) / 16.0], 0).astype(bf)
    angL = 2 * np.pi * (np.outer(np.arange(1024), np.arange(1024)) % 1024) / 1024.0
    dft_sample = np.stack([np.cos(angL) / 32.0, np.sin(angL) / 32.0], 0).astype(bf)
    dft_prompt = np.zeros((2, 1024, 1024), np.float32)
    for i in range(4):
        dft_prompt[0, i * 256:(i + 1) * 256, i * 256:(i + 1) * 256] = np.cos(ang) / 16.0
        dft_prompt[1, i * 256:(i + 1) * 256, i * 256:(i + 1) * 256] = np.sin(ang) / 16.0
    dft_prompt = dft_prompt.astype(bf)
    KBh = [[0, 1, 2, 3], [0, 1, 2, 3], [0, 1, 2, 3, 4], [1, 2, 3, 4, 5], [2, 3, 4, 5, 6], [3, 4, 5, 6, 7],
           [4, 5, 6, 7], [4, 5, 6, 7]]
    kr_ = np.arange(2)[:, None].repeat(64, 1).reshape(128)
    kc_ = np.arange(64)[None, :].repeat(2, 0).reshape(128)
    mask_s = np.zeros((128, 40, 128), np.float32)
    mask_p = np.zeros((128, 40, 128), np.float32)
    UNh = [[0, 1, 2, 3], [0, 1, 2, 3, 4, 5], [2, 3, 4, 5, 6, 7], [4, 5, 6, 7]]
    mi = 0
    for mp in range(4):
        for n in UNh[mp]:
            for mm in range(2):
                m = 2 * mp + mm
                qrow = 2 * m + kr_[None, :]; qcol = kc_[None, :]
                krow = 2 * n + kr_[:, None]; kcol = kc_[:, None]
                rs = np.clip(qrow - 4, 0, 8); cs = np.clip(qcol - 8, 0, 48)
                ok = (krow >= rs) & (krow < rs + 8) & (kcol >= cs) & (kcol < cs + 16)
                mask_s[:, mi, :] = np.where(ok, 0.0, NEG)
                mask_p[:, mi, :] = 0.0 if (n // 2 == m // 2) else NEG
                mi += 1
    mask_s = mask_s.astype(bf); mask_p = mask_p.astype(bf)
    R_ = inp['na_rpb'][0]
    rpbH = np.zeros((8, 19, 128), f32)
    rpbH[:, 2:17, 48:79] = R_
    shared['w_in_even'] = np.ascontiguousarray(inp['w_in_even'][0])
    shared['w_out_even'] = np.ascontiguousarray(inp['w_out_even'][0])
    lam = np.stack([inp['s5_lam_re'][0], inp['s5_lam_im'][0]], 0)
    bb = np.stack([inp['s5_b_re'][0], inp['s5_b_im'][0]], 0)
    cc = np.stack([inp['s5_c_re'][0], inp['s5_c_im'][0]], 0)
    ls = inp['s5_log_step'][0]
    lam_q = lam.reshape(2, 2, 4, 8, 64)
    sA_lam = np.broadcast_to(lam_q.transpose(3, 0, 2, 1, 4)[:, None], (8, 16, 2, 4, 2, 64)).reshape(128, 2, 4, 2, 64)
    shared['sA_lam'] = np.ascontiguousarray(sA_lam).astype(f32)
    ls_q = ls.reshape(2, 4, 8)
    shared['sA_ls'] = np.ascontiguousarray(
        np.broadcast_to(ls_q.transpose(2, 1, 0)[:, None], (8, 16, 4, 2)).reshape(128, 4, 2)).astype(f32)
    bq = bb.reshape(2, 2, 4, 8, 64, 16)
    shared['sA_b'] = np.ascontiguousarray(bq.transpose(3, 5, 0, 2, 1, 4).reshape(128, 2, 4, 2, 64)).astype(f32)
    par = np.zeros((128, 2), f32)
    gpar = (np.arange(128) // 16) % 2
    par[gpar == 0, 0] = 1.0
    par[gpar == 1, 1] = 1.0
    shared['parA'] = par
    lam_s = lam.reshape(2, 2, 4, 4, 2, 64)
    shared['sB_lam'] = np.ascontiguousarray(lam_s.transpose(4, 5, 0, 2, 1, 3).reshape(128, 2, 32)).astype(f32)
    ls_s = ls.reshape(2, 4, 4, 2)
    shared['sB_ls'] = np.ascontiguousarray(
        np.broadcast_to(ls_s.transpose(3, 1, 0, 2)[:, None], (2, 64, 4, 2, 4)).reshape(128, 32)).astype(f32)
    c_s = cc.reshape(2, 2, 4, 4, 2, 16, 64)
    shared['sB_c'] = np.ascontiguousarray(c_s.transpose(4, 6, 0, 2, 1, 3, 5).reshape(128, 2, 32, 16)).astype(f32)
    b_s = bb.reshape(2, 2, 4, 4, 2, 64, 16)
    shared['sB_b'] = np.ascontiguousarray(b_s.transpose(4, 5, 0, 2, 1, 3, 6).reshape(128, 2, 32, 16)).astype(f32)
    shared['sdT'] = _fm(inp['s5_d'][0], 4).astype(f32)
    shared['glubT'] = _fm(inp['s5_glu_b'][0], 4).astype(f32)
    shared['s5_glu_w'] = np.ascontiguousarray(inp['s5_glu_w'][0])
    maps = []
    for c in cores:
        m = dict(shared)
        m['xin'] = np.ascontiguousarray(_core_tokens(c, inp['x_prompt'], inp['x_sample']))
        cond_long = inp['c'][c] if c < 2 else inp['c_ctx']
        cond = np.stack([cond_long, inp['c_ctx']], 0)
        m['condT'] = np.ascontiguousarray(cond.reshape(2, 8, 128).transpose(2, 1, 0)).astype(f32)
        m['dftL'] = dft_sample if c < 2 else dft_prompt
        if c < 2:
            m['ctxkT'] = np.ascontiguousarray(inp['cache_na_k'][c, 0].reshape(512, 512).T)
            m['ctxv'] = np.ascontiguousarray(inp['cache_na_v'][c, 0].reshape(512, 512))
            m['ctxbias'] = np.zeros((128, 1), f32)
            m['maskt'] = mask_s
            m['rpbH'] = rpbH
            st = inp['state_s5'][c, 0].reshape(2, 2, 4, 4, 2, 64)
            m['s0B'] = np.ascontiguousarray(st.transpose(4, 5, 2, 0, 3, 1).reshape(128, 64)).astype(f32)
            m['carry'] = np.ones((128, 1), f32)
        else:
            m['ctxkT'] = np.zeros((512, 512), f32)
            m['ctxv'] = np.zeros((512, 512), f32)
            m['ctxbias'] = np.full((128, 1), NEG, f32)
            m['maskt'] = mask_p
            m['rpbH'] = np.zeros((8, 19, 128), f32)
            m['s0B'] = np.zeros((128, 64), f32)
            m['carry'] = np.zeros((128, 1), f32)
        maps.append(m)
    return maps


_PROG = {}


def run_cores(inp, cores, cfg=None):
    cfg = dict(CFG) if cfg is None else cfg
    key = tuple(sorted(cfg.items()))
    if key not in _PROG:
        _PROG[key] = build_program(cfg)
    P = _PROG[key]
    maps = make_in_maps(inp, cores)
    maps = [{k: v for k, v in m.items() if k in P.ins} for m in maps]
    res = run_bass_kernel_spmd(P.nc, maps, core_ids=list(range(len(cores))))
    return res.results


def kernel(**inputs):
    inp = {k: np.asarray(v) for k, v in inputs.items()}
    res = run_cores(inp, list(range(8)))
    y_prompt = np.zeros((32, 256, D), np.float32)
    y_sample = np.zeros((2, 1024, D), np.float32)
    for c in range(8):
        y = res[c]['y']
        if c < 2:
            y_sample[c] = y[0:1024]
            y_prompt[c] = y[1024:1280]
        else:
            s = 2 + 5 * (c - 2)
            y_prompt[s:s + 5] = y.reshape(5, 256, D)
    nk = np.zeros((32, 1, 256, 8, 64), np.float32)
    nv = np.zeros((32, 1, 256, 8, 64), np.float32)
    for c in range(8):
        for nm, dst in (('ko', nk), ('vo', nv)):
            a = res[c][nm]
            if c < 2:
                dst[c, 0] = a[1024:1280].reshape(256, 8, 64)
            else:
                s0 = 2 + 5 * (c - 2)
                dst[s0:s0 + 5, 0] = a.reshape(5, 256, 8, 64)
    ns = np.zeros((32, 1, 2, 2, 32, 64), np.float32)
    for c in range(8):
        so = res[c]['so']
        so = so.reshape(5, 2, 64, 4, 2, 4, 2)
        st = so.transpose(0, 4, 6, 3, 5, 1, 2).reshape(5, 2, 2, 32, 64)
        if c < 2:
            ns[c, 0] = st[4]
        else:
            s0 = 2 + 5 * (c - 2)
            for j in range(4):
                ns[s0 + j, 0, 0] = st[j, 0]
                ns[s0 + j, 0, 1] = st[3 - j, 1]
            ns[s0 + 4, 0] = st[4]
    return (y_prompt, y_sample, nk, nv, ns)
```

```python
import numpy as np
import ml_dtypes
from contextlib import ExitStack
import concourse.bass as bass
import concourse.mybir as mybir
from concourse.bass_utils import run_bass_kernel_spmd

F32 = mybir.dt.float32
BF16 = mybir.dt.bfloat16
I32 = mybir.dt.int32
AF = mybir.ActivationFunctionType
ALU = mybir.AluOpType

_DTSIZE = {F32: 4, BF16: 2, I32: 4}

NT = 1280
TB = [(0, 512), (512, 512), (1024, 256)]
CR = [(0, 1024, 0), (1024, 256, 1)]
D = 1024
DFF = 2816
NEG = -30000.0

CFG = dict(attn=True, s5=True, fnet=True, ffn=True, dbg=False, att=9)


def _region(ap):
    t = ap.tensor
    name = t.name
    space = str(ap.space).upper()
    pat = ap.ap
    esz = _DTSIZE[ap.dtype]
    off = int(ap.offset)
    if 'DRAM' in space or 'HBM' in space:
        lo = hi = off
        for st, cn in pat:
            if st >= 0:
                hi += st * (cn - 1)
            else:
                lo += st * (cn - 1)
        return ('D:' + name, 0, 1, lo * esz, (hi + 1) * esz)
    pstep, pcnt = pat[0]
    p0 = off // pstep if pstep else 0
    foff = off - p0 * pstep if pstep else off
    lo = hi = foff
    for st, cn in pat[1:]:
        if st >= 0:
            hi += st * (cn - 1)
        else:
            lo += st * (cn - 1)
    if 'PSUM' in space:
        b0 = (lo * esz) // 2048
        b1 = ((hi + 1) * esz - 1) // 2048
        return ('P:' + name, 0, 128, b0 * 2048, (b1 + 1) * 2048)
    return (name, p0, p0 + pcnt, lo * esz, (hi + 1) * esz)


class Op:
    __slots__ = ('eng', 'fn', 'reads', 'writes', 'dma', 'deps', 'signal', 'sem', 'val', 'idx')


class Sched:
    ENGS = ('pe', 'act', 'dve', 'pool', 'sp')
    NDSEM = 12

    def __init__(self, nc):
        self.nc = nc
        self.ops = []
        self.track_dram = set()

    def add(self, eng, fn, reads=(), writes=(), dma=False):
        op = Op()
        op.eng = eng
        op.fn = fn
        op.reads = [_region(a) for a in reads]
        op.writes = [_region(a) for a in writes]
        op.dma = dma
        op.idx = len(self.ops)
        self.ops.append(op)
        return op

    def matmul(self, out, lhsT, rhs, start=True, stop=True, **kw):
        rd = [lhsT, rhs] + ([] if start else [out])
        return self.add('pe', lambda e: e.matmul(out, lhsT=lhsT, rhs=rhs, start=start, stop=stop, **kw), rd, [out])

    def transpose(self, out, in_, ident):
        return self.add('pe', lambda e: e.transpose(out, in_, ident), [in_, ident], [out])

    def act(self, out, in_, func, bias=None, scale=None, accum_out=None):
        rd = [in_]
        kw = {}
        if bias is not None:
            kw['bias'] = bias
            if not isinstance(bias, (int, float)):
                rd.append(bias)
        if scale is not None:
            kw['scale'] = scale
            if not isinstance(scale, (int, float)):
                rd.append(scale)
        wr = [out]
        if accum_out is not None:
            kw['accum_out'] = accum_out
            wr.append(accum_out)
        return self.add('act', lambda e: e.activation(out=out, in_=in_, func=func, **kw), rd, wr)

    def tt(self, eng, out, in0, in1, op):
        return self.add(eng, lambda e: e.tensor_tensor(out=out, in0=in0, in1=in1, op=op), [in0, in1], [out])

    def ts(self, eng, out, in0, s1, op0, s2=None, op1=None):
        rd = [in0]
        if not isinstance(s1, (int, float)):
            rd.append(s1)
        if s2 is not None and not isinstance(s2, (int, float)):
            rd.append(s2)
        if op1 is None:
            return self.add(eng, lambda e: e.tensor_scalar(out=out, in0=in0, scalar1=s1, scalar2=None, op0=op0),
                            rd, [out])
        return self.add(eng, lambda e: e.tensor_scalar(out=out, in0=in0, scalar1=s1, scalar2=s2, op0=op0, op1=op1),
                        rd, [out])

    def stt(self, out, in0, scalar, in1, op0, op1):
        rd = [in0, in1]
        if not isinstance(scalar, (int, float)):
            rd.append(scalar)
        return self.add('dve', lambda e: e.scalar_tensor_tensor(out=out, in0=in0, scalar=scalar, in1=in1,
                                                                  op0=op0, op1=op1), rd, [out])

    def copy(self, eng, out, in_):
        if eng == 'act':
            return self.add('act', lambda e: e.copy(out=out, in_=in_), [in_], [out])
        return self.add(eng, lambda e: e.tensor_copy(out=out, in_=in_), [in_], [out])

    def recip(self, out, in_):
        return self.add('dve', lambda e: e.reciprocal(out=out, in_=in_), [in_], [out])

    def memset(self, eng, out, val):
        return self.add(eng, lambda e: e.memset(out, val), [], [out])

    def dma(self, eng, out, in_, **kw):
        return self.add(eng, lambda e: e.dma_start(out=out, in_=in_, **kw), [in_], [out], dma=True)

    def _analyze(self):
        live = {}
        ndma = {e: 0 for e in self.ENGS}
        dma_ops = {e: [] for e in self.ENGS}
        ops = self.ops
        for op in ops:
            deps = {}
            for regs, is_write in ((op.reads, False), (op.writes, True)):
                for (nm, p0, p1, b0, b1) in regs:
                    if nm[0:2] == 'D:' and nm not in self.track_dram:
                        continue
                    lst = live.get(nm)
                    if not lst:
                        continue
                    psum = nm[0:2] == 'P:'
                    for (q0, q1, c0, c1, j, w) in lst:
                        if q0 < p1 and p0 < q1 and c0 < b1 and b0 < c1 and j != op.idx:
                            if is_write:
                                kind = 'WAW' if w else 'WAR'
                            else:
                                if not w:
                                    if psum and ops[j].eng != op.eng:
                                        kind = 'RAR'
                                    else:
                                        continue
                                else:
                                    kind = 'RAW'
                            if j not in deps or kind == 'RAW':
                                deps[j] = kind
            for (nm, p0, p1, b0, b1) in op.writes:
                if nm[0:2] == 'D:' and nm not in self.track_dram:
                    continue
                lst = live.setdefault(nm, [])
                lst[:] = [t for t in lst if not (p0 <= t[0] and t[1] <= p1 and b0 <= t[2] and t[3] <= b1)]
                lst.append((p0, p1, b0, b1, op.idx, True))
            for (nm, p0, p1, b0, b1) in op.reads:
                if nm[0:2] == 'D:' and nm not in self.track_dram:
                    continue
                live.setdefault(nm, []).append((p0, p1, b0, b1, op.idx, False))
            if op.dma:
                n = ndma[op.eng]
                if n >= self.NDSEM:
                    deps.setdefault(dma_ops[op.eng][n - self.NDSEM].idx, 'SEM')
                dma_ops[op.eng].append(op)
                ndma[op.eng] = n + 1
            need = []
            best = {}
            for j, kind in deps.items():
                p = ops[j]
                if p.dma:
                    need.append(j)
                    continue
                if p.eng == op.eng and not op.dma:
                    if op.eng == 'pe':
                        continue
                if p.eng not in best or best[p.eng] < j:
                    best[p.eng] = j
            need.extend(best.values())
            op.deps = need
            op.signal = False
        for op in ops:
            for j in op.deps:
                ops[j].signal = True
        self.dma_ops = dma_ops

    def emit(self, es):
        nc = self.nc
        self._analyze()
        csem = {e: es.enter_context(nc.semaphore('c_' + e)) for e in ('pe', 'act', 'dve', 'pool')}
        dsem = {e: [es.enter_context(nc.semaphore('d_%s%d' % (e, i))) for i in range(self.NDSEM)]
                for e in ('sp', 'pool', 'act')}
        cnt = {e: 0 for e in self.ENGS}
        nd = {e: 0 for e in self.ENGS}
        for op in self.ops:
            if op.dma:
                n = nd[op.eng]
                nd[op.eng] = n + 1
                op.sem = dsem[op.eng][n % self.NDSEM]
                op.val = 16 * (n // self.NDSEM + 1)
                op.signal = True
            elif op.signal:
                cnt[op.eng] += 1
                op.sem = csem[op.eng]
                op.val = cnt[op.eng]
        self.stats = {e: (len([o for o in self.ops if o.eng == e]), cnt[e], nd[e]) for e in self.ENGS}
        block = es.enter_context(nc.Block())
        per = {e: [op for op in self.ops if op.eng == e] for e in self.ENGS}
        ops = self.ops
        dma_ops = self.dma_ops

        def run(engname, e):
            waited = {}
            for op in per[engname]:
                for j in op.deps:
                    p = ops[j]
                    key = id(p.sem)
                    if waited.get(key, 0) < p.val:
                        e.wait_ge(p.sem, p.val)
                        waited[key] = p.val
                inst = op.fn(e)
                if op.signal:
                    inst.then_inc(op.sem, 16 if op.dma else 1)
            for op in dma_ops[engname]:
                key = id(op.sem)
                if waited.get(key, 0) < op.val:
                    e.wait_ge(op.sem, op.val)
                    waited[key] = op.val

        @block.tensor
        def _(e):
            run('pe', e)

        @block.scalar
        def _(e):
            run('act', e)

        @block.vector
        def _(e):
            run('dve', e)

        @block.gpsimd
        def _(e):
            run('pool', e)

        @block.sync
        def _(e):
            run('sp', e)


def _ap(base, off, pat):
    return bass.AP(base.tensor, int(base.offset) + off, [list(x) for x in pat])


class Prog:
    def __init__(self, cfg):
        self.cfg = cfg
        self.nc = bass.Bass("TRN2", target_bir_lowering=False)
        self.es = ExitStack()
        self.S = Sched(self.nc)
        self.ins = {}
        self.outs = {}
        self.wslot_i = 0
        self.bank_i = 0

    def din(self, name, shape, dt=F32):
        a = self.nc.dram_tensor(name, list(shape), dt, kind="ExternalInput").ap()
        self.ins[name] = a
        return a

    def dout(self, name, shape, dt=F32):
        a = self.nc.dram_tensor(name, list(shape), dt, kind="ExternalOutput").ap()
        self.outs[name] = a
        return a

    def sb(self, name, shape, dt):
        return self.es.enter_context(self.nc.sbuf_tensor(name, list(shape), dt))

    def wload(self, src3):
        kc, mw = src3.shape[1], src3.shape[2]
        assert kc * mw <= self.WELEMS, (kc, mw)
        slot = self.wslots[self.wslot_i % getattr(self, 'NW_eff', self.NW)]
        self.wslot_i += 1
        v = slot[:, 0:kc * mw].rearrange("p (k m) -> p k m", m=mw)
        self.S.dma('pool', v, src3)
        return v

    def bank(self, lo=0, hi=4):
        b = lo + (self.bank_i % (hi - lo))
        self.bank_i += 1
        return b


def build_program(cfg):
    P = Prog(cfg)
    nc, S, es = P.nc, P.S, P.es
    din, dout, sb = P.din, P.dout, P.sb

    xin = din("xin", [NT, D])
    condT_d = din("condT", [128, 8, 2])
    bmodT_d = din("bmodT", [128, 2, 48])
    gvec_d = din("gvec", [128, 5, 8])
    ident_d = din("ident", [128, 128])
    w_mod = din("w_mod", [2, D, 6 * D])
    wg_d = din("ffn_w_gate", [2, D, DFF])
    wu_d = din("ffn_w_up", [2, D, DFF])
    wd_d = din("ffn_w_down", [2, DFF, D])
    y_d = dout("y", [NT, D])
    w_in_odd = din("w_in_odd", [D, D])
    w_in_even = din("w_in_even", [D, 2048])
    w_out_even = din("w_out_even", [D, D])
    ctxkT_d = din("ctxkT", [512, 512])
    ctxv_d = din("ctxv", [512, 512])
    ctxbias_d = din("ctxbias", [128, 1])
    maskt_d = din("maskt", [128, 40, 128], BF16)
    rpbH_d = din("rpbH", [8, 19, 128])
    sA_lam_d = din("sA_lam", [128, 2, 4, 2, 64])
    sA_ls_d = din("sA_ls", [128, 4, 2])
    sA_b_d = din("sA_b", [128, 2, 4, 2, 64])
    parA_d = din("parA", [128, 2])
    sB_lam_d = din("sB_lam", [128, 2, 32])
    sB_ls_d = din("sB_ls", [128, 32])
    sB_c_d = din("sB_c", [128, 2, 32, 16])
    sB_b_d = din("sB_b", [128, 2, 32, 16])
    s0B_d = din("s0B", [128, 64])
    carry_d = din("carry", [128, 1])
    sdT_d = din("sdT", [128, 4])
    glubT_d = din("glubT", [128, 4])
    glu_w_d = din("s5_glu_w", [512, 512])
    so_d = dout("so", [5, 128, 64])
    ko_d = dout("ko", [NT, 512])
    vo_d = dout("vo", [NT, 512])
    w_out_odd = din("w_out_odd", [D, D])
    cs256_d = din("cs256", [256, 512], BF16)
    dftL_d = din("dftL", [2, 1024, 1024], BF16)
    dft4_d = din("dft4", [2, 256, 256], BF16)

    xT = sb("xT", [128, 8, NT], F32)
    rstd = sb("rstd", [128, NT], F32)
    ntmp = sb("ntmp", [128, 2, 512], F32)
    ident_bf = sb("ident_bf", [128, 128], BF16)
    ctxbias = sb("ctxbias_s", [128, 1], F32)
    sbm = sb("sbm", [128, 1280], F32)
    ident = sb("ident_s", [128, 128], F32)
    ones_bf = sb("ones_bf", [128, 128], BF16)
    condT = sb("condT_s", [128, 8, 2], F32)
    scT = sb("scT", [128, 8, 2], BF16)
    bmodT = sb("bmodT_s", [128, 2, 48], F32)
    gvec = sb("gvec_s", [128, 5, 8], F32)
    modT = sb("modT", [128, 2, 48, 2], F32)
    gs = sb("gs", [128, 4, 8, 2], F32)
    P.NW = 5
    P.WELEMS = 4096
    wsl = sb("wsl", [128, P.NW * P.WELEMS + 768], BF16)
    P.wslots = [wsl[:, i * P.WELEMS:(i + 1) * P.WELEMS] for i in range(P.NW)]
    ARENA_BYTES = 111104
    arena = sb("arena", [128, ARENA_BYTES // 2], BF16)
    PS = es.enter_context(nc.psum_tensor("PS", [128, 8, 512], F32))
    PSM = PS[:, 7, 0:256].rearrange("p (l c) -> p l c", l=2)

    def aview(off_bytes, shape, dt):
        esz = _DTSIZE[dt]
        n = int(np.prod(shape[1:]))
        assert off_bytes % 4 == 0 and off_bytes + n * esz <= ARENA_BYTES, (off_bytes, shape)
        a = arena[:, off_bytes // 2: off_bytes // 2 + n * esz // 2]
        if dt != BF16:
            a = a.bitcast(dt)
        if len(shape) == 2:
            return a
        names = " ".join("d%d" % i for i in range(len(shape) - 1))
        kw = {"d%d" % i: shape[i + 1] for i in range(len(shape) - 1)}
        return a.rearrange("p (%s) -> p %s" % (names, names), **kw)

    S.dma('sp', ident[:], ident_d)
    S.dma('sp', condT[:], condT_d)
    S.dma('sp', bmodT[:], bmodT_d)
    S.dma('sp', gvec[:], gvec_d)
    S.memset('dve', ones_bf[:], 1.0)
    S.copy('dve', ident_bf[:], ident[:])
    S.dma('sp', ctxbias[:], ctxbias_d)
    S.act(scT[:], condT[:], AF.Silu)

    def modulation_gen(layer):
        for u in range(12):
            W = P.wload(w_mod[layer][:, u * 512:(u + 1) * 512].rearrange("(k p) m -> p k m", p=128))
            for mt in range(4):
                col = (u * 4 + mt) * 2
                for kc in range(8):
                    S.matmul(PSM[:, layer, col:col + 2], W[:, kc, mt * 128:(mt + 1) * 128], scT[:, kc, :],
                             start=(kc == 0), stop=(kc == 7))
            yield u
        src = PSM[:, layer, 0:96].rearrange("p (m c) -> p m c", c=2)
        b_ = bmodT[:, layer, :]
        bb = _ap(b_, 0, [b_.ap[0], [1, 48], [0, 2]])
        S.tt('dve', modT[:, layer, :, :], src, bb, ALU.add)
        for which in range(2):
            sc = modT[:, layer, (1 + 3 * which) * 8:(2 + 3 * which) * 8, :]
            g_ = gvec[:, layer * 2 + which, :]
            gb = _ap(g_, 0, [g_.ap[0], [1, 8], [0, 2]])
            S.stt(gs[:, layer * 2 + which, :, :], sc, 1.0, gb, ALU.add, ALU.mult)

    def modulation(layer):
        for _ in modulation_gen(layer):
            pass

    def mod_vec(layer, j, c, ci):
        return modT[:, layer, j * 8 + c, ci:ci + 1]

    hT = aview(0, [128, 8, NT], BF16)
    sq = aview(20480, [128, 8, NT], BF16)

    def sumsq_rstd_tb(s, n):
        S.act(sq[:, :, s:s + n], xT[:, :, s:s + n], AF.Square)
        b = P.bank(0, 4)
        for c in range(8):
            S.matmul(PS[:, b, 0:n], ones_bf[:], sq[:, c, s:s + n], start=(c == 0), stop=(c == 7))
        S.act(rstd[:, s:s + n], PS[:, b, 0:n], AF.Sqrt, bias=epsb[:, 0:1], scale=1.0 / D)
        S.recip(rstd[:, s:s + n], rstd[:, s:s + n])

    def sumsq_rstd():
        for (s, n) in TB:
            sumsq_rstd_tb(s, n)

    def norm_mod(layer, which):
        k = 0
        for ti, (s, n) in enumerate(TB):
            sumsq_rstd_tb(s, n)
        for ti, (s, n) in enumerate(TB):
            ci = 0 if ti < 2 else 1
            for c in range(8):
                t = ntmp[:, k % 2, 0:n]
                k += 1
                S.tt('dve', t, xT[:, c, s:s + n], rstd[:, s:s + n], ALU.mult)
                S.act(hT[:, c, s:s + n], t, AF.Identity,
                      bias=mod_vec(layer, 3 * which, c, ci), scale=gs[:, layer * 2 + which, c, ci:ci + 1])

    epsb = sb("epsb", [128, 1], F32)
    S.memset('dve', epsb[:], 1e-6)

    aT = aview(20480, [128, 22, NT], BF16)
    sgt = [aview(76800, [128, 512], BF16), aview(77824, [128, 512], BF16)]

    def ffn(layer):
        groups = [(i * 512, 512) for i in range(5)] + [(2560, 256)]
        k = 0
        for (m0, mw) in groups:
            Wg = P.wload(wg_d[layer][:, m0:m0 + mw].rearrange("(k p) m -> p k m", p=128))
            Wu = P.wload(wu_d[layer][:, m0:m0 + mw].rearrange("(k p) m -> p k m", p=128))
            for mt in range(mw // 128):
                j = m0 // 128 + mt
                for (s, n) in TB:
                    bg = P.bank(0, 6)
                    bu = P.bank(0, 6)
                    for kc in range(8):
                        S.matmul(PS[:, bg, 0:n], Wg[:, kc, mt * 128:(mt + 1) * 128], hT[:, kc, s:s + n],
                                 start=(kc == 0), stop=(kc == 7))
                    for kc in range(8):
                        S.matmul(PS[:, bu, 0:n], Wu[:, kc, mt * 128:(mt + 1) * 128], hT[:, kc, s:s + n],
                                 start=(kc == 0), stop=(kc == 7))
                    t = sgt[k % 2][:, 0:n]
                    k += 1
                    S.act(t, PS[:, bg, 0:n], AF.Silu)
                    S.tt('dve', aT[:, j, s:s + n], t, PS[:, bu, 0:n], ALU.mult)
        for mg in range(4):
            Wd = [P.wload(wd_d[layer][kh * 1408:(kh + 1) * 1408, mg * 256:(mg + 1) * 256]
                          .rearrange("(k p) m -> p k m", p=128)) for kh in range(2)]
            for mt in range(2):
                m = mg * 2 + mt
                for ti, (s, n) in enumerate(TB):
                    b = P.bank(0, 6)
                    for kk in range(22):
                        S.matmul(PS[:, b, 0:n], Wd[kk // 11][:, kk % 11, mt * 128:(mt + 1) * 128],
                                 aT[:, kk, s:s + n], start=(kk == 0), stop=(kk == 21))
                    ci = 0 if ti < 2 else 1
                    S.stt(xT[:, m, s:s + n], PS[:, b, 0:n], mod_vec(layer, 5, m, ci), xT[:, m, s:s + n],
                          ALU.mult, ALU.add)

    def evac(i, out, in_):
        S.copy('act' if i % 2 == 0 else 'dve', out, in_)

    def fnet_mixer(layer):
        zT = aview(20480, [128, 8, NT], BF16)
        ZCS = aview(40960, [128, 10, 4, 512], BF16)
        fT = aview(0, [128, 8, NT], BF16)
        cs256 = aview(81920, [128, 2, 512], BF16)
        dft4 = aview(83968, [128, 2, 2, 256], BF16)
        S.dma('sp', cs256, cs256_d.rearrange("(c p) m -> p c m", p=128))
        S.dma('sp', dft4, dft4_d.rearrange("a (t p) k -> p a t k", p=128))
        norm_mod(layer, 0)
        k = 0
        for u in range(2):
            W = P.wload(w_in_odd[:, u * 512:(u + 1) * 512].rearrange("(k p) m -> p k m", p=128))
            for mt in range(4):
                m = u * 4 + mt
                for (s_, n) in TB:
                    b = P.bank(0, 6)
                    for kc in range(8):
                        S.matmul(PS[:, b, 0:n], W[:, kc, mt * 128:(mt + 1) * 128], hT[:, kc, s_:s_ + n],
                                 start=(kc == 0), stop=(kc == 7))
                    evac(k, zT[:, m, s_:s_ + n], PS[:, b, 0:n]); k += 1
        for tt in range(10):
            for gq in range(4):
                b = P.bank(0, 6)
                for cc in range(2):
                    S.matmul(PS[:, b, :], zT[:, 2 * gq + cc, tt * 128:(tt + 1) * 128], cs256[:, cc, :],
                             start=(cc == 0), stop=(cc == 1))
                evac(k, ZCS[:, tt, gq, :], PS[:, b, :]); k += 1
        for kb in range(2):
            CL = P.wload(dftL_d[0][:, kb * 512:(kb + 1) * 512].rearrange("(t p) k -> p t k", p=128))
            SL = P.wload(dftL_d[1][:, kb * 512:(kb + 1) * 512].rearrange("(t p) k -> p t k", p=128))
            for m in range(8):
                gq, half = m // 2, m % 2
                b = P.bank(0, 6)
                for tt in range(8):
                    S.matmul(PS[:, b, :], ZCS[:, tt, gq, half * 128:(half + 1) * 128], CL[:, tt, :],
                             start=(tt == 0), stop=False)
                for tt in range(8):
                    S.matmul(PS[:, b, :], ZCS[:, tt, gq, 256 + half * 128:256 + (half + 1) * 128], SL[:, tt, :],
                             start=False, stop=(tt == 7))
                evac(k, fT[:, m, kb * 512:(kb + 1) * 512], PS[:, b, :]); k += 1
        for m in range(8):
            gq, half = m // 2, m % 2
            b = P.bank(0, 6)
            i = 0
            for a in range(2):
                for t in range(2):
                    S.matmul(PS[:, b, 0:256], ZCS[:, 8 + t, gq, a * 256 + half * 128:a * 256 + (half + 1) * 128],
                             dft4[:, a, t, :], start=(i == 0), stop=(i == 3))
                    i += 1
            evac(k, fT[:, m, 1024:1280], PS[:, b, 0:256]); k += 1
        for u in range(2):
            W = P.wload(w_out_odd[:, u * 512:(u + 1) * 512].rearrange("(k p) m -> p k m", p=128))
            for mt in range(4):
                m = u * 4 + mt
                for ti, (s_, n) in enumerate(TB):
                    b = P.bank(0, 6)
                    for kc in range(8):
                        S.matmul(PS[:, b, 0:n], W[:, kc, mt * 128:(mt + 1) * 128], fT[:, kc, s_:s_ + n],
                                 start=(kc == 0), stop=(kc == 7))
                    ci = 0 if ti < 2 else 1
                    S.stt(xT[:, m, s_:s_ + n], PS[:, b, 0:n], mod_vec(layer, 2, m, ci), xT[:, m, s_:s_ + n],
                          ALU.mult, ALU.add)

    KB = [[0, 1, 2, 3], [0, 1, 2, 3], [0, 1, 2, 3, 4], [1, 2, 3, 4, 5], [2, 3, 4, 5, 6], [3, 4, 5, 6, 7],
          [4, 5, 6, 7], [4, 5, 6, 7]]
    qT = aview(40960, [128, 4, NT], BF16)
    kT = aview(51200, [128, 4, NT], BF16)
    uTp = aview(61440, [128, 4, 8, 160], BF16)
    vtok = aview(71680, [128, 10, 512], BF16)
    catT = aview(81920, [128, 8, NT], BF16)
    kvst = [aview(102400, [128, 512], F32), aview(104448, [128, 512], F32)]
    rec = aview(106496, [128, 2, 256], F32)
    rstage = [aview(108544, [128, 9, 128], BF16)]

    def pp(n, nb=None):
        for attr, cnt in (('s5prepA', n), ('s5prepB', n if nb is None else nb)):
            for _ in range(cnt):
                g_ = getattr(P, attr, None)
                if g_ is None:
                    break
                try:
                    next(g_)
                except StopIteration:
                    setattr(P, attr, None)

    def projections():
        norm_mod(0, 0)

        def wl(i):
            return P.wload(w_in_even[:, i * 512:(i + 1) * 512].rearrange("(k p) m -> p k m", p=128))
        k = 0
        Wq = wl(0)
        Wk = wl(1)
        for mt in range(4):
            for (s_, n) in TB:
                b = P.bank(0, 6)
                for kc in range(8):
                    S.matmul(PS[:, b, 0:n], Wq[:, kc, mt * 128:(mt + 1) * 128], hT[:, kc, s_:s_ + n],
                             start=(kc == 0), stop=(kc == 7))
                S.act(qT[:, mt, s_:s_ + n], PS[:, b, 0:n], AF.Identity, scale=0.125)
            pp(1)
        Wv = wl(2)
        for mt in range(4):
            for (s_, n) in TB:
                b = P.bank(0, 6)
                for kc in range(8):
                    S.matmul(PS[:, b, 0:n], Wk[:, kc, mt * 128:(mt + 1) * 128], hT[:, kc, s_:s_ + n],
                             start=(kc == 0), stop=(kc == 7))
                S.copy('act', kT[:, mt, s_:s_ + n], PS[:, b, 0:n])
            pp(1)
        for tt in range(10):
            b = P.bank(0, 6)
            for kc in range(8):
                S.matmul(PS[:, b, :], hT[:, kc, tt * 128:(tt + 1) * 128], Wk[:, kc, :],
                         start=(kc == 0), stop=(kc == 7))
            st = kvst[tt % 2]
            S.copy('act', st, PS[:, b, :])
            S.dma('sp', ko_d[tt * 128:(tt + 1) * 128, :], st)
            pp(1)
        Wu = wl(3)
        for tt in range(10):
            b = P.bank(0, 6)
            for kc in range(8):
                S.matmul(PS[:, b, :], hT[:, kc, tt * 128:(tt + 1) * 128], Wv[:, kc, :],
                         start=(kc == 0), stop=(kc == 7))
            st = kvst[tt % 2]
            S.copy('act', st, PS[:, b, :])
            S.copy('act', vtok[:, tt, :], PS[:, b, :])
            S.dma('sp', vo_d[tt * 128:(tt + 1) * 128, :], st)
            pp(1)
        for mt in range(4):
            for (s_, n) in TB:
                b = P.bank(0, 6)
                for kc in range(8):
                    S.matmul(PS[:, b, 0:n], Wu[:, kc, mt * 128:(mt + 1) * 128], hT[:, kc, s_:s_ + n],
                             start=(kc == 0), stop=(kc == 7))
                pv = PS[:, b, 0:n]
                src = _ap(pv, 0, [pv.ap[0], [1, 8], [8, n // 8]])
                dst = uTp[:, mt, :, s_ // 8:(s_ + n) // 8]
                S.copy('act', dst, src)
            pp(1)

    def attention():
        ctxkT = aview(0, [128, 4, 512], BF16)
        ctxv = aview(4096, [128, 4, 512], BF16)
        E = aview(8192, [128, 2, 10, 256], BF16)
        maskt = aview(18432, [128, 40, 128], BF16)
        rpbT = aview(28672, [128, 2, 9, 128], BF16)
        stgs = [rstage[0], aview(103680, [128, 9, 128], BF16)]
        rec2 = aview(106496, [128, 2, 256], F32)
        S.dma('pool', ctxkT, ctxkT_d.rearrange("(c p) k -> p c k", p=128))
        S.dma('pool', ctxv, ctxv_d.rearrange("(j p) f -> p j f", p=128))
        S.dma('sp', maskt, maskt_d)
        UN = [[0, 1, 2, 3], [0, 1, 2, 3, 4, 5], [2, 3, 4, 5, 6, 7], [4, 5, 6, 7]]
        ti0 = [0, 8, 20, 32]

        def build_bias(h):
            stg = stgs[h % 2]
            for krl in range(2):
                src = bass.AP(rpbH_d.tensor, h * 19 * 128 + krl * 128, [[1, 64], [256, 9], [128, 2], [1, 64]])
                dst = stg[krl * 64:(krl + 1) * 64, :, :].rearrange("p a (r c) -> p a r c", c=64)
                S.dma('pool', dst, src)

        its = []
        for j_ in range(4):
            for (hh, mp_) in ((0, 0), (0, 1), (1, 0), (1, 1), (0, 2), (0, 3), (1, 2), (1, 3)):
                its.append((2 * j_ + hh, mp_))

        def build_bm(it):
            h, mp = its[it]
            if mp == 0:
                a_ = stgs[h % 2][:, :, :]
                rev = _ap(a_, 127, [a_.ap[0], [128, 9], [-1, 128]])
                S.copy('act', rpbT[:, h % 2, :, :], rev)
            if mp == 3 and h + 2 < 8:
                build_bias(h + 2)

        def sreg(slot):
            b = slot // 2
            return PS[:, b, (slot % 2) * 256:(slot % 2) * 256 + 256]

        def scores(it):
            h, mp = its[it]
            hp, hc = h % 2, h // 2
            pr = slice(64 * hp, 64 * hp + 64)
            Ju = len(UN[mp])
            qsl = qT[pr, hc, mp * 256:(mp + 1) * 256]
            for jn, n in enumerate(UN[mp]):
                S.matmul(sreg(jn), kT[pr, hc, n * 128:(n + 1) * 128], qsl, start=(jn % 2 == 0), stop=False,
                         skip_group_check=True)
            for j in range(4):
                S.matmul(sreg(6 + j), ctxkT[pr, hc, j * 128:(j + 1) * 128], qsl, start=(j % 2 == 0), stop=True,
                         skip_group_check=True)
            rp = rpbT[:, h % 2, :, :]
            for jn, n in enumerate(UN[mp]):
                t0_ = ti0[mp] + 2 * jn
                S.matmul(sreg(jn), ident_bf[:], maskt[:, t0_:t0_ + 2, :].rearrange("p a c -> p (a c)"),
                         start=False, stop=False, skip_group_check=True)
                d0 = n - 2 * mp + 4
                rhs = _ap(rp, d0 * 128, [rp.ap[0], [-128, 2], [1, 128]])
                S.matmul(sreg(jn), ident_bf[:], rhs, start=False, stop=True, skip_group_check=True)

        def exps(it):
            h, mp = its[it]
            Ju = len(UN[mp])
            Et = E[:, it % 2, :, :].rearrange("p a c -> p (a c)")
            Sl = PS[:, 0:3, :].rearrange("p b c -> p (b c)")
            Sc = PS[:, 3:5, :].rearrange("p b c -> p (b c)")
            S.act(Et[:, 0:Ju * 256], Sl[:, 0:Ju * 256], AF.Exp)
            S.act(Et[:, 1536:2560], Sc, AF.Exp, bias=ctxbias[:, 0:1])

        def pv(it):
            h, mp = its[it]
            hp, hc = h % 2, h // 2
            pr = slice(64 * hp, 64 * hp + 64)
            Ju = len(UN[mp])
            bnk = 5 + mp % 2
            num = PS[pr, bnk, 0:256]
            den = PS[pr, bnk, 256:512]
            tot = Ju + 4
            for j in range(tot):
                lv = vtok[:, UN[mp][j], h * 64:(h + 1) * 64] if j < Ju else ctxv[:, j - Ju, h * 64:(h + 1) * 64]
                ev = E[:, it % 2, j if j < Ju else 6 + j - Ju, :]
                S.matmul(num, lv, ev, start=(j == 0), stop=(j == tot - 1))
            for j in range(tot):
                ev = E[:, it % 2, j if j < Ju else 6 + j - Ju, :]
                S.matmul(den, ones_bf[:, 0:64], ev, start=(j == 0), stop=(j == tot - 1))
            if hp == 1:
                rc = rec2[:, mp % 2, :]
                S.recip(rc, PS[:, bnk, 256:512])
                S.tt('dve', catT[:, hc, mp * 256:(mp + 1) * 256], PS[:, bnk, 0:256], rc, ALU.mult)

        def pump():
            if getattr(P, 'mod1_gen', None) is not None:
                try:
                    next(P.mod1_gen)
                except StopIteration:
                    P.mod1_gen = None

        def pump_prep(n):
            for _ in range(n):
                if getattr(P, 's5prep', None) is None:
                    return
                try:
                    next(P.s5prep)
                except StopIteration:
                    P.s5prep = None

        NI = len(its)
        build_bias(0)
        build_bias(1)
        build_bm(0)
        build_bm(1)
        scores(0)
        exps(0)
        for it in range(1, NI + 1):
            if it + 1 < NI:
                build_bm(it + 1)
            if it < NI:
                scores(it)
                exps(it)
            pv(it - 1)
            for _ in range(4):
                if getattr(P, 'scan_gen', None) is not None:
                    try:
                        next(P.scan_gen)
                    except StopIteration:
                        P.scan_gen = None
        it = 0
        for h in range(8):
            hp, hc = h % 2, h // 2
            pr = slice(64 * hp, 64 * hp + 64)
            Sreg = PS[:, it % 2, :]
            Et = E[:, it % 2, 0:2, :]
            for kb in range(2):
                S.matmul(Sreg[:, kb * 256:(kb + 1) * 256], kT[pr, hc, 1024 + kb * 128:1024 + (kb + 1) * 128],
                         qT[pr, hc, 1024:1280], start=(kb == 0), stop=True, skip_group_check=True)
            S.act(Et.rearrange("p a c -> p (a c)"), Sreg, AF.Exp)
            bnk = 5 + hc % 2
            num = PS[pr, bnk, 0:256]
            den = PS[pr, bnk, 256:512]
            for kb in range(2):
                S.matmul(num, vtok[:, 8 + kb, h * 64:(h + 1) * 64], Et[:, kb, :], start=(kb == 0), stop=(kb == 1))
            for kb in range(2):
                S.matmul(den, ones_bf[:, 0:64], Et[:, kb, :], start=(kb == 0), stop=(kb == 1))
            if hp == 1:
                rc = rec2[:, hc % 2, :]
                S.recip(rc, PS[:, bnk, 256:512])
                S.tt('dve', catT[:, hc, 1024:1280], PS[:, bnk, 0:256], rc, ALU.mult)
            it += 1

    def out_proj_even():
        for u in range(2):
            W = P.wload(w_out_even[:, u * 512:(u + 1) * 512].rearrange("(k p) m -> p k m", p=128))
            for mt in range(4):
                m = u * 4 + mt
                for ti, (s_, n) in enumerate(TB):
                    b = P.bank(0, 6)
                    for kc in range(8):
                        S.matmul(PS[:, b, 0:n], W[:, kc, mt * 128:(mt + 1) * 128], catT[:, kc, s_:s_ + n],
                                 start=(kc == 0), stop=(kc == 7))
                    ci = 0 if ti < 2 else 1
                    S.stt(xT[:, m, s_:s_ + n], PS[:, b, 0:n], mod_vec(0, 2, m, ci), xT[:, m, s_:s_ + n],
                          ALU.mult, ALU.add)

    TWO_PI = 6.283185

    def disc(eng, lam, ls_b, T, F):
        ar, ai, cr, ci, lr, t0, t1, t2, t3, mag = T[:10]
        ti = T[10].bitcast(I32)
        S.ts(eng, lr, lam[:, 0], -1e-4, ALU.min)
        S.tt(eng, t0, lr, ls_b, ALU.mult)
        S.act(mag, t0, AF.Exp)
        S.tt(eng, t0, lam[:, 1], ls_b, ALU.mult)
        S.ts(eng, t0, t0, 1.0 / (2 * np.pi), ALU.mult)
        S.copy(eng, ti, t0)
        S.copy(eng, t1, ti)
        S.tt(eng, t1, t0, t1, ALU.subtract)
        S.act(t2, t1, AF.Sin, scale=TWO_PI)
        S.tt(eng, ai, mag, t2, ALU.mult)
        S.ts(eng, t0, t0, 0.25, ALU.add)
        S.copy(eng, ti, t0)
        S.copy(eng, t1, ti)
        S.tt(eng, t1, t0, t1, ALU.subtract)
        S.act(t2, t1, AF.Sin, scale=TWO_PI)
        S.tt(eng, ar, mag, t2, ALU.mult)
        li = lam[:, 1]
        S.ts(eng, t0, ar, -1.0, ALU.add)
        S.tt(eng, t1, lr, lr, ALU.mult)
        S.tt(eng, t2, li, li, ALU.mult)
        S.tt(eng, t1, t1, t2, ALU.add)
        S.recip(t1, t1)
        S.tt(eng, t2, t0, lr, ALU.mult)
        S.tt(eng, t3, ai, li, ALU.mult)
        S.tt(eng, t2, t2, t3, ALU.add)
        S.tt(eng, cr, t2, t1, ALU.mult)
        S.tt(eng, t2, ai, lr, ALU.mult)
        S.tt(eng, t3, t0, li, ALU.mult)
        S.tt(eng, t2, t2, t3, ALU.subtract)
        S.tt(eng, ci, t2, t1, ALU.mult)
        return ar, ai, cr, ci

    def bc(ap2, shape, pat):
        return _ap(ap2, 0, [ap2.ap[0]] + pat)

    def tview(t2d, off_bytes, shape, dt):
        esz0 = _DTSIZE[t2d.dtype]
        esz = _DTSIZE[dt]
        n = int(np.prod(shape[1:]))
        a = t2d[:, off_bytes // esz0:(off_bytes + n * esz) // esz0]
        if dt != t2d.dtype:
            a = a.bitcast(dt)
        if len(shape) == 2:
            return a
        names = " ".join("d%d" % i for i in range(len(shape) - 1))
        kw = {"d%d" % i: shape[i + 1] for i in range(len(shape) - 1)}
        return a.rearrange("p (%s) -> p %s" % (names, names), **kw)

    lsA = sbm[:, 0:8]
    parA = sbm[:, 8:10]
    carry = sbm[:, 10:11]
    sdT = sbm[:, 12:16]
    glubT = sbm[:, 16:20]
    dtA = sbm[:, 20:28]
    lamB = sbm[:, 32:96].rearrange("p (a f) -> p a f", a=2)
    lsB = sbm[:, 96:128]
    dtB = sbm[:, 128:160]
    Fin = sbm[:, 160:480].rearrange("p (i t) -> p i t", i=5)
    PB = sbm[:, 480:1056].rearrange("p (a e f) -> p a e f", a=2, e=9)
    lam_rr = sbm[:, 1056:1120].rearrange("p (f r) -> p f r", r=2)
    lam_is = sbm[:, 1120:1184].rearrange("p (f r) -> p f r", r=2)
    crB = sbm[:, 1184:1216]
    ciB = sbm[:, 1216:1248]
    WinD = nc.dram_tensor("WinD", [4, 128, 4096], BF16, kind="Internal").ap()
    WoutD = nc.dram_tensor("WoutD", [4, 128, 4096], BF16, kind="Internal").ap()
    ToepD = nc.dram_tensor("ToepD", [4, 128, 2048], BF16, kind="Internal").ap()
    for nm_ in ("WinD", "WoutD", "ToepD"):
        S.track_dram.add('D:' + nm_)
    rs2 = rstd[:, :]
    nt2 = ntmp[:, :, :].rearrange("p a b -> p (a b)")

    def s5_small():
        eng = 'dve'
        S.dma('sp', lsA, sA_ls_d.rearrange("p q d -> p (q d)"))
        S.dma('sp', parA, parA_d)
        S.dma('sp', carry, carry_d)
        S.dma('sp', sdT, sdT_d)
        S.dma('sp', glubT, glubT_d)
        S.act(dtA, lsA, AF.Exp)
        S.dma('sp', lamB, sB_lam_d)
        S.dma('sp', lsB, sB_ls_d)
        S.act(dtB, lsB, AF.Exp)
        TB_ = [tview(rs2, 1536 + 128 * i, [128, 32], F32) for i in range(11)]
        arB, aiB, crB_, ciB_ = disc(eng, lamB, dtB, TB_, 32)
        S.copy(eng, crB, crB_)
        S.copy(eng, ciB, ciB_)
        S.memset(eng, PB[:, 0, 0, :], 1.0)
        S.memset(eng, PB[:, 1, 0, :], 0.0)
        S.copy(eng, PB[:, 0, 1, :], arB)
        S.copy(eng, PB[:, 1, 1, :], aiB)
        u1, u2 = TB_[4], TB_[5]
        for e in range(2, 9):
            S.tt(eng, u1, PB[:, 0, e - 1, :], arB, ALU.mult)
            S.tt(eng, u2, PB[:, 1, e - 1, :], aiB, ALU.mult)
            S.tt(eng, PB[:, 0, e, :], u1, u2, ALU.subtract)
            S.tt(eng, u1, PB[:, 0, e - 1, :], aiB, ALU.mult)
            S.tt(eng, u2, PB[:, 1, e - 1, :], arB, ALU.mult)
            S.tt(eng, PB[:, 1, e, :], u1, u2, ALU.add)
        S.copy(eng, lam_rr[:, :, 0], PB[:, 0, 8, :])
        S.copy(eng, lam_rr[:, :, 1], PB[:, 0, 8, :])
        S.copy(eng, lam_is[:, :, 0], PB[:, 1, 8, :])
        S.ts(eng, lam_is[:, :, 1], PB[:, 1, 8, :], -1.0, ALU.mult)

    def s5_prep_batched():
        eng = 'dve'
        Win = aview(8192, [128, 4, 2, 8, 2, 128], BF16)
        T_ = [aview(40960 + 2048 * i, [128, 4, 2, 64], F32) for i in range(11)]
        lamA = aview(63488, [128, 2, 4, 2, 64], F32)
        bA = aview(67584, [128, 2, 4, 2, 64], F32)
        S.dma('sp', lamA, sA_lam_d)
        S.dma('sp', bA, sA_b_d)
        dt_b = _ap(dtA, 0, [dtA.ap[0], [2, 4], [1, 2], [0, 64]])
        arA, aiA, crA, ciA = disc(eng, lamA, dt_b, T_, 512)
        bre = bA[:, 0]
        bim = bA[:, 1]
        bbr, bbi, ua, ub = T_[4], T_[5], T_[6], T_[7]
        S.tt(eng, ua, crA, bre, ALU.mult)
        S.tt(eng, ub, ciA, bim, ALU.mult)
        S.tt(eng, bbr, ua, ub, ALU.subtract)
        S.tt(eng, ua, crA, bim, ALU.mult)
        S.tt(eng, ub, ciA, bre, ALU.mult)
        S.tt(eng, bbi, ua, ub, ALU.add)
        Wr = [bbr, T_[8]]
        Wi = [bbi, T_[9]]
        yield
        for e in range(8):
            cr_, ci_ = Wr[e % 2], Wi[e % 2]
            if e > 0:
                pr_, pi_ = Wr[(e - 1) % 2], Wi[(e - 1) % 2]
                S.tt(eng, ua, pr_, arA, ALU.mult)
                S.tt(eng, ub, pi_, aiA, ALU.mult)
                S.tt(eng, cr_, ua, ub, ALU.subtract)
                S.tt(eng, ua, pr_, aiA, ALU.mult)
                S.tt(eng, ub, pi_, arA, ALU.mult)
                S.tt(eng, ci_, ua, ub, ALU.add)
            for gp in range(2):
                S.act(Win[:, :, 0, e, :, gp * 64:(gp + 1) * 64], cr_, AF.Copy, scale=parA[:, gp:gp + 1])
                S.act(Win[:, :, 1, e, :, gp * 64:(gp + 1) * 64], ci_, AF.Copy, scale=parA[:, gp:gp + 1])
        for q in range(4):
            S.dma('sp', WinD[q], Win[:, q].rearrange("p a b c d -> p (a b c d)"))
        Toep = aview(8192, [128, 4, 16, 128], BF16)
        Wo = aview(40960, [128, 32, 2, 8, 32], BF16)
        Wo0 = aview(73728, [128, 32, 2, 32], BF16)
        BbS = aview(77824, [128, 32, 2, 32], BF16)
        cB = aview(81920, [128, 2, 32, 16], F32)
        bB = aview(86016, [128, 2, 32, 16], F32)
        w1b = [aview(90112, [128, 32, 16], F32), aview(92160, [128, 32, 16], F32)]
        w2 = aview(94208, [128, 32, 16], F32)
        Dd = aview(96256, [128, 4, 128], F32)
        S.dma('sp', cB, sB_c_d)
        S.dma('sp', bB, sB_b_d)
        cr_b = _ap(crB, 0, [crB.ap[0], [1, 32], [0, 16]])
        ci_b = _ap(ciB, 0, [ciB.ap[0], [1, 32], [0, 16]])
        S.memset('pool', BbS[:, :, :, :].rearrange("p a b c -> p (a b c)"), 0.0)
        S.memset('pool', Wo[:, :, :, :, :].rearrange("p a b c d -> p (a b c d)"), 0.0)
        S.memset('pool', Wo0[:, :, :, :].rearrange("p a b c -> p (a b c)"), 0.0)
        for ri in range(2):
            wx = w1b[ri]
            if ri == 0:
                S.tt(eng, wx, cr_b, bB[:, 0], ALU.mult)
                S.tt(eng, w2, ci_b, bB[:, 1], ALU.mult)
                S.tt(eng, wx, wx, w2, ALU.subtract)
            else:
                S.tt(eng, wx, cr_b, bB[:, 1], ALU.mult)
                S.tt(eng, w2, ci_b, bB[:, 0], ALU.mult)
                S.tt(eng, wx, wx, w2, ALU.add)
            for gp in range(2):
                ps_ = slice(64 * gp, 64 * gp + 64)
                S.copy('act', BbS[ps_, :, ri, 16 * gp:16 * gp + 16], wx[ps_, :, :])
        for e in range(9):
            pr_b = _ap(PB, (0 * 9 + e) * 32, [PB.ap[0], [1, 32], [0, 16]])
            pi_b = _ap(PB, (1 * 9 + e) * 32, [PB.ap[0], [1, 32], [0, 16]])
            for ri in range(2):
                wx = w1b[ri]
                if ri == 0:
                    S.tt(eng, wx, cB[:, 0], pr_b, ALU.mult)
                    S.tt(eng, w2, cB[:, 1], pi_b, ALU.mult)
                    S.tt(eng, wx, wx, w2, ALU.subtract)
                    sgn = 1.0
                else:
                    S.tt(eng, wx, cB[:, 0], pi_b, ALU.mult)
                    S.tt(eng, w2, cB[:, 1], pr_b, ALU.mult)
                    S.tt(eng, wx, wx, w2, ALU.add)
                    sgn = -1.0
                for gp in range(2):
                    ps_ = slice(64 * gp, 64 * gp + 64)
                    if e == 0:
                        dst = Wo0[ps_, :, ri, 16 * gp:16 * gp + 16]
                    else:
                        dst = Wo[ps_, :, ri, e - 1, 16 * gp:16 * gp + 16]
                    S.act(dst, wx[ps_, :, :], AF.Copy, scale=sgn)
        for q in range(4):
            S.dma('sp', WoutD[q], Wo[:, q * 8:(q + 1) * 8].rearrange("p a b c d -> p (a b c d)"))

        def wo(f, ri, tau):
            return Wo0[:, f, ri, :] if tau == 0 else Wo[:, f, ri, tau - 1, :]

        kk = 0
        for q in range(4):
            S.ts(eng, Dd[:, q, :], ident[:], sdT[:, q:q + 1], ALU.mult)
            for bi in range(4):
                b = P.bank(0, 6)
                S.memset(eng, PS[:, b, :], 0.0)
                for s4 in range(4):
                    slot = bi * 4 + s4
                    if slot > 14:
                        continue
                    if slot < 7:
                        combos = [(0, slot + 1)]
                    elif slot < 14:
                        combos = [(1, slot - 6)]
                    else:
                        combos = [(0, 0), (1, 0)]
                    n_ = len(combos) * 2
                    i_ = 0
                    for (d, tau) in combos:
                        for ri in range(2):
                            for pr in range(4):
                                f = (q * 2 + d) * 4 + pr
                                o = PS[32 * pr:32 * pr + 32, b, s4 * 128 + 32 * pr:s4 * 128 + 32 * pr + 32]
                                S.matmul(o, BbS[:, f, ri, :], wo(f, ri, tau),
                                         start=(i_ == 0), stop=(i_ == n_ - 1), tile_position=(0, 32 * pr))
                            i_ += 1
                if bi < 3:
                    evac(kk, Toep[:, q, bi * 4:bi * 4 + 4, :], PS[:, b, :].rearrange("p (s c) -> p s c", c=128)); kk += 1
                else:
                    S.copy('act', Toep[:, q, 12:14, :], PS[:, b, 0:256].rearrange("p (s c) -> p s c", c=128))
                    S.tt(eng, Toep[:, q, 14, :], PS[:, b, 256:384], Dd[:, q, :], ALU.add)
            S.dma('sp', ToepD[q], Toep[:, q].rearrange("p a c -> p (a c)"))

    def s5_defs():
        P.Pst = tview(wsl[:, :], 0, [128, 64, 162], F32)

    def s5_V():
        eng = 'dve'
        Pst = P.Pst
        Winb = [aview(8192 * i, [128, 2, 8, 2, 128], BF16) for i in range(2)]
        BT = 102400
        s0tmp = aview(BT + 1024, [128, 64], F32)
        S.dma('sp', s0tmp, s0B_d)
        S.copy(eng, Pst[:, :, 0], s0tmp)
        S.memset(eng, Pst[:, :, 129], 0.0)
        kk = 0
        for q in range(4):
            Win = Winb[q % 2]
            S.dma('sp', Win[:, :, :, :, :].rearrange("p a b c d -> p (a b c d)"), WinD[q])
            for d in range(2):
                for pr in range(4):
                    for ri in range(2):
                        t = q * 16 + d * 8 + pr * 2 + ri
                        b = P.bank(0, 6)
                        rows = slice(32 * pr, 32 * pr + 32)
                        for j in range(8):
                            e = 7 - j if d == 0 else j
                            S.matmul(PS[:, b, 0:160], Win[rows, ri, e, d, :], uTp[rows, q, j, :],
                                     start=(j == 0), stop=(j == 7), tile_position=(32 * pr, 0))
                        pt = Pst[:, t, :]
                        if d == 0:
                            evac(kk, pt[:, 1:129], PS[:, b, 0:128]); kk += 1
                            evac(kk, pt[:, 130:162], PS[:, b, 128:160]); kk += 1
                        else:
                            evac(kk, _ap(pt, 128, [pt.ap[0], [-1, 128]]), PS[:, b, 0:128]); kk += 1
                            evac(kk, _ap(pt, 161, [pt.ap[0], [-1, 32]]), PS[:, b, 128:160]); kk += 1


    def s5_scan_gen():
        eng = 'dve'
        Pst = P.Pst
        BT = 102400
        sc1 = aview(BT, [128, 32, 2, 2], F32)
        sc2 = aview(BT + 512, [128, 32, 2, 2], F32)
        pa = Pst[:, :, :]
        pstep = pa.ap[0]
        for k in range(128):
            if k % 1 == 0 and k > 0:
                yield
            ncol = 2 if k < 32 else 1
            src = _ap(pa, k, [pstep, [324, 32], [162, 2], [129, ncol]])
            dst = _ap(pa, k + 1, [pstep, [324, 32], [162, 2], [129, ncol]])
            if ncol == 1:
                src2 = _ap(pa, k, [pstep, [0, 2], [324, 32], [162, 2]])
                lam2 = _ap(lam_rr, 0, [lam_rr.ap[0], [64, 2], [2, 32], [1, 2]])
                out2 = _ap(sc1, 0, [sc1.ap[0], [128, 2], [4, 32], [2, 2]])
                S.tt(eng, out2, src2, lam2, ALU.mult)
                a1 = _ap(sc1, 0, [sc1.ap[0], [4, 32], [2, 2], [1, 1]])
                a2s = _ap(sc2, 2, [sc2.ap[0], [4, 32], [-2, 2], [1, 1]])
                S.tt(eng, dst, dst, a1, ALU.add)
                S.tt(eng, dst, dst, a2s, ALU.add)
            else:
                lr_b = _ap(lam_rr, 0, [lam_rr.ap[0], [2, 32], [1, 2], [0, ncol]])
                li_b = _ap(lam_is, 0, [lam_is.ap[0], [2, 32], [1, 2], [0, ncol]])
                a1 = sc1[:, :, :, 0:ncol]
                a2 = sc2[:, :, :, 0:ncol]
                a2s = _ap(sc2, 2, [sc2.ap[0], [4, 32], [-2, 2], [1, ncol]])
                S.tt(eng, a1, src, lr_b, ALU.mult)
                S.tt(eng, a2, src, li_b, ALU.mult)
                S.tt(eng, dst, dst, a1, ALU.add)
                S.tt(eng, dst, dst, a2s, ALU.add)
            if (k + 1) % 32 == 0:
                idx = (k + 1) // 32 - 1
                S.copy(eng, Fin[:, idx, :], Pst[:, :, k + 1])
                if k + 1 < 128:
                    S.ts(eng, Pst[:, :, k + 1], Pst[:, :, k + 1], carry, ALU.mult)
                if k == 31:
                    S.copy(eng, Fin[:, 4, :], Pst[:, :, 161])

    def s5_rest():
        eng = 'dve'
        Pst = P.Pst
        ygT = aview(71680, [128, 4, NT], BF16)
        SinA = aview(92160, [128, 32, 160], BF16)
        SinB = aview(51200, [128, 32, 160], BF16)
        Wob = [aview(8192 * i, [128, 2, 4, 2, 8, 32], BF16) for i in range(2)]
        Tpb = [aview(16384 + 4096 * i, [128, 16, 128], BF16) for i in range(2)]
        BT = 102400
        S.dma('sp', so_d.rearrange("i p t -> p i t"), Fin)

        for q in (2, 3, 0, 1):
            Sq = (SinA if q < 2 else SinB)[:, (q % 2) * 16:(q % 2) * 16 + 16, :]
            for d in range(2):
                pqd = Pst[:, q * 16 + d * 8:q * 16 + d * 8 + 8, :]
                sd_ = Sq[:, d * 8:(d + 1) * 8, :]
                if d == 0:
                    S.copy('act', sd_[:, :, 0:128], pqd[:, :, 0:128])
                    S.copy(eng, sd_[:, :, 128:160], pqd[:, :, 129:161])
                else:
                    S.copy('act', sd_[:, :, 0:128], _ap(pqd, 127, [pqd.ap[0], [162, 8], [-1, 128]]))
                    S.copy(eng, sd_[:, :, 128:160], _ap(pqd, 129 + 31, [pqd.ap[0], [162, 8], [-1, 32]]))
        gluW = P.wload(glu_w_d.rearrange("(k p) m -> p k m", p=128))
        for q in range(4):
            Wout = Wob[q % 2]
            Toep = Tpb[q % 2]
            S.dma('sp', Wout[:, :, :, :, :, :].rearrange("p a b c d e -> p (a b c d e)"), WoutD[q])
            S.dma('sp', Toep[:, :, :].rearrange("p a c -> p (a c)"), ToepD[q])
            Sin = (SinA if q < 2 else SinB)[:, (q % 2) * 16:(q % 2) * 16 + 16, :]
            for i in range(8):
                b = P.bank(0, 6)
                o = PS[:, b, 0:160]
                for j in range(8):
                    slot = (i - j - 1) if j < i else ((7 + j - i - 1) if j > i else 14)
                    S.matmul(o, Toep[:, slot, :], uTp[:, q, j, :], start=(j == 0), stop=False)
                i_ = 0
                for d in range(2):
                    e = i + 1 if d == 0 else 8 - i
                    for ri in range(2):
                        for pr in range(4):
                            i_ += 1
                            S.matmul(PS[32 * pr:32 * pr + 32, b, 0:160], Wout[:, d, pr, ri, e - 1, :],
                                     Sin[:, d * 8 + pr * 2 + ri, :], start=False, stop=(i_ > 12),
                                     tile_position=(0, 32 * pr))
                yq = ygT[:, q, :]
                S.act(_ap(yq, i, [yq.ap[0], [8, 160]]), o, AF.Gelu)
        mark('s5_y')
        sg = [aview(BT + 1024, [128, 512], BF16), aview(BT + 2048, [128, 512], BF16)]
        k2 = 0
        for m in range(4):
            for (s_, n) in TB:
                b = P.bank(0, 6)
                for kc in range(4):
                    S.matmul(PS[:, b, 0:n], gluW[:, kc, m * 128:(m + 1) * 128], ygT[:, kc, s_:s_ + n],
                             start=(kc == 0), stop=(kc == 3))
                t = sg[k2 % 2][:, 0:n]
                k2 += 1
                S.act(t, PS[:, b, 0:n], AF.Sigmoid, bias=glubT[:, m:m + 1])
                S.tt('dve', catT[:, 4 + m, s_:s_ + n], ygT[:, m, s_:s_ + n], t, ALU.mult)

    def mixer_even():
        projections()
        mark('proj')
        P.scan_gen = None
        if cfg['s5']:
            s5_defs()
            s5_V()
            mark('s5_V')
            P.scan_gen = s5_scan_gen()
        if cfg['attn']:
            attention()
        else:
            S.memset('dve', catT[:, 0:4, :], 0.0)
        if P.scan_gen is not None:
            for _ in P.scan_gen:
                pass
            P.scan_gen = None
        mark('attn')
        if cfg['s5']:
            s5_rest()
        else:
            S.memset('dve', catT[:, 4:8, :], 0.0)
        mark('s5')
        out_proj_even()
        mark('wout')

    def input_transposes():
        xstage = [aview(0, [128, D], F32), aview(4096, [128, D], F32),
                  aview(98304, [128, D], F32), aview(102400, [128, D], F32)]
        for tt in range(10):
            st = xstage[tt % 4]
            S.dma('sp', st, xin[tt * 128:(tt + 1) * 128, :])
            for half in range(2):
                b = P.bank(0, 4)
                for c4 in range(4):
                    c = half * 4 + c4
                    S.transpose(PS[:, b, c4 * 128:(c4 + 1) * 128], st[:, c * 128:(c + 1) * 128], ident[:])
                src = PS[:, b, :].rearrange("p (c t) -> p c t", t=128)
                dst = xT[:, half * 4:half * 4 + 4, tt * 128:(tt + 1) * 128]
                S.copy('act', dst, src)


    P.marks = []

    def mark(name):
        P.marks.append((name, sum(1 for o in S.ops if o.eng == 'pe'), sum(1 for o in S.ops if o.eng == 'dve'),
                        sum(1 for o in S.ops if o.eng == 'act')))
    gprep = None
    if cfg['s5']:
        s5_small()
        gprep = s5_prep_batched()
        next(gprep)
    input_transposes()
    mark('xin')
    P.s5prepA = None
    P.s5prepB = None
    g0_ = modulation_gen(0)
    if cfg['s5']:
        for _ in range(12):
            next(g0_)
        P.mod1_gen = modulation_gen(1)
        for _ in range(12):
            next(P.mod1_gen)
        for _ in gprep:
            pass
    for _ in g0_:
        pass
    if getattr(P, 'mod1_gen', None) is not None:
        for _ in P.mod1_gen:
            pass
        P.mod1_gen = 'done'
    mark('mod0')
    for layer in range(2):
        if layer == 1 and cfg['fnet']:
            fnet_mixer(1)
            mark('fnet')
        if layer == 0 and (cfg['attn'] or cfg['s5']):
            mixer_even()
        if layer == 0:
            g_ = getattr(P, 'mod1_gen', None)
            if g_ is None:
                g_ = modulation_gen(1)
            if g_ != 'done':
                for _ in g_:
                    pass
            mark('mod1')
        if cfg['ffn']:
            norm_mod(layer, 1)
            ffn(layer)
            mark('ffn%d' % layer)

    sumsq_rstd()
    for c in range(8):
        S.stt(xT[:, c, :], xT[:, c, :], gvec[:, 4, c:c + 1], rstd[:, :], ALU.mult, ALU.mult)
    ystage = [aview(0, [128, D], F32), aview(4096, [128, D], F32)]
    for tt in range(10):
        st = ystage[tt % 2]
        for half in range(2):
            b = P.bank(0, 4)
            for c4 in range(4):
                c = half * 4 + c4
                S.transpose(PS[:, b, c4 * 128:(c4 + 1) * 128], xT[:, c, tt * 128:(tt + 1) * 128], ident[:])
            S.copy('act' if (tt + half) % 2 == 0 else 'dve', st[:, half * 512:(half + 1) * 512], PS[:, b, :])
        S.dma('sp', y_d[tt * 128:(tt + 1) * 128, :], st)

    S.emit(es)
    P.es.close()
    return P


def _core_tokens(c, x_prompt, x_sample):
    if c < 2:
        return np.concatenate([x_sample[c], x_prompt[c]], 0)
    s = 2 + 5 * (c - 2)
    return x_prompt[s:s + 5].reshape(NT, D)


def _fm(v, nch):
    return np.ascontiguousarray(v.reshape(nch, 128).T)


def make_in_maps(inp, cores):
    f32 = np.float32
    shared = {}
    shared['bmodT'] = np.ascontiguousarray(np.stack([_fm(inp['b_mod'][l], 48) for l in range(2)], 1)).astype(f32)
    shared['gvec'] = np.ascontiguousarray(np.stack([
        _fm(inp['norm_mix_g'][0], 8), _fm(inp['norm_ffn_g'][0], 8),
        _fm(inp['norm_mix_g'][1], 8), _fm(inp['norm_ffn_g'][1], 8),
        _fm(inp['final_norm_g'], 8)], 1)).astype(f32)
    shared['ident'] = np.eye(128, dtype=f32)
    for k in ('w_mod', 'ffn_w_gate', 'ffn_w_up', 'ffn_w_down'):
        shared[k] = np.ascontiguousarray(inp[k])
    shared['w_in_odd'] = np.ascontiguousarray(inp['w_in_odd'][0])
    shared['w_out_odd'] = np.ascontiguousarray(inp['w_out_odd'][0])
    bf = ml_dtypes.bfloat16
    ang = 2 * np.pi * np.outer(np.arange(256), np.arange(256)) / 256.0
    shared['cs256'] = np.concatenate([np.cos(ang) / 16.0, -np.sin(ang) / 16.0], 1).astype(bf)
    shared['dft4'] = np.stack([np.cos(ang) / 16.0, np.sin(ang) / 16.0], 0).astype(bf)
    angL = 2 * np.pi * (np.outer(np.arange(1024), np.arange(1024)) % 1024) / 1024.0
    dft_sample = np.stack([np.cos(angL) / 32.0, np.sin(angL) / 32.0], 0).astype(bf)
    dft_prompt = np.zeros((2, 1024, 1024), np.float32)
    for i in range(4):
        dft_prompt[0, i * 256:(i + 1) * 256, i * 256:(i + 1) * 256] = np.cos(ang) / 16.0
        dft_prompt[1, i * 256:(i + 1) * 256, i * 256:(i + 1) * 256] = np.sin(ang) / 16.0
    dft_prompt = dft_prompt.astype(bf)
    KBh = [[0, 1, 2, 3], [0, 1, 2, 3], [0, 1, 2, 3, 4], [1, 2, 3, 4, 5], [2, 3, 4, 5, 6], [3, 4, 5, 6, 7],
           [4, 5, 6, 7], [4, 5, 6, 7]]
    kr_ = np.arange(2)[:, None].repeat(64, 1).reshape(128)
    kc_ = np.arange(64)[None, :].repeat(2, 0).reshape(128)
    mask_s = np.zeros((128, 40, 128), np.float32)
    mask_p = np.zeros((128, 40, 128), np.float32)
    UNh = [[0, 1, 2, 3], [0, 1, 2, 3, 4, 5], [2, 3, 4, 5, 6, 7], [4, 5, 6, 7]]
    mi = 0
    for mp in range(4):
        for n in UNh[mp]:
            for mm in range(2):
                m = 2 * mp + mm
                qrow = 2 * m + kr_[None, :]; qcol = kc_[None, :]
                krow = 2 * n + kr_[:, None]; kcol = kc_[:, None]
                rs = np.clip(qrow - 4, 0, 8); cs = np.clip(qcol - 8, 0, 48)
                ok = (krow >= rs) & (krow < rs + 8) & (kcol >= cs) & (kcol < cs + 16)
                mask_s[:, mi, :] = np.where(ok, 0.0, NEG)
                mask_p[:, mi, :] = 0.0 if (n // 2 == m // 2) else NEG
                mi += 1
    mask_s = mask_s.astype(bf); mask_p = mask_p.astype(bf)
    R_ = inp['na_rpb'][0]
    rpbH = np.zeros((8, 19, 128), f32)
    rpbH[:, 2:17, 48:79] = R_
    shared['w_in_even'] = np.ascontiguousarray(inp['w_in_even'][0])
    shared['w_out_even'] = np.ascontiguousarray(inp['w_out_even'][0])
    lam = np.stack([inp['s5_lam_re'][0], inp['s5_lam_im'][0]], 0)
    bb = np.stack([inp['s5_b_re'][0], inp['s5_b_im'][0]], 0)
    cc = np.stack([inp['s5_c_re'][0], inp['s5_c_im'][0]], 0)
    ls = inp['s5_log_step'][0]
    lam_q = lam.reshape(2, 2, 4, 8, 64)
    sA_lam = np.broadcast_to(lam_q.transpose(3, 0, 2, 1, 4)[:, None], (8, 16, 2, 4, 2, 64)).reshape(128, 2, 4, 2, 64)
    shared['sA_lam'] = np.ascontiguousarray(sA_lam).astype(f32)
    ls_q = ls.reshape(2, 4, 8)
    shared['sA_ls'] = np.ascontiguousarray(
        np.broadcast_to(ls_q.transpose(2, 1, 0)[:, None], (8, 16, 4, 2)).reshape(128, 4, 2)).astype(f32)
    bq = bb.reshape(2, 2, 4, 8, 64, 16)
    shared['sA_b'] = np.ascontiguousarray(bq.transpose(3, 5, 0, 2, 1, 4).reshape(128, 2, 4, 2, 64)).astype(f32)
    par = np.zeros((128, 2), f32)
    gpar = (np.arange(128) // 16) % 2
    par[gpar == 0, 0] = 1.0
    par[gpar == 1, 1] = 1.0
    shared['parA'] = par
    lam_s = lam.reshape(2, 2, 4, 4, 2, 64)
    shared['sB_lam'] = np.ascontiguousarray(lam_s.transpose(4, 5, 0, 2, 1, 3).reshape(128, 2, 32)).astype(f32)
    ls_s = ls.reshape(2, 4, 4, 2)
    shared['sB_ls'] = np.ascontiguousarray(
        np.broadcast_to(ls_s.transpose(3, 1, 0, 2)[:, None], (2, 64, 4, 2, 4)).reshape(128, 32)).astype(f32)
    c_s = cc.reshape(2, 2, 4, 4, 2, 16, 64)
    shared['sB_c'] = np.ascontiguousarray(c_s.transpose(4, 6, 0, 2, 1, 3, 5).reshape(128, 2, 32, 16)).astype(f32)
    b_s = bb.reshape(2, 2, 4, 4, 2, 64, 16)
    shared['sB_b'] = np.ascontiguousarray(b_s.transpose(4, 5, 0, 2, 1, 3, 6).reshape(128, 2, 32, 16)).astype(f32)
    shared['sdT'] = _fm(inp['s5_d'][0], 4).astype(f32)
    shared['glubT'] = _fm(inp['s5_glu_b'][0], 4).astype(f32)
    shared['s5_glu_w'] = np.ascontiguousarray(inp['s5_glu_w'][0])
    maps = []
    for c in cores:
        m = dict(shared)
        m['xin'] = np.ascontiguousarray(_core_tokens(c, inp['x_prompt'], inp['x_sample']))
        cond_long = inp['c'][c] if c < 2 else inp['c_ctx']
        cond = np.stack([cond_long, inp['c_ctx']], 0)
        m['condT'] = np.ascontiguousarray(cond.reshape(2, 8, 128).transpose(2, 1, 0)).astype(f32)
        m['dftL'] = dft_sample if c < 2 else dft_prompt
        if c < 2:
            m['ctxkT'] = np.ascontiguousarray(inp['cache_na_k'][c, 0].reshape(512, 512).T)
            m['ctxv'] = np.ascontiguousarray(inp['cache_na_v'][c, 0].reshape(512, 512))
            m['ctxbias'] = np.zeros((128, 1), f32)
            m['maskt'] = mask_s
            m['rpbH'] = rpbH
            st = inp['state_s5'][c, 0].reshape(2, 2, 4, 4, 2, 64)
            m['s0B'] = np.ascontiguousarray(st.transpose(4, 5, 2, 0, 3, 1).reshape(128, 64)).astype(f32)
            m['carry'] = np.ones((128, 1), f32)
        else:
            m['ctxkT'] = np.zeros((512, 512), f32)
            m['ctxv'] = np.zeros((512, 512), f32)
            m['ctxbias'] = np.full((128, 1), NEG, f32)
            m['maskt'] = mask_p
            m['rpbH'] = np.zeros((8, 19, 128), f32)
            m['s0B'] = np.zeros((128, 64), f32)
            m['carry'] = np.zeros((128, 1), f32)
        maps.append(m)
    return maps


_PROG = {}


def run_cores(inp, cores, cfg=None):
    cfg = dict(CFG) if cfg is None else cfg
    key = tuple(sorted(cfg.items()))
    if key not in _PROG:
        _PROG[key] = build_program(cfg)
    P = _PROG[key]
    maps = make_in_maps(inp, cores)
    maps = [{k: v for k, v in m.items() if k in P.ins} for m in maps]
    res = run_bass_kernel_spmd(P.nc, maps, core_ids=list(range(len(cores))))
    return res.results


def kernel(**inputs):
    inp = {k: np.asarray(v) for k, v in inputs.items()}
    res = run_cores(inp, list(range(8)))
    y_prompt = np.zeros((32, 256, D), np.float32)
    y_sample = np.zeros((2, 1024, D), np.float32)
    for c in range(8):
        y = res[c]['y']
        if c < 2:
            y_sample[c] = y[0:1024]
            y_prompt[c] = y[1024:1280]
        else:
            s = 2 + 5 * (c - 2)
            y_prompt[s:s + 5] = y.reshape(5, 256, D)
    nk = np.zeros((32, 1, 256, 8, 64), np.float32)
    nv = np.zeros((32, 1, 256, 8, 64), np.float32)
    for c in range(8):
        for nm, dst in (('ko', nk), ('vo', nv)):
            a = res[c][nm]
            if c < 2:
                dst[c, 0] = a[1024:1280].reshape(256, 8, 64)
            else:
                s0 = 2 + 5 * (c - 2)
                dst[s0:s0 + 5, 0] = a.reshape(5, 256, 8, 64)
    ns = np.zeros((32, 1, 2, 2, 32, 64), np.float32)
    for c in range(8):
        so = res[c]['so']
        so = so.reshape(5, 2, 64, 4, 2, 4, 2)
        st = so.transpose(0, 4, 6, 3, 5, 1, 2).reshape(5, 2, 2, 32, 64)
        if c < 2:
            ns[c, 0] = st[4]
        else:
            s0 = 2 + 5 * (c - 2)
            for j in range(4):
                ns[s0 + j, 0, 0] = st[j, 0]
                ns[s0 + j, 0, 1] = st[3 - j, 1]
            ns[s0 + 4, 0] = st[4]
    return (y_prompt, y_sample, nk, nv, ns)
```

```python
import numpy as np
import ml_dtypes
from contextlib import ExitStack
import concourse.bass as bass
import concourse.mybir as mybir
from concourse.bass_utils import run_bass_kernel_spmd

F32 = mybir.dt.float32
BF16 = mybir.dt.bfloat16
I32 = mybir.dt.int32
AF = mybir.ActivationFunctionType
ALU = mybir.AluOpType

_DTSIZE = {F32: 4, BF16: 2, I32: 4}

NT = 1280
TB = [(0, 512), (512, 512), (1024, 256)]
CR = [(0, 1024, 0), (1024, 256, 1)]
D = 1024
DFF = 2816
NEG = -30000.0

CFG = dict(attn=True, s5=True, fnet=True, ffn=True, dbg=False, att=9)


def _region(ap):
    t = ap.tensor
    name = t.name
    space = str(ap.space).upper()
    pat = ap.ap
    esz = _DTSIZE[ap.dtype]
    off = int(ap.offset)
    if 'DRAM' in space or 'HBM' in space:
        lo = hi = off
        for st, cn in pat:
            if st >= 0:
                hi += st * (cn - 1)
            else:
                lo += st * (cn - 1)
        return ('D:' + name, 0, 1, lo * esz, (hi + 1) * esz)
    pstep, pcnt = pat[0]
    p0 = off // pstep if pstep else 0
    foff = off - p0 * pstep if pstep else off
    lo = hi = foff
    for st, cn in pat[1:]:
        if st >= 0:
            hi += st * (cn - 1)
        else:
            lo += st * (cn - 1)
    if 'PSUM' in space:
        b0 = (lo * esz) // 2048
        b1 = ((hi + 1) * esz - 1) // 2048
        return ('P:' + name, 0, 128, b0 * 2048, (b1 + 1) * 2048)
    return (name, p0, p0 + pcnt, lo * esz, (hi + 1) * esz)


class Op:
    __slots__ = ('eng', 'fn', 'reads', 'writes', 'dma', 'deps', 'signal', 'sem', 'val', 'idx')


class Sched:
    ENGS = ('pe', 'act', 'dve', 'pool', 'sp')
    NDSEM = 12

    def __init__(self, nc):
        self.nc = nc
        self.ops = []
        self.track_dram = set()

    def add(self, eng, fn, reads=(), writes=(), dma=False):
        op = Op()
        op.eng = eng
        op.fn = fn
        op.reads = [_region(a) for a in reads]
        op.writes = [_region(a) for a in writes]
        op.dma = dma
        op.idx = len(self.ops)
        self.ops.append(op)
        return op

    def matmul(self, out, lhsT, rhs, start=True, stop=True, **kw):
        rd = [lhsT, rhs] + ([] if start else [out])
        return self.add('pe', lambda e: e.matmul(out, lhsT=lhsT, rhs=rhs, start=start, stop=stop, **kw), rd, [out])

    def transpose(self, out, in_, ident):
        return self.add('pe', lambda e: e.transpose(out, in_, ident), [in_, ident], [out])

    def act(self, out, in_, func, bias=None, scale=None, accum_out=None):
        rd = [in_]
        kw = {}
        if bias is not None:
            kw['bias'] = bias
            if not isinstance(bias, (int, float)):
                rd.append(bias)
        if scale is not None:
            kw['scale'] = scale
            if not isinstance(scale, (int, float)):
                rd.append(scale)
        wr = [out]
        if accum_out is not None:
            kw['accum_out'] = accum_out
            wr.append(accum_out)
        return self.add('act', lambda e: e.activation(out=out, in_=in_, func=func, **kw), rd, wr)

    def tt(self, eng, out, in0, in1, op):
        return self.add(eng, lambda e: e.tensor_tensor(out=out, in0=in0, in1=in1, op=op), [in0, in1], [out])

    def ts(self, eng, out, in0, s1, op0, s2=None, op1=None):
        rd = [in0]
        if not isinstance(s1, (int, float)):
            rd.append(s1)
        if s2 is not None and not isinstance(s2, (int, float)):
            rd.append(s2)
        if op1 is None:
            return self.add(eng, lambda e: e.tensor_scalar(out=out, in0=in0, scalar1=s1, scalar2=None, op0=op0),
                            rd, [out])
        return self.add(eng, lambda e: e.tensor_scalar(out=out, in0=in0, scalar1=s1, scalar2=s2, op0=op0, op1=op1),
                        rd, [out])

    def stt(self, out, in0, scalar, in1, op0, op1):
        rd = [in0, in1]
        if not isinstance(scalar, (int, float)):
            rd.append(scalar)
        return self.add('dve', lambda e: e.scalar_tensor_tensor(out=out, in0=in0, scalar=scalar, in1=in1,
                                                                  op0=op0, op1=op1), rd, [out])

    def copy(self, eng, out, in_):
        if eng == 'act':
            return self.add('act', lambda e: e.copy(out=out, in_=in_), [in_], [out])
        return self.add(eng, lambda e: e.tensor_copy(out=out, in_=in_), [in_], [out])

    def recip(self, out, in_):
        return self.add('dve', lambda e: e.reciprocal(out=out, in_=in_), [in_], [out])

    def memset(self, eng, out, val):
        return self.add(eng, lambda e: e.memset(out, val), [], [out])

    def dma(self, eng, out, in_, **kw):
        return self.add(eng, lambda e: e.dma_start(out=out, in_=in_, **kw), [in_], [out], dma=True)

    def _analyze(self):
        live = {}
        ndma = {e: 0 for e in self.ENGS}
        dma_ops = {e: [] for e in self.ENGS}
        ops = self.ops
        for op in ops:
            deps = {}
            for regs, is_write in ((op.reads, False), (op.writes, True)):
                for (nm, p0, p1, b0, b1) in regs:
                    if nm[0:2] == 'D:' and nm not in self.track_dram:
                        continue
                    lst = live.get(nm)
                    if not lst:
                        continue
                    psum = nm[0:2] == 'P:'
                    for (q0, q1, c0, c1, j, w) in lst:
                        if q0 < p1 and p0 < q1 and c0 < b1 and b0 < c1 and j != op.idx:
                            if is_write:
                                kind = 'WAW' if w else 'WAR'
                            else:
                                if not w:
                                    if psum and ops[j].eng != op.eng:
                                        kind = 'RAR'
                                    else:
                                        continue
                                else:
                                    kind = 'RAW'
                            if j not in deps or kind == 'RAW':
                                deps[j] = kind
            for (nm, p0, p1, b0, b1) in op.writes:
                if nm[0:2] == 'D:' and nm not in self.track_dram:
                    continue
                lst = live.setdefault(nm, [])
                lst[:] = [t for t in lst if not (p0 <= t[0] and t[1] <= p1 and b0 <= t[2] and t[3] <= b1)]
                lst.append((p0, p1, b0, b1, op.idx, True))
            for (nm, p0, p1, b0, b1) in op.reads:
                if nm[0:2] == 'D:' and nm not in self.track_dram:
                    continue
                live.setdefault(nm, []).append((p0, p1, b0, b1, op.idx, False))
            if op.dma:
                n = ndma[op.eng]
                if n >= self.NDSEM:
                    deps.setdefault(dma_ops[op.eng][n - self.NDSEM].idx, 'SEM')
                dma_ops[op.eng].append(op)
                ndma[op.eng] = n + 1
            need = []
            best = {}
            for j, kind in deps.items():
                p = ops[j]
                if p.dma:
                    need.append(j)
                    continue
                if p.eng == op.eng and not op.dma:
                    if op.eng == 'pe':
                        continue
                if p.eng not in best or best[p.eng] < j:
                    best[p.eng] = j
            need.extend(best.values())
            op.deps = need
            op.signal = False
        for op in ops:
            for j in op.deps:
                ops[j].signal = True
        self.dma_ops = dma_ops

    def emit(self, es):
        nc = self.nc
        self._analyze()
        csem = {e: es.enter_context(nc.semaphore('c_' + e)) for e in ('pe', 'act', 'dve', 'pool')}
        dsem = {e: [es.enter_context(nc.semaphore('d_%s%d' % (e, i))) for i in range(self.NDSEM)]
                for e in ('sp', 'pool', 'act')}
        cnt = {e: 0 for e in self.ENGS}
        nd = {e: 0 for e in self.ENGS}
        for op in self.ops:
            if op.dma:
                n = nd[op.eng]
                nd[op.eng] = n + 1
                op.sem = dsem[op.eng][n % self.NDSEM]
                op.val = 16 * (n // self.NDSEM + 1)
                op.signal = True
            elif op.signal:
                cnt[op.eng] += 1
                op.sem = csem[op.eng]
                op.val = cnt[op.eng]
        self.stats = {e: (len([o for o in self.ops if o.eng == e]), cnt[e], nd[e]) for e in self.ENGS}
        block = es.enter_context(nc.Block())
        per = {e: [op for op in self.ops if op.eng == e] for e in self.ENGS}
        ops = self.ops
        dma_ops = self.dma_ops

        def run(engname, e):
            waited = {}
            for op in per[engname]:
                for j in op.deps:
                    p = ops[j]
                    key = id(p.sem)
                    if waited.get(key, 0) < p.val:
                        e.wait_ge(p.sem, p.val)
                        waited[key] = p.val
                inst = op.fn(e)
                if op.signal:
                    inst.then_inc(op.sem, 16 if op.dma else 1)
            for op in dma_ops[engname]:
                key = id(op.sem)
                if waited.get(key, 0) < op.val:
                    e.wait_ge(op.sem, op.val)
                    waited[key] = op.val

        @block.tensor
        def _(e):
            run('pe', e)

        @block.scalar
        def _(e):
            run('act', e)

        @block.vector
        def _(e):
            run('dve', e)

        @block.gpsimd
        def _(e):
            run('pool', e)

        @block.sync
        def _(e):
            run('sp', e)


def _ap(base, off, pat):
    return bass.AP(base.tensor, int(base.offset) + off, [list(x) for x in pat])


class Prog:
    def __init__(self, cfg):
        self.cfg = cfg
        self.nc = bass.Bass("TRN2", target_bir_lowering=False)
        self.es = ExitStack()
        self.S = Sched(self.nc)
        self.ins = {}
        self.outs = {}
        self.wslot_i = 0
        self.bank_i = 0

    def din(self, name, shape, dt=F32):
        a = self.nc.dram_tensor(name, list(shape), dt, kind="ExternalInput").ap()
        self.ins[name] = a
        return a

    def dout(self, name, shape, dt=F32):
        a = self.nc.dram_tensor(name, list(shape), dt, kind="ExternalOutput").ap()
        self.outs[name] = a
        return a

    def sb(self, name, shape, dt):
        return self.es.enter_context(self.nc.sbuf_tensor(name, list(shape), dt))

    def wload(self, src3):
        kc, mw = src3.shape[1], src3.shape[2]
        assert kc * mw <= self.WELEMS, (kc, mw)
        slot = self.wslots[self.wslot_i % getattr(self, 'NW_eff', self.NW)]
        self.wslot_i += 1
        v = slot[:, 0:kc * mw].rearrange("p (k m) -> p k m", m=mw)
        self.S.dma('pool', v, src3)
        return v

    def bank(self, lo=0, hi=4):
        b = lo + (self.bank_i % (hi - lo))
        self.bank_i += 1
        return b


def build_program(cfg):
    P = Prog(cfg)
    nc, S, es = P.nc, P.S, P.es
    din, dout, sb = P.din, P.dout, P.sb

    xin = din("xin", [NT, D])
    condT_d = din("condT", [128, 8, 2])
    bmodT_d = din("bmodT", [128, 2, 48])
    gvec_d = din("gvec", [128, 5, 8])
    ident_d = din("ident", [128, 128])
    w_mod = din("w_mod", [2, D, 6 * D])
    wg_d = din("ffn_w_gate", [2, D, DFF])
    wu_d = din("ffn_w_up", [2, D, DFF])
    wd_d = din("ffn_w_down", [2, DFF, D])
    y_d = dout("y", [NT, D])
    w_in_odd = din("w_in_odd", [D, D])
    w_in_even = din("w_in_even", [D, 2048])
    w_out_even = din("w_out_even", [D, D])
    ctxkT_d = din("ctxkT", [512, 512])
    ctxv_d = din("ctxv", [512, 512])
    ctxbias_d = din("ctxbias", [128, 1])
    maskt_d = din("maskt", [128, 40, 128], BF16)
    rpbH_d = din("rpbH", [8, 19, 128])
    sA_lam_d = din("sA_lam", [128, 2, 4, 2, 64])
    sA_ls_d = din("sA_ls", [128, 4, 2])
    sA_b_d = din("sA_b", [128, 2, 4, 2, 64])
    parA_d = din("parA", [128, 2])
    sB_lam_d = din("sB_lam", [128, 2, 32])
    sB_ls_d = din("sB_ls", [128, 32])
    sB_c_d = din("sB_c", [128, 2, 32, 16])
    sB_b_d = din("sB_b", [128, 2, 32, 16])
    s0B_d = din("s0B", [128, 64])
    carry_d = din("carry", [128, 1])
    sdT_d = din("sdT", [128, 4])
    glubT_d = din("glubT", [128, 4])
    glu_w_d = din("s5_glu_w", [512, 512])
    so_d = dout("so", [5, 128, 64])
    ko_d = dout("ko", [NT, 512])
    vo_d = dout("vo", [NT, 512])
    w_out_odd = din("w_out_odd", [D, D])
    cs256_d = din("cs256", [256, 512], BF16)
    dftL_d = din("dftL", [2, 1024, 1024], BF16)
    dft4_d = din("dft4", [2, 256, 256], BF16)

    xT = sb("xT", [128, 8, NT], F32)
    rstd = sb("rstd", [128, NT], F32)
    ntmp = sb("ntmp", [128, 2, 512], F32)
    ident_bf = sb("ident_bf", [128, 128], BF16)
    ctxbias = sb("ctxbias_s", [128, 1], F32)
    sbm = sb("sbm", [128, 1280], F32)
    ident = sb("ident_s", [128, 128], F32)
    ones_bf = sb("ones_bf", [128, 128], BF16)
    condT = sb("condT_s", [128, 8, 2], F32)
    scT = sb("scT", [128, 8, 2], BF16)
    bmodT = sb("bmodT_s", [128, 2, 48], F32)
    gvec = sb("gvec_s", [128, 5, 8], F32)
    modT = sb("modT", [128, 2, 48, 2], F32)
    gs = sb("gs", [128, 4, 8, 2], F32)
    P.NW = 5
    P.WELEMS = 4096
    wsl = sb("wsl", [128, P.NW * P.WELEMS + 768], BF16)
    P.wslots = [wsl[:, i * P.WELEMS:(i + 1) * P.WELEMS] for i in range(P.NW)]
    ARENA_BYTES = 111104
    arena = sb("arena", [128, ARENA_BYTES // 2], BF16)
    PS = es.enter_context(nc.psum_tensor("PS", [128, 8, 512], F32))
    PSM = PS[:, 7, 0:256].rearrange("p (l c) -> p l c", l=2)

    def aview(off_bytes, shape, dt):
        esz = _DTSIZE[dt]
        n = int(np.prod(shape[1:]))
        assert off_bytes % 4 == 0 and off_bytes + n * esz <= ARENA_BYTES, (off_bytes, shape)
        a = arena[:, off_bytes // 2: off_bytes // 2 + n * esz // 2]
        if dt != BF16:
            a = a.bitcast(dt)
        if len(shape) == 2:
            return a
        names = " ".join("d%d" % i for i in range(len(shape) - 1))
        kw = {"d%d" % i: shape[i + 1] for i in range(len(shape) - 1)}
        return a.rearrange("p (%s) -> p %s" % (names, names), **kw)

    S.dma('sp', ident[:], ident_d)
    S.dma('sp', condT[:], condT_d)
    S.dma('sp', bmodT[:], bmodT_d)
    S.dma('sp', gvec[:], gvec_d)
    S.memset('dve', ones_bf[:], 1.0)
    S.copy('dve', ident_bf[:], ident[:])
    S.dma('sp', ctxbias[:], ctxbias_d)
    S.act(scT[:], condT[:], AF.Silu)

    def modulation_gen(layer):
        for u in range(12):
            W = P.wload(w_mod[layer][:, u * 512:(u + 1) * 512].rearrange("(k p) m -> p k m", p=128))
            for mt in range(4):
                col = (u * 4 + mt) * 2
                for kc in range(8):
                    S.matmul(PSM[:, layer, col:col + 2], W[:, kc, mt * 128:(mt + 1) * 128], scT[:, kc, :],
                             start=(kc == 0), stop=(kc == 7))
            yield u
        src = PSM[:, layer, 0:96].rearrange("p (m c) -> p m c", c=2)
        b_ = bmodT[:, layer, :]
        bb = _ap(b_, 0, [b_.ap[0], [1, 48], [0, 2]])
        S.tt('dve', modT[:, layer, :, :], src, bb, ALU.add)
        for which in range(2):
            sc = modT[:, layer, (1 + 3 * which) * 8:(2 + 3 * which) * 8, :]
            g_ = gvec[:, layer * 2 + which, :]
            gb = _ap(g_, 0, [g_.ap[0], [1, 8], [0, 2]])
            S.stt(gs[:, layer * 2 + which, :, :], sc, 1.0, gb, ALU.add, ALU.mult)

    def modulation(layer):
        for _ in modulation_gen(layer):
            pass

    def mod_vec(layer, j, c, ci):
        return modT[:, layer, j * 8 + c, ci:ci + 1]

    hT = aview(0, [128, 8, NT], BF16)
    sq = aview(20480, [128, 8, NT], BF16)

    def sumsq_rstd_tb(s, n):
        S.act(sq[:, :, s:s + n], xT[:, :, s:s + n], AF.Square)
        b = P.bank(0, 4)
        for c in range(8):
            S.matmul(PS[:, b, 0:n], ones_bf[:], sq[:, c, s:s + n], start=(c == 0), stop=(c == 7))
        S.act(rstd[:, s:s + n], PS[:, b, 0:n], AF.Sqrt, bias=epsb[:, 0:1], scale=1.0 / D)
        S.recip(rstd[:, s:s + n], rstd[:, s:s + n])

    def sumsq_rstd():
        for (s, n) in TB:
            sumsq_rstd_tb(s, n)

    def norm_mod(layer, which):
        k = 0
        for ti, (s, n) in enumerate(TB):
            sumsq_rstd_tb(s, n)
        for ti, (s, n) in enumerate(TB):
            ci = 0 if ti < 2 else 1
            for c in range(8):
                t = ntmp[:, k % 2, 0:n]
                k += 1
                S.tt('dve', t, xT[:, c, s:s + n], rstd[:, s:s + n], ALU.mult)
                S.act(hT[:, c, s:s + n], t, AF.Identity,
                      bias=mod_vec(layer, 3 * which, c, ci), scale=gs[:, layer * 2 + which, c, ci:ci + 1])

    epsb = sb("epsb", [128, 1], F32)
    S.memset('dve', epsb[:], 1e-6)

    aT = aview(20480, [128, 22, NT], BF16)
    sgt = [aview(76800, [128, 512], BF16), aview(77824, [128, 512], BF16)]

    def ffn(layer):
        groups = [(i * 512, 512) for i in range(5)] + [(2560, 256)]
        k = 0
        for (m0, mw) in groups:
            Wg = P.wload(wg_d[layer][:, m0:m0 + mw].rearrange("(k p) m -> p k m", p=128))
            Wu = P.wload(wu_d[layer][:, m0:m0 + mw].rearrange("(k p) m -> p k m", p=128))
            for mt in range(mw // 128):
                j = m0 // 128 + mt
                for (s, n) in TB:
                    bg = P.bank(0, 6)
                    bu = P.bank(0, 6)
                    for kc in range(8):
                        S.matmul(PS[:, bg, 0:n], Wg[:, kc, mt * 128:(mt + 1) * 128], hT[:, kc, s:s + n],
                                 start=(kc == 0), stop=(kc == 7))
                    for kc in range(8):
                        S.matmul(PS[:, bu, 0:n], Wu[:, kc, mt * 128:(mt + 1) * 128], hT[:, kc, s:s + n],
                                 start=(kc == 0), stop=(kc == 7))
                    t = sgt[k % 2][:, 0:n]
                    k += 1
                    S.act(t, PS[:, bg, 0:n], AF.Silu)
                    S.tt('dve', aT[:, j, s:s + n], t, PS[:, bu, 0:n], ALU.mult)
        for mg in range(4):
            Wd = [P.wload(wd_d[layer][kh * 1408:(kh + 1) * 1408, mg * 256:(mg + 1) * 256]
                          .rearrange("(k p) m -> p k m", p=128)) for kh in range(2)]
            for mt in range(2):
                m = mg * 2 + mt
                for ti, (s, n) in enumerate(TB):
                    b = P.bank(0, 6)
                    for kk in range(22):
                        S.matmul(PS[:, b, 0:n], Wd[kk // 11][:, kk % 11, mt * 128:(mt + 1) * 128],
                                 aT[:, kk, s:s + n], start=(kk == 0), stop=(kk == 21))
                    ci = 0 if ti < 2 else 1
                    S.stt(xT[:, m, s:s + n], PS[:, b, 0:n], mod_vec(layer, 5, m, ci), xT[:, m, s:s + n],
                          ALU.mult, ALU.add)

    def evac(i, out, in_):
        S.copy('act' if i % 2 == 0 else 'dve', out, in_)

    def fnet_mixer(layer):
        zT = aview(20480, [128, 8, NT], BF16)
        ZCS = aview(40960, [128, 10, 4, 512], BF16)
        fT = aview(0, [128, 8, NT], BF16)
        cs256 = aview(81920, [128, 2, 512], BF16)
        dft4 = aview(83968, [128, 2, 2, 256], BF16)
        S.dma('sp', cs256, cs256_d.rearrange("(c p) m -> p c m", p=128))
        S.dma('sp', dft4, dft4_d.rearrange("a (t p) k -> p a t k", p=128))
        norm_mod(layer, 0)
        k = 0
        for u in range(2):
            W = P.wload(w_in_odd[:, u * 512:(u + 1) * 512].rearrange("(k p) m -> p k m", p=128))
            for mt in range(4):
                m = u * 4 + mt
                for (s_, n) in TB:
                    b = P.bank(0, 6)
                    for kc in range(8):
                        S.matmul(PS[:, b, 0:n], W[:, kc, mt * 128:(mt + 1) * 128], hT[:, kc, s_:s_ + n],
                                 start=(kc == 0), stop=(kc == 7))
                    evac(k, zT[:, m, s_:s_ + n], PS[:, b, 0:n]); k += 1
        for tt in range(10):
            for gq in range(4):
                b = P.bank(0, 6)
                for cc in range(2):
                    S.matmul(PS[:, b, :], zT[:, 2 * gq + cc, tt * 128:(tt + 1) * 128], cs256[:, cc, :],
                             start=(cc == 0), stop=(cc == 1))
                evac(k, ZCS[:, tt, gq, :], PS[:, b, :]); k += 1
        for kb in range(2):
            CL = P.wload(dftL_d[0][:, kb * 512:(kb + 1) * 512].rearrange("(t p) k -> p t k", p=128))
            SL = P.wload(dftL_d[1][:, kb * 512:(kb + 1) * 512].rearrange("(t p) k -> p t k", p=128))
            for m in range(8):
                gq, half = m // 2, m % 2
                b = P.bank(0, 6)
                for tt in range(8):
                    S.matmul(PS[:, b, :], ZCS[:, tt, gq, half * 128:(half + 1) * 128], CL[:, tt, :],
                             start=(tt == 0), stop=False)
                for tt in range(8):
                    S.matmul(PS[:, b, :], ZCS[:, tt, gq, 256 + half * 128:256 + (half + 1) * 128], SL[:, tt, :],
                             start=False, stop=(tt == 7))
                evac(k, fT[:, m, kb * 512:(kb + 1) * 512], PS[:, b, :]); k += 1
        for m in range(8):
            gq, half = m // 2, m % 2
            b = P.bank(0, 6)
            i = 0
            for a in range(2):
                for t in range(2):
                    S.matmul(PS[:, b, 0:256], ZCS[:, 8 + t, gq, a * 256 + half * 128:a * 256 + (half + 1) * 128],
                             dft4[:, a, t, :], start=(i == 0), stop=(i == 3))
                    i += 1
            evac(k, fT[:, m, 1024:1280], PS[:, b, 0:256]); k += 1
        for u in range(2):
            W = P.wload(w_out_odd[:, u * 512:(u + 1) * 512].rearrange("(k p) m -> p k m", p=128))
            for mt in range(4):
                m = u * 4 + mt
                for ti, (s_, n) in enumerate(TB):
                    b = P.bank(0, 6)
                    for kc in range(8):
                        S.matmul(PS[:, b, 0:n], W[:, kc, mt * 128:(mt + 1) * 128], fT[:, kc, s_:s_ + n],
                                 start=(kc == 0), stop=(kc == 7))
                    ci = 0 if ti < 2 else 1
                    S.stt(xT[:, m, s_:s_ + n], PS[:, b, 0:n], mod_vec(layer, 2, m, ci), xT[:, m, s_:s_ + n],
                          ALU.mult, ALU.add)

    KB = [[0, 1, 2, 3], [0, 1, 2, 3], [0, 1, 2, 3, 4], [1, 2, 3, 4, 5], [2, 3, 4, 5, 6], [3, 4, 5, 6, 7],
          [4, 5, 6, 7], [4, 5, 6, 7]]
    qT = aview(40960, [128, 4, NT], BF16)
    kT = aview(51200, [128, 4, NT], BF16)
    uTp = aview(61440, [128, 4, 8, 160], BF16)
    vtok = aview(71680, [128, 10, 512], BF16)
    catT = aview(81920, [128, 8, NT], BF16)
    kvst = [aview(102400, [128, 512], F32), aview(104448, [128, 512], F32)]
    rec = aview(106496, [128, 2, 256], F32)
    rstage = [aview(108544, [128, 9, 128], BF16)]

    def pp(n, nb=None):
        for attr, cnt in (('s5prepA', n), ('s5prepB', n if nb is None else nb)):
            for _ in range(cnt):
                g_ = getattr(P, attr, None)
                if g_ is None:
                    break
                try:
                    next(g_)
                except StopIteration:
                    setattr(P, attr, None)

    def projections():
        norm_mod(0, 0)

        def wl(i):
            return P.wload(w_in_even[:, i * 512:(i + 1) * 512].rearrange("(k p) m -> p k m", p=128))
        k = 0
        Wq = wl(0)
        Wk = wl(1)
        for mt in range(4):
            for (s_, n) in TB:
                b = P.bank(0, 6)
                for kc in range(8):
                    S.matmul(PS[:, b, 0:n], Wq[:, kc, mt * 128:(mt + 1) * 128], hT[:, kc, s_:s_ + n],
                             start=(kc == 0), stop=(kc == 7))
                S.act(qT[:, mt, s_:s_ + n], PS[:, b, 0:n], AF.Identity, scale=0.125)
            pp(1)
        Wv = wl(2)
        for mt in range(4):
            for (s_, n) in TB:
                b = P.bank(0, 6)
                for kc in range(8):
                    S.matmul(PS[:, b, 0:n], Wk[:, kc, mt * 128:(mt + 1) * 128], hT[:, kc, s_:s_ + n],
                             start=(kc == 0), stop=(kc == 7))
                S.copy('act', kT[:, mt, s_:s_ + n], PS[:, b, 0:n])
            pp(1)
        for tt in range(10):
            b = P.bank(0, 6)
            for kc in range(8):
                S.matmul(PS[:, b, :], hT[:, kc, tt * 128:(tt + 1) * 128], Wk[:, kc, :],
                         start=(kc == 0), stop=(kc == 7))
            st = kvst[tt % 2]
            S.copy('act', st, PS[:, b, :])
            S.dma('sp', ko_d[tt * 128:(tt + 1) * 128, :], st)
            pp(1)
        Wu = wl(3)
        for tt in range(10):
            b = P.bank(0, 6)
            for kc in range(8):
                S.matmul(PS[:, b, :], hT[:, kc, tt * 128:(tt + 1) * 128], Wv[:, kc, :],
                         start=(kc == 0), stop=(kc == 7))
            st = kvst[tt % 2]
            S.copy('act', st, PS[:, b, :])
            S.copy('act', vtok[:, tt, :], PS[:, b, :])
            S.dma('sp', vo_d[tt * 128:(tt + 1) * 128, :], st)
            pp(1)
        for mt in range(4):
            for (s_, n) in TB:
                b = P.bank(0, 6)
                for kc in range(8):
                    S.matmul(PS[:, b, 0:n], Wu[:, kc, mt * 128:(mt + 1) * 128], hT[:, kc, s_:s_ + n],
                             start=(kc == 0), stop=(kc == 7))
                pv = PS[:, b, 0:n]
                src = _ap(pv, 0, [pv.ap[0], [1, 8], [8, n // 8]])
                dst = uTp[:, mt, :, s_ // 8:(s_ + n) // 8]
                S.copy('act', dst, src)
            pp(1)

    def attention():
        ctxkT = aview(0, [128, 4, 512], BF16)
        ctxv = aview(4096, [128, 4, 512], BF16)
        E = aview(8192, [128, 2, 10, 256], BF16)
        maskt = aview(18432, [128, 40, 128], BF16)
        rpbT = aview(28672, [128, 2, 9, 128], BF16)
        stgs = [rstage[0], aview(103680, [128, 9, 128], BF16)]
        rec2 = aview(106496, [128, 2, 256], F32)
        S.dma('pool', ctxkT, ctxkT_d.rearrange("(c p) k -> p c k", p=128))
        S.dma('pool', ctxv, ctxv_d.rearrange("(j p) f -> p j f", p=128))
        S.dma('sp', maskt, maskt_d)
        UN = [[0, 1, 2, 3], [0, 1, 2, 3, 4, 5], [2, 3, 4, 5, 6, 7], [4, 5, 6, 7]]
        ti0 = [0, 8, 20, 32]

        def build_bias(h):
            stg = stgs[h % 2]
            for krl in range(2):
                src = bass.AP(rpbH_d.tensor, h * 19 * 128 + krl * 128, [[1, 64], [256, 9], [128, 2], [1, 64]])
                dst = stg[krl * 64:(krl + 1) * 64, :, :].rearrange("p a (r c) -> p a r c", c=64)
                S.dma('pool', dst, src)

        its = []
        for j_ in range(4):
            for (hh, mp_) in ((0, 0), (0, 1), (1, 0), (1, 1), (0, 2), (0, 3), (1, 2), (1, 3)):
                its.append((2 * j_ + hh, mp_))

        def build_bm(it):
            h, mp = its[it]
            if mp == 0:
                a_ = stgs[h % 2][:, :, :]
                rev = _ap(a_, 127, [a_.ap[0], [128, 9], [-1, 128]])
                S.copy('act', rpbT[:, h % 2, :, :], rev)
            if mp == 3 and h + 2 < 8:
                build_bias(h + 2)

        def sreg(slot):
            b = slot // 2
            return PS[:, b, (slot % 2) * 256:(slot % 2) * 256 + 256]

        def scores(it):
            h, mp = its[it]
            hp, hc = h % 2, h // 2
            pr = slice(64 * hp, 64 * hp + 64)
            Ju = len(UN[mp])
            qsl = qT[pr, hc, mp * 256:(mp + 1) * 256]
            for jn, n in enumerate(UN[mp]):
                S.matmul(sreg(jn), kT[pr, hc, n * 128:(n + 1) * 128], qsl, start=(jn % 2 == 0), stop=False,
                         skip_group_check=True)
            for j in range(4):
                S.matmul(sreg(6 + j), ctxkT[pr, hc, j * 128:(j + 1) * 128], qsl, start=(j % 2 == 0), stop=True,
                         skip_group_check=True)
            rp = rpbT[:, h % 2, :, :]
            for jn, n in enumerate(UN[mp]):
                t0_ = ti0[mp] + 2 * jn
                S.matmul(sreg(jn), ident_bf[:], maskt[:, t0_:t0_ + 2, :].rearrange("p a c -> p (a c)"),
                         start=False, stop=False, skip_group_check=True)
                d0 = n - 2 * mp + 4
                rhs = _ap(rp, d0 * 128, [rp.ap[0], [-128, 2], [1, 128]])
                S.matmul(sreg(jn), ident_bf[:], rhs, start=False, stop=True, skip_group_check=True)

        def exps(it):
            h, mp = its[it]
            Ju = len(UN[mp])
            Et = E[:, it % 2, :, :].rearrange("p a c -> p (a c)")
            Sl = PS[:, 0:3, :].rearrange("p b c -> p (b c)")
            Sc = PS[:, 3:5, :].rearrange("p b c -> p (b c)")
            S.act(Et[:, 0:Ju * 256], Sl[:, 0:Ju * 256], AF.Exp)
            S.act(Et[:, 1536:2560], Sc, AF.Exp, bias=ctxbias[:, 0:1])

        def pv(it):
            h, mp = its[it]
            hp, hc = h % 2, h // 2
            pr = slice(64 * hp, 64 * hp + 64)
            Ju = len(UN[mp])
            bnk = 5 + mp % 2
            num = PS[pr, bnk, 0:256]
            den = PS[pr, bnk, 256:512]
            tot = Ju + 4
            for j in range(tot):
                lv = vtok[:, UN[mp][j], h * 64:(h + 1) * 64] if j < Ju else ctxv[:, j - Ju, h * 64:(h + 1) * 64]
                ev = E[:, it % 2, j if j < Ju else 6 + j - Ju, :]
                S.matmul(num, lv, ev, start=(j == 0), stop=(j == tot - 1))
            for j in range(tot):
                ev = E[:, it % 2, j if j < Ju else 6 + j - Ju, :]
                S.matmul(den, ones_bf[:, 0:64], ev, start=(j == 0), stop=(j == tot - 1))
            if hp == 1:
                rc = rec2[:, mp % 2, :]
                S.recip(rc, PS[:, bnk, 256:512])
                S.tt('dve', catT[:, hc, mp * 256:(mp + 1) * 256], PS[:, bnk, 0:256], rc, ALU.mult)

        def pump():
            if getattr(P, 'mod1_gen', None) is not None:
                try:
                    next(P.mod1_gen)
                except StopIteration:
                    P.mod1_gen = None

        def pump_prep(n):
            for _ in range(n):
                if getattr(P, 's5prep', None) is None:
                    return
                try:
                    next(P.s5prep)
                except StopIteration:
                    P.s5prep = None

        NI = len(its)
        build_bias(0)
        build_bias(1)
        build_bm(0)
        build_bm(1)
        scores(0)
        exps(0)
        for it in range(1, NI + 1):
            if it + 1 < NI:
                build_bm(it + 1)
            if it < NI:
                scores(it)
                exps(it)
            pv(it - 1)
            for _ in range(3):
                if getattr(P, 'scan_gen', None) is not None:
                    try:
                        next(P.scan_gen)
                    except StopIteration:
                        P.scan_gen = None
        it = 0
        for h in range(8):
            hp, hc = h % 2, h // 2
            pr = slice(64 * hp, 64 * hp + 64)
            Sreg = PS[:, it % 2, :]
            Et = E[:, it % 2, 0:2, :]
            for kb in range(2):
                S.matmul(Sreg[:, kb * 256:(kb + 1) * 256], kT[pr, hc, 1024 + kb * 128:1024 + (kb + 1) * 128],
                         qT[pr, hc, 1024:1280], start=(kb == 0), stop=True, skip_group_check=True)
            S.act(Et.rearrange("p a c -> p (a c)"), Sreg, AF.Exp)
            bnk = 5 + hc % 2
            num = PS[pr, bnk, 0:256]
            den = PS[pr, bnk, 256:512]
            for kb in range(2):
                S.matmul(num, vtok[:, 8 + kb, h * 64:(h + 1) * 64], Et[:, kb, :], start=(kb == 0), stop=(kb == 1))
            for kb in range(2):
                S.matmul(den, ones_bf[:, 0:64], Et[:, kb, :], start=(kb == 0), stop=(kb == 1))
            if hp == 1:
                rc = rec2[:, hc % 2, :]
                S.recip(rc, PS[:, bnk, 256:512])
                S.tt('dve', catT[:, hc, 1024:1280], PS[:, bnk, 0:256], rc, ALU.mult)
            for _ in range(4):
                if getattr(P, 'scan_gen', None) is not None:
                    try:
                        next(P.scan_gen)
                    except StopIteration:
                        P.scan_gen = None
            it += 1

    def out_proj_even():
        for u in range(2):
            W = P.wload(w_out_even[:, u * 512:(u + 1) * 512].rearrange("(k p) m -> p k m", p=128))
            for mt in range(4):
                m = u * 4 + mt
                for ti, (s_, n) in enumerate(TB):
                    b = P.bank(0, 6)
                    for kc in range(8):
                        S.matmul(PS[:, b, 0:n], W[:, kc, mt * 128:(mt + 1) * 128], catT[:, kc, s_:s_ + n],
                                 start=(kc == 0), stop=(kc == 7))
                    ci = 0 if ti < 2 else 1
                    S.stt(xT[:, m, s_:s_ + n], PS[:, b, 0:n], mod_vec(0, 2, m, ci), xT[:, m, s_:s_ + n],
                          ALU.mult, ALU.add)

    TWO_PI = 6.283185

    def disc(eng, lam, ls_b, T, F):
        ar, ai, cr, ci, lr, t0, t1, t2, t3, mag = T[:10]
        ti = T[10].bitcast(I32)
        S.ts(eng, lr, lam[:, 0], -1e-4, ALU.min)
        S.tt(eng, t0, lr, ls_b, ALU.mult)
        S.act(mag, t0, AF.Exp)
        S.tt(eng, t0, lam[:, 1], ls_b, ALU.mult)
        S.ts(eng, t0, t0, 1.0 / (2 * np.pi), ALU.mult)
        S.copy(eng, ti, t0)
        S.copy(eng, t1, ti)
        S.tt(eng, t1, t0, t1, ALU.subtract)
        S.act(t2, t1, AF.Sin, scale=TWO_PI)
        S.tt(eng, ai, mag, t2, ALU.mult)
        S.ts(eng, t0, t0, 0.25, ALU.add)
        S.copy(eng, ti, t0)
        S.copy(eng, t1, ti)
        S.tt(eng, t1, t0, t1, ALU.subtract)
        S.act(t2, t1, AF.Sin, scale=TWO_PI)
        S.tt(eng, ar, mag, t2, ALU.mult)
        li = lam[:, 1]
        S.ts(eng, t0, ar, -1.0, ALU.add)
        S.tt(eng, t1, lr, lr, ALU.mult)
        S.tt(eng, t2, li, li, ALU.mult)
        S.tt(eng, t1, t1, t2, ALU.add)
        S.recip(t1, t1)
        S.tt(eng, t2, t0, lr, ALU.mult)
        S.tt(eng, t3, ai, li, ALU.mult)
        S.tt(eng, t2, t2, t3, ALU.add)
        S.tt(eng, cr, t2, t1, ALU.mult)
        S.tt(eng, t2, ai, lr, ALU.mult)
        S.tt(eng, t3, t0, li, ALU.mult)
        S.tt(eng, t2, t2, t3, ALU.subtract)
        S.tt(eng, ci, t2, t1, ALU.mult)
        return ar, ai, cr, ci

    def bc(ap2, shape, pat):
        return _ap(ap2, 0, [ap2.ap[0]] + pat)

    def tview(t2d, off_bytes, shape, dt):
        esz0 = _DTSIZE[t2d.dtype]
        esz = _DTSIZE[dt]
        n = int(np.prod(shape[1:]))
        a = t2d[:, off_bytes // esz0:(off_bytes + n * esz) // esz0]
        if dt != t2d.dtype:
            a = a.bitcast(dt)
        if len(shape) == 2:
            return a
        names = " ".join("d%d" % i for i in range(len(shape) - 1))
        kw = {"d%d" % i: shape[i + 1] for i in range(len(shape) - 1)}
        return a.rearrange("p (%s) -> p %s" % (names, names), **kw)

    lsA = sbm[:, 0:8]
    parA = sbm[:, 8:10]
    carry = sbm[:, 10:11]
    sdT = sbm[:, 12:16]
    glubT = sbm[:, 16:20]
    dtA = sbm[:, 20:28]
    lamB = sbm[:, 32:96].rearrange("p (a f) -> p a f", a=2)
    lsB = sbm[:, 96:128]
    dtB = sbm[:, 128:160]
    Fin = sbm[:, 160:480].rearrange("p (i t) -> p i t", i=5)
    PB = sbm[:, 480:1056].rearrange("p (a e f) -> p a e f", a=2, e=9)
    lam_rr = sbm[:, 1056:1120].rearrange("p (f r) -> p f r", r=2)
    lam_is = sbm[:, 1120:1184].rearrange("p (f r) -> p f r", r=2)
    crB = sbm[:, 1184:1216]
    ciB = sbm[:, 1216:1248]
    WinD = nc.dram_tensor("WinD", [4, 128, 4096], BF16, kind="Internal").ap()
    WoutD = nc.dram_tensor("WoutD", [4, 128, 4096], BF16, kind="Internal").ap()
    ToepD = nc.dram_tensor("ToepD", [4, 128, 2048], BF16, kind="Internal").ap()
    for nm_ in ("WinD", "WoutD", "ToepD"):
        S.track_dram.add('D:' + nm_)
    rs2 = rstd[:, :]
    nt2 = ntmp[:, :, :].rearrange("p a b -> p (a b)")

    def s5_small():
        eng = 'dve'
        S.dma('sp', lsA, sA_ls_d.rearrange("p q d -> p (q d)"))
        S.dma('sp', parA, parA_d)
        S.dma('sp', carry, carry_d)
        S.dma('sp', sdT, sdT_d)
        S.dma('sp', glubT, glubT_d)
        S.act(dtA, lsA, AF.Exp)
        S.dma('sp', lamB, sB_lam_d)
        S.dma('sp', lsB, sB_ls_d)
        S.act(dtB, lsB, AF.Exp)
        TB_ = [tview(rs2, 1536 + 128 * i, [128, 32], F32) for i in range(11)]
        arB, aiB, crB_, ciB_ = disc(eng, lamB, dtB, TB_, 32)
        S.copy(eng, crB, crB_)
        S.copy(eng, ciB, ciB_)
        S.memset(eng, PB[:, 0, 0, :], 1.0)
        S.memset(eng, PB[:, 1, 0, :], 0.0)
        S.copy(eng, PB[:, 0, 1, :], arB)
        S.copy(eng, PB[:, 1, 1, :], aiB)
        u1, u2 = TB_[4], TB_[5]
        for e in range(2, 9):
            S.tt(eng, u1, PB[:, 0, e - 1, :], arB, ALU.mult)
            S.tt(eng, u2, PB[:, 1, e - 1, :], aiB, ALU.mult)
            S.tt(eng, PB[:, 0, e, :], u1, u2, ALU.subtract)
            S.tt(eng, u1, PB[:, 0, e - 1, :], aiB, ALU.mult)
            S.tt(eng, u2, PB[:, 1, e - 1, :], arB, ALU.mult)
            S.tt(eng, PB[:, 1, e, :], u1, u2, ALU.add)
        S.copy(eng, lam_rr[:, :, 0], PB[:, 0, 8, :])
        S.copy(eng, lam_rr[:, :, 1], PB[:, 0, 8, :])
        S.copy(eng, lam_is[:, :, 0], PB[:, 1, 8, :])
        S.ts(eng, lam_is[:, :, 1], PB[:, 1, 8, :], -1.0, ALU.mult)

    def s5_prep_batched():
        eng = 'dve'
        Win = aview(8192, [128, 4, 2, 8, 2, 128], BF16)
        T_ = [aview(40960 + 2048 * i, [128, 4, 2, 64], F32) for i in range(11)]
        lamA = aview(63488, [128, 2, 4, 2, 64], F32)
        bA = aview(67584, [128, 2, 4, 2, 64], F32)
        S.dma('sp', lamA, sA_lam_d)
        S.dma('sp', bA, sA_b_d)
        dt_b = _ap(dtA, 0, [dtA.ap[0], [2, 4], [1, 2], [0, 64]])
        arA, aiA, crA, ciA = disc(eng, lamA, dt_b, T_, 512)
        bre = bA[:, 0]
        bim = bA[:, 1]
        bbr, bbi, ua, ub = T_[4], T_[5], T_[6], T_[7]
        S.tt(eng, ua, crA, bre, ALU.mult)
        S.tt(eng, ub, ciA, bim, ALU.mult)
        S.tt(eng, bbr, ua, ub, ALU.subtract)
        S.tt(eng, ua, crA, bim, ALU.mult)
        S.tt(eng, ub, ciA, bre, ALU.mult)
        S.tt(eng, bbi, ua, ub, ALU.add)
        Wr = [bbr, T_[8]]
        Wi = [bbi, T_[9]]
        for e in range(8):
            cr_, ci_ = Wr[e % 2], Wi[e % 2]
            if e > 0:
                pr_, pi_ = Wr[(e - 1) % 2], Wi[(e - 1) % 2]
                S.tt(eng, ua, pr_, arA, ALU.mult)
                S.tt(eng, ub, pi_, aiA, ALU.mult)
                S.tt(eng, cr_, ua, ub, ALU.subtract)
                S.tt(eng, ua, pr_, aiA, ALU.mult)
                S.tt(eng, ub, pi_, arA, ALU.mult)
                S.tt(eng, ci_, ua, ub, ALU.add)
            for gp in range(2):
                S.act(Win[:, :, 0, e, :, gp * 64:(gp + 1) * 64], cr_, AF.Copy, scale=parA[:, gp:gp + 1])
                S.act(Win[:, :, 1, e, :, gp * 64:(gp + 1) * 64], ci_, AF.Copy, scale=parA[:, gp:gp + 1])
        for q in range(4):
            S.dma('sp', WinD[q], Win[:, q].rearrange("p a b c d -> p (a b c d)"))
        Toep = aview(8192, [128, 4, 16, 128], BF16)
        Wo = aview(40960, [128, 32, 2, 8, 32], BF16)
        Wo0 = aview(73728, [128, 32, 2, 32], BF16)
        BbS = aview(77824, [128, 32, 2, 32], BF16)
        cB = aview(81920, [128, 2, 32, 16], F32)
        bB = aview(86016, [128, 2, 32, 16], F32)
        w1b = [aview(90112, [128, 32, 16], F32), aview(92160, [128, 32, 16], F32)]
        w2 = aview(94208, [128, 32, 16], F32)
        Dd = aview(96256, [128, 4, 128], F32)
        S.dma('sp', cB, sB_c_d)
        S.dma('sp', bB, sB_b_d)
        cr_b = _ap(crB, 0, [crB.ap[0], [1, 32], [0, 16]])
        ci_b = _ap(ciB, 0, [ciB.ap[0], [1, 32], [0, 16]])
        S.memset('pool', BbS[:, :, :, :].rearrange("p a b c -> p (a b c)"), 0.0)
        S.memset('pool', Wo[:, :, :, :, :].rearrange("p a b c d -> p (a b c d)"), 0.0)
        S.memset('pool', Wo0[:, :, :, :].rearrange("p a b c -> p (a b c)"), 0.0)
        for ri in range(2):
            wx = w1b[ri]
            if ri == 0:
                S.tt(eng, wx, cr_b, bB[:, 0], ALU.mult)
                S.tt(eng, w2, ci_b, bB[:, 1], ALU.mult)
                S.tt(eng, wx, wx, w2, ALU.subtract)
            else:
                S.tt(eng, wx, cr_b, bB[:, 1], ALU.mult)
                S.tt(eng, w2, ci_b, bB[:, 0], ALU.mult)
                S.tt(eng, wx, wx, w2, ALU.add)
            for gp in range(2):
                ps_ = slice(64 * gp, 64 * gp + 64)
                S.copy('act', BbS[ps_, :, ri, 16 * gp:16 * gp + 16], wx[ps_, :, :])
        for e in range(9):
            pr_b = _ap(PB, (0 * 9 + e) * 32, [PB.ap[0], [1, 32], [0, 16]])
            pi_b = _ap(PB, (1 * 9 + e) * 32, [PB.ap[0], [1, 32], [0, 16]])
            for ri in range(2):
                wx = w1b[ri]
                if ri == 0:
                    S.tt(eng, wx, cB[:, 0], pr_b, ALU.mult)
                    S.tt(eng, w2, cB[:, 1], pi_b, ALU.mult)
                    S.tt(eng, wx, wx, w2, ALU.subtract)
                    sgn = 1.0
                else:
                    S.tt(eng, wx, cB[:, 0], pi_b, ALU.mult)
                    S.tt(eng, w2, cB[:, 1], pr_b, ALU.mult)
                    S.tt(eng, wx, wx, w2, ALU.add)
                    sgn = -1.0
                for gp in range(2):
                    ps_ = slice(64 * gp, 64 * gp + 64)
                    if e == 0:
                        dst = Wo0[ps_, :, ri, 16 * gp:16 * gp + 16]
                    else:
                        dst = Wo[ps_, :, ri, e - 1, 16 * gp:16 * gp + 16]
                    S.act(dst, wx[ps_, :, :], AF.Copy, scale=sgn)
        for q in range(4):
            S.dma('sp', WoutD[q], Wo[:, q * 8:(q + 1) * 8].rearrange("p a b c d -> p (a b c d)"))

        def wo(f, ri, tau):
            return Wo0[:, f, ri, :] if tau == 0 else Wo[:, f, ri, tau - 1, :]

        kk = 0
        for q in range(4):
            S.ts(eng, Dd[:, q, :], ident[:], sdT[:, q:q + 1], ALU.mult)
            for bi in range(4):
                b = P.bank(0, 6)
                S.memset(eng, PS[:, b, :], 0.0)
                for s4 in range(4):
                    slot = bi * 4 + s4
                    if slot > 14:
                        continue
                    if slot < 7:
                        combos = [(0, slot + 1)]
                    elif slot < 14:
                        combos = [(1, slot - 6)]
                    else:
                        combos = [(0, 0), (1, 0)]
                    n_ = len(combos) * 2
                    i_ = 0
                    for (d, tau) in combos:
                        for ri in range(2):
                            for pr in range(4):
                                f = (q * 2 + d) * 4 + pr
                                o = PS[32 * pr:32 * pr + 32, b, s4 * 128 + 32 * pr:s4 * 128 + 32 * pr + 32]
                                S.matmul(o, BbS[:, f, ri, :], wo(f, ri, tau),
                                         start=(i_ == 0), stop=(i_ == n_ - 1), tile_position=(0, 32 * pr))
                            i_ += 1
                if bi < 3:
                    evac(kk, Toep[:, q, bi * 4:bi * 4 + 4, :], PS[:, b, :].rearrange("p (s c) -> p s c", c=128)); kk += 1
                else:
                    S.copy('act', Toep[:, q, 12:14, :], PS[:, b, 0:256].rearrange("p (s c) -> p s c", c=128))
                    S.tt(eng, Toep[:, q, 14, :], PS[:, b, 256:384], Dd[:, q, :], ALU.add)
            S.dma('sp', ToepD[q], Toep[:, q].rearrange("p a c -> p (a c)"))

    def s5_defs():
        P.Pst = tview(wsl[:, :], 0, [128, 64, 162], F32)

    def s5_V():
        eng = 'dve'
        Pst = P.Pst
        Winb = [aview(8192 * i, [128, 2, 8, 2, 128], BF16) for i in range(2)]
        BT = 102400
        s0tmp = aview(BT + 1024, [128, 64], F32)
        S.dma('sp', s0tmp, s0B_d)
        S.copy(eng, Pst[:, :, 0], s0tmp)
        S.memset(eng, Pst[:, :, 129], 0.0)
        kk = 0
        for q in range(4):
            Win = Winb[q % 2]
            S.dma('sp', Win[:, :, :, :, :].rearrange("p a b c d -> p (a b c d)"), WinD[q])
            for d in range(2):
                for pr in range(4):
                    for ri in range(2):
                        t = q * 16 + d * 8 + pr * 2 + ri
                        b = P.bank(0, 6)
                        rows = slice(32 * pr, 32 * pr + 32)
                        for j in range(8):
                            e = 7 - j if d == 0 else j
                            S.matmul(PS[:, b, 0:160], Win[rows, ri, e, d, :], uTp[rows, q, j, :],
                                     start=(j == 0), stop=(j == 7), tile_position=(32 * pr, 0))
                        pt = Pst[:, t, :]
                        if d == 0:
                            evac(kk, pt[:, 1:129], PS[:, b, 0:128]); kk += 1
                            evac(kk, pt[:, 130:162], PS[:, b, 128:160]); kk += 1
                        else:
                            evac(kk, _ap(pt, 128, [pt.ap[0], [-1, 128]]), PS[:, b, 0:128]); kk += 1
                            evac(kk, _ap(pt, 161, [pt.ap[0], [-1, 32]]), PS[:, b, 128:160]); kk += 1


    def s5_scan_gen():
        eng = 'dve'
        Pst = P.Pst
        BT = 102400
        sc1 = aview(BT, [128, 32, 2, 2], F32)
        sc2 = aview(BT + 512, [128, 32, 2, 2], F32)
        pa = Pst[:, :, :]
        pstep = pa.ap[0]
        for k in range(128):
            if k % 1 == 0 and k > 0:
                yield
            ncol = 2 if k < 32 else 1
            src = _ap(pa, k, [pstep, [324, 32], [162, 2], [129, ncol]])
            dst = _ap(pa, k + 1, [pstep, [324, 32], [162, 2], [129, ncol]])
            if ncol == 1:
                src2 = _ap(pa, k, [pstep, [0, 2], [324, 32], [162, 2]])
                lam2 = _ap(lam_rr, 0, [lam_rr.ap[0], [64, 2], [2, 32], [1, 2]])
                out2 = _ap(sc1, 0, [sc1.ap[0], [128, 2], [4, 32], [2, 2]])
                S.tt(eng, out2, src2, lam2, ALU.mult)
                a1 = _ap(sc1, 0, [sc1.ap[0], [4, 32], [2, 2], [1, 1]])
                a2s = _ap(sc2, 2, [sc2.ap[0], [4, 32], [-2, 2], [1, 1]])
                S.tt(eng, dst, dst, a1, ALU.add)
                S.tt(eng, dst, dst, a2s, ALU.add)
            else:
                lr_b = _ap(lam_rr, 0, [lam_rr.ap[0], [2, 32], [1, 2], [0, ncol]])
                li_b = _ap(lam_is, 0, [lam_is.ap[0], [2, 32], [1, 2], [0, ncol]])
                a1 = sc1[:, :, :, 0:ncol]
                a2 = sc2[:, :, :, 0:ncol]
                a2s = _ap(sc2, 2, [sc2.ap[0], [4, 32], [-2, 2], [1, ncol]])
                S.tt(eng, a1, src, lr_b, ALU.mult)
                S.tt(eng, a2, src, li_b, ALU.mult)
                S.tt(eng, dst, dst, a1, ALU.add)
                S.tt(eng, dst, dst, a2s, ALU.add)
            if (k + 1) % 32 == 0:
                idx = (k + 1) // 32 - 1
                S.copy(eng, Fin[:, idx, :], Pst[:, :, k + 1])
                if k + 1 < 128:
                    S.ts(eng, Pst[:, :, k + 1], Pst[:, :, k + 1], carry, ALU.mult)
                if k == 31:
                    S.copy(eng, Fin[:, 4, :], Pst[:, :, 161])

    def s5_rest():
        eng = 'dve'
        Pst = P.Pst
        ygT = aview(71680, [128, 4, NT], BF16)
        SinA = aview(92160, [128, 32, 160], BF16)
        SinB = aview(51200, [128, 32, 160], BF16)
        Wob = [aview(8192 * i, [128, 2, 4, 2, 8, 32], BF16) for i in range(2)]
        Tpb = [aview(16384 + 4096 * i, [128, 16, 128], BF16) for i in range(2)]
        BT = 102400
        S.dma('sp', so_d.rearrange("i p t -> p i t"), Fin)

        for q in (2, 3, 0, 1):
            Sq = (SinA if q < 2 else SinB)[:, (q % 2) * 16:(q % 2) * 16 + 16, :]
            for d in range(2):
                pqd = Pst[:, q * 16 + d * 8:q * 16 + d * 8 + 8, :]
                sd_ = Sq[:, d * 8:(d + 1) * 8, :]
                if d == 0:
                    S.copy('act', sd_[:, :, 0:128], pqd[:, :, 0:128])
                    S.copy(eng, sd_[:, :, 128:160], pqd[:, :, 129:161])
                else:
                    S.copy('act', sd_[:, :, 0:128], _ap(pqd, 127, [pqd.ap[0], [162, 8], [-1, 128]]))
                    S.copy(eng, sd_[:, :, 128:160], _ap(pqd, 129 + 31, [pqd.ap[0], [162, 8], [-1, 32]]))
        gluW = P.wload(glu_w_d.rearrange("(k p) m -> p k m", p=128))
        for q in range(4):
            Wout = Wob[q % 2]
            Toep = Tpb[q % 2]
            S.dma('sp', Wout[:, :, :, :, :, :].rearrange("p a b c d e -> p (a b c d e)"), WoutD[q])
            S.dma('sp', Toep[:, :, :].rearrange("p a c -> p (a c)"), ToepD[q])
            Sin = (SinA if q < 2 else SinB)[:, (q % 2) * 16:(q % 2) * 16 + 16, :]
            for i in range(8):
                b = P.bank(0, 6)
                o = PS[:, b, 0:160]
                for j in range(8):
                    slot = (i - j - 1) if j < i else ((7 + j - i - 1) if j > i else 14)
                    S.matmul(o, Toep[:, slot, :], uTp[:, q, j, :], start=(j == 0), stop=False)
                i_ = 0
                for d in range(2):
                    e = i + 1 if d == 0 else 8 - i
                    for ri in range(2):
                        for pr in range(4):
                            i_ += 1
                            S.matmul(PS[32 * pr:32 * pr + 32, b, 0:160], Wout[:, d, pr, ri, e - 1, :],
                                     Sin[:, d * 8 + pr * 2 + ri, :], start=False, stop=(i_ > 12),
                                     tile_position=(0, 32 * pr))
                yq = ygT[:, q, :]
                S.act(_ap(yq, i, [yq.ap[0], [8, 160]]), o, AF.Gelu)
        mark('s5_y')
        sg = [aview(BT + 1024, [128, 512], BF16), aview(BT + 2048, [128, 512], BF16)]
        k2 = 0
        for m in range(4):
            for (s_, n) in TB:
                b = P.bank(0, 6)
                for kc in range(4):
                    S.matmul(PS[:, b, 0:n], gluW[:, kc, m * 128:(m + 1) * 128], ygT[:, kc, s_:s_ + n],
                             start=(kc == 0), stop=(kc == 3))
                t = sg[k2 % 2][:, 0:n]
                k2 += 1
                S.act(t, PS[:, b, 0:n], AF.Sigmoid, bias=glubT[:, m:m + 1])
                S.tt('dve', catT[:, 4 + m, s_:s_ + n], ygT[:, m, s_:s_ + n], t, ALU.mult)

    def mixer_even():
        projections()
        mark('proj')
        P.scan_gen = None
        if cfg['s5']:
            s5_defs()
            s5_V()
            mark('s5_V')
            P.scan_gen = s5_scan_gen()
        if cfg['attn']:
            attention()
        else:
            S.memset('dve', catT[:, 0:4, :], 0.0)
        if P.scan_gen is not None:
            for _ in P.scan_gen:
                pass
            P.scan_gen = None
        mark('attn')
        if cfg['s5']:
            s5_rest()
        else:
            S.memset('dve', catT[:, 4:8, :], 0.0)
        mark('s5')
        out_proj_even()
        mark('wout')

    def input_transposes():
        xstage = [aview(0, [128, D], F32), aview(4096, [128, D], F32),
                  aview(98304, [128, D], F32), aview(102400, [128, D], F32)]
        for tt in range(10):
            st = xstage[tt % 4]
            S.dma('sp', st, xin[tt * 128:(tt + 1) * 128, :])
            for half in range(2):
                b = P.bank(0, 4)
                for c4 in range(4):
                    c = half * 4 + c4
                    S.transpose(PS[:, b, c4 * 128:(c4 + 1) * 128], st[:, c * 128:(c + 1) * 128], ident[:])
                src = PS[:, b, :].rearrange("p (c t) -> p c t", t=128)
                dst = xT[:, half * 4:half * 4 + 4, tt * 128:(tt + 1) * 128]
                S.copy('act' if (tt + half) % 2 == 0 else 'dve', dst, src)


    P.marks = []

    def mark(name):
        P.marks.append((name, sum(1 for o in S.ops if o.eng == 'pe'), sum(1 for o in S.ops if o.eng == 'dve'),
                        sum(1 for o in S.ops if o.eng == 'act')))
    if cfg['s5']:
        s5_small()
    input_transposes()
    mark('xin')
    P.s5prepA = None
    P.s5prepB = None
    g0_ = modulation_gen(0)
    if cfg['s5']:
        for _ in range(12):
            next(g0_)
        P.mod1_gen = modulation_gen(1)
        for _ in range(12):
            next(P.mod1_gen)
        s5_prep_batched()
    for _ in g0_:
        pass
    if getattr(P, 'mod1_gen', None) is not None:
        for _ in P.mod1_gen:
            pass
        P.mod1_gen = 'done'
    mark('mod0')
    for layer in range(2):
        if layer == 1 and cfg['fnet']:
            fnet_mixer(1)
            mark('fnet')
        if layer == 0 and (cfg['attn'] or cfg['s5']):
            mixer_even()
        if layer == 0:
            g_ = getattr(P, 'mod1_gen', None)
            if g_ is None:
                g_ = modulation_gen(1)
            if g_ != 'done':
                for _ in g_:
                    pass
            mark('mod1')
        if cfg['ffn']:
            norm_mod(layer, 1)
            ffn(layer)
            mark('ffn%d' % layer)

    sumsq_rstd()
    for c in range(8):
        S.stt(xT[:, c, :], xT[:, c, :], gvec[:, 4, c:c + 1], rstd[:, :], ALU.mult, ALU.mult)
    ystage = [aview(0, [128, D], F32), aview(4096, [128, D], F32)]
    for tt in range(10):
        st = ystage[tt % 2]
        for half in range(2):
            b = P.bank(0, 4)
            for c4 in range(4):
                c = half * 4 + c4
                S.transpose(PS[:, b, c4 * 128:(c4 + 1) * 128], xT[:, c, tt * 128:(tt + 1) * 128], ident[:])
            S.copy('act' if (tt + half) % 2 == 0 else 'dve', st[:, half * 512:(half + 1) * 512], PS[:, b, :])
        S.dma('sp', y_d[tt * 128:(tt + 1) * 128, :], st)

    S.emit(es)
    P.es.close()
    return P


def _core_tokens(c, x_prompt, x_sample):
    if c < 2:
        return np.concatenate([x_sample[c], x_prompt[c]], 0)
    s = 2 + 5 * (c - 2)
    return x_prompt[s:s + 5].reshape(NT, D)


def _fm(v, nch):
    return np.ascontiguousarray(v.reshape(nch, 128).T)


def make_in_maps(inp, cores):
    f32 = np.float32
    shared = {}
    shared['bmodT'] = np.ascontiguousarray(np.stack([_fm(inp['b_mod'][l], 48) for l in range(2)], 1)).astype(f32)
    shared['gvec'] = np.ascontiguousarray(np.stack([
        _fm(inp['norm_mix_g'][0], 8), _fm(inp['norm_ffn_g'][0], 8),
        _fm(inp['norm_mix_g'][1], 8), _fm(inp['norm_ffn_g'][1], 8),
        _fm(inp['final_norm_g'], 8)], 1)).astype(f32)
    shared['ident'] = np.eye(128, dtype=f32)
    for k in ('w_mod', 'ffn_w_gate', 'ffn_w_up', 'ffn_w_down'):
        shared[k] = np.ascontiguousarray(inp[k])
    shared['w_in_odd'] = np.ascontiguousarray(inp['w_in_odd'][0])
    shared['w_out_odd'] = np.ascontiguousarray(inp['w_out_odd'][0])
    bf = ml_dtypes.bfloat16
    ang = 2 * np.pi * np.outer(np.arange(256), np.arange(256)) / 256.0
    shared['cs256'] = np.concatenate([np.cos(ang) / 16.0, -np.sin(ang) / 16.0], 1).astype(bf)
    shared['dft4'] = np.stack([np.cos(ang) / 16.0, np.sin(ang) / 16.0], 0).astype(bf)
    angL = 2 * np.pi * (np.outer(np.arange(1024), np.arange(1024)) % 1024) / 1024.0
    dft_sample = np.stack([np.cos(angL) / 32.0, np.sin(angL) / 32.0], 0).astype(bf)
    dft_prompt = np.zeros((2, 1024, 1024), np.float32)
    for i in range(4):
        dft_prompt[0, i * 256:(i + 1) * 256, i * 256:(i + 1) * 256] = np.cos(ang) / 16.0
        dft_prompt[1, i * 256:(i + 1) * 256, i * 256:(i + 1) * 256] = np.sin(ang) / 16.0
    dft_prompt = dft_prompt.astype(bf)
    KBh = [[0, 1, 2, 3], [0, 1, 2, 3], [0, 1, 2, 3, 4], [1, 2, 3, 4, 5], [2, 3, 4, 5, 6], [3, 4, 5, 6, 7],
           [4, 5, 6, 7], [4, 5, 6, 7]]
    kr_ = np.arange(2)[:, None].repeat(64, 1).reshape(128)
    kc_ = np.arange(64)[None, :].repeat(2, 0).reshape(128)
    mask_s = np.zeros((128, 40, 128), np.float32)
    mask_p = np.zeros((128, 40, 128), np.float32)
    UNh = [[0, 1, 2, 3], [0, 1, 2, 3, 4, 5], [2, 3, 4, 5, 6, 7], [4, 5, 6, 7]]
    mi = 0
    for mp in range(4):
        for n in UNh[mp]:
            for mm in range(2):
                m = 2 * mp + mm
                qrow = 2 * m + kr_[None, :]; qcol = kc_[None, :]
                krow = 2 * n + kr_[:, None]; kcol = kc_[:, None]
                rs = np.clip(qrow - 4, 0, 8); cs = np.clip(qcol - 8, 0, 48)
                ok = (krow >= rs) & (krow < rs + 8) & (kcol >= cs) & (kcol < cs + 16)
                mask_s[:, mi, :] = np.where(ok, 0.0, NEG)
                mask_p[:, mi, :] = 0.0 if (n // 2 == m // 2) else NEG
                mi += 1
    mask_s = mask_s.astype(bf); mask_p = mask_p.astype(bf)
    R_ = inp['na_rpb'][0]
    rpbH = np.zeros((8, 19, 128), f32)
    rpbH[:, 2:17, 48:79] = R_
    shared['w_in_even'] = np.ascontiguousarray(inp['w_in_even'][0])
    shared['w_out_even'] = np.ascontiguousarray(inp['w_out_even'][0])
    lam = np.stack([inp['s5_lam_re'][0], inp['s5_lam_im'][0]], 0)
    bb = np.stack([inp['s5_b_re'][0], inp['s5_b_im'][0]], 0)
    cc = np.stack([inp['s5_c_re'][0], inp['s5_c_im'][0]], 0)
    ls = inp['s5_log_step'][0]
    lam_q = lam.reshape(2, 2, 4, 8, 64)
    sA_lam = np.broadcast_to(lam_q.transpose(3, 0, 2, 1, 4)[:, None], (8, 16, 2, 4, 2, 64)).reshape(128, 2, 4, 2, 64)
    shared['sA_lam'] = np.ascontiguousarray(sA_lam).astype(f32)
    ls_q = ls.reshape(2, 4, 8)
    shared['sA_ls'] = np.ascontiguousarray(
        np.broadcast_to(ls_q.transpose(2, 1, 0)[:, None], (8, 16, 4, 2)).reshape(128, 4, 2)).astype(f32)
    bq = bb.reshape(2, 2, 4, 8, 64, 16)
    shared['sA_b'] = np.ascontiguousarray(bq.transpose(3, 5, 0, 2, 1, 4).reshape(128, 2, 4, 2, 64)).astype(f32)
    par = np.zeros((128, 2), f32)
    gpar = (np.arange(128) // 16) % 2
    par[gpar == 0, 0] = 1.0
    par[gpar == 1, 1] = 1.0
    shared['parA'] = par
    lam_s = lam.reshape(2, 2, 4, 4, 2, 64)
    shared['sB_lam'] = np.ascontiguousarray(lam_s.transpose(4, 5, 0, 2, 1, 3).reshape(128, 2, 32)).astype(f32)
    ls_s = ls.reshape(2, 4, 4, 2)
    shared['sB_ls'] = np.ascontiguousarray(
        np.broadcast_to(ls_s.transpose(3, 1, 0, 2)[:, None], (2, 64, 4, 2, 4)).reshape(128, 32)).astype(f32)
    c_s = cc.reshape(2, 2, 4, 4, 2, 16, 64)
    shared['sB_c'] = np.ascontiguousarray(c_s.transpose(4, 6, 0, 2, 1, 3, 5).reshape(128, 2, 32, 16)).astype(f32)
    b_s = bb.reshape(2, 2, 4, 4, 2, 64, 16)
    shared['sB_b'] = np.ascontiguousarray(b_s.transpose(4, 5, 0, 2, 1, 3, 6).reshape(128, 2, 32, 16)).astype(f32)
    shared['sdT'] = _fm(inp['s5_d'][0], 4).astype(f32)
    shared['glubT'] = _fm(inp['s5_glu_b'][0], 4).astype(f32)
    shared['s5_glu_w'] = np.ascontiguousarray(inp['s5_glu_w'][0])
    maps = []
    for c in cores:
        m = dict(shared)
        m['xin'] = np.ascontiguousarray(_core_tokens(c, inp['x_prompt'], inp['x_sample']))
        cond_long = inp['c'][c] if c < 2 else inp['c_ctx']
        cond = np.stack([cond_long, inp['c_ctx']], 0)
        m['condT'] = np.ascontiguousarray(cond.reshape(2, 8, 128).transpose(2, 1, 0)).astype(f32)
        m['dftL'] = dft_sample if c < 2 else dft_prompt
        if c < 2:
            m['ctxkT'] = np.ascontiguousarray(inp['cache_na_k'][c, 0].reshape(512, 512).T)
            m['ctxv'] = np.ascontiguousarray(inp['cache_na_v'][c, 0].reshape(512, 512))
            m['ctxbias'] = np.zeros((128, 1), f32)
            m['maskt'] = mask_s
            m['rpbH'] = rpbH
            st = inp['state_s5'][c, 0].reshape(2, 2, 4, 4, 2, 64)
            m['s0B'] = np.ascontiguousarray(st.transpose(4, 5, 2, 0, 3, 1).reshape(128, 64)).astype(f32)
            m['carry'] = np.ones((128, 1), f32)
        else:
            m['ctxkT'] = np.zeros((512, 512), f32)
            m['ctxv'] = np.zeros((512, 512), f32)
            m['ctxbias'] = np.full((128, 1), NEG, f32)
            m['maskt'] = mask_p
            m['rpbH'] = np.zeros((8, 19, 128), f32)
            m['s0B'] = np.zeros((128, 64), f32)
            m['carry'] = np.zeros((128, 1), f32)
        maps.append(m)
    return maps


_PROG = {}


def run_cores(inp, cores, cfg=None):
    cfg = dict(CFG) if cfg is None else cfg
    key = tuple(sorted(cfg.items()))
    if key not in _PROG:
        _PROG[key] = build_program(cfg)
    P = _PROG[key]
    maps = make_in_maps(inp, cores)
    maps = [{k: v for k, v in m.items() if k in P.ins} for m in maps]
    res = run_bass_kernel_spmd(P.nc, maps, core_ids=list(range(len(cores))))
    return res.results


def kernel(**inputs):
    inp = {k: np.asarray(v) for k, v in inputs.items()}
    res = run_cores(inp, list(range(8)))
    y_prompt = np.zeros((32, 256, D), np.float32)
    y_sample = np.zeros((2, 1024, D), np.float32)
    for c in range(8):
        y = res[c]['y']
        if c < 2:
            y_sample[c] = y[0:1024]
            y_prompt[c] = y[1024:1280]
        else:
            s = 2 + 5 * (c - 2)
            y_prompt[s:s + 5] = y.reshape(5, 256, D)
    nk = np.zeros((32, 1, 256, 8, 64), np.float32)
    nv = np.zeros((32, 1, 256, 8, 64), np.float32)
    for c in range(8):
        for nm, dst in (('ko', nk), ('vo', nv)):
            a = res[c][nm]
            if c < 2:
                dst[c, 0] = a[1024:1280].reshape(256, 8, 64)
            else:
                s0 = 2 + 5 * (c - 2)
                dst[s0:s0 + 5, 0] = a.reshape(5, 256, 8, 64)
    ns = np.zeros((32, 1, 2, 2, 32, 64), np.float32)
    for c in range(8):
        so = res[c]['so']
        so = so.reshape(5, 2, 64, 4, 2, 4, 2)
        st = so.transpose(0, 4, 6, 3, 5, 1, 2).reshape(5, 2, 2, 32, 64)
        if c < 2:
            ns[c, 0] = st[4]
        else:
            s0 = 2 + 5 * (c - 2)
            for j in range(4):
                ns[s0 + j, 0, 0] = st[j, 0]
                ns[s0 + j, 0, 1] = st[3 - j, 1]
            ns[s0 + 4, 0] = st[4]
    return (y_prompt, y_sample, nk, nv, ns)
```

```python
import numpy as np
import ml_dtypes
from contextlib import ExitStack
import concourse.bass as bass
import concourse.mybir as mybir
from concourse.bass_utils import run_bass_kernel_spmd

F32 = mybir.dt.float32
BF16 = mybir.dt.bfloat16
I32 = mybir.dt.int32
AF = mybir.ActivationFunctionType
ALU = mybir.AluOpType

_DTSIZE = {F32: 4, BF16: 2, I32: 4}

NT = 1280
TB = [(0, 512), (512, 512), (1024, 256)]
CR = [(0, 1024, 0), (1024, 256, 1)]
D = 1024
DFF = 2816
NEG = -30000.0

CFG = dict(attn=True, s5=True, fnet=True, ffn=True, dbg=False, att=9)


def _region(ap):
    t = ap.tensor
    name = t.name
    space = str(ap.space).upper()
    pat = ap.ap
    esz = _DTSIZE[ap.dtype]
    off = int(ap.offset)
    if 'DRAM' in space or 'HBM' in space:
        lo = hi = off
        for st, cn in pat:
            if st >= 0:
                hi += st * (cn - 1)
            else:
                lo += st * (cn - 1)
        return ('D:' + name, 0, 1, lo * esz, (hi + 1) * esz)
    pstep, pcnt = pat[0]
    p0 = off // pstep if pstep else 0
    foff = off - p0 * pstep if pstep else off
    lo = hi = foff
    for st, cn in pat[1:]:
        if st >= 0:
            hi += st * (cn - 1)
        else:
            lo += st * (cn - 1)
    if 'PSUM' in space:
        b0 = (lo * esz) // 2048
        b1 = ((hi + 1) * esz - 1) // 2048
        return ('P:' + name, 0, 128, b0 * 2048, (b1 + 1) * 2048)
    return (name, p0, p0 + pcnt, lo * esz, (hi + 1) * esz)


class Op:
    __slots__ = ('eng', 'fn', 'reads', 'writes', 'dma', 'deps', 'signal', 'sem', 'val', 'idx')


class Sched:
    ENGS = ('pe', 'act', 'dve', 'pool', 'sp')
    NDSEM = 12

    def __init__(self, nc):
        self.nc = nc
        self.ops = []
        self.track_dram = set()

    def add(self, eng, fn, reads=(), writes=(), dma=False):
        op = Op()
        op.eng = eng
        op.fn = fn
        op.reads = [_region(a) for a in reads]
        op.writes = [_region(a) for a in writes]
        op.dma = dma
        op.idx = len(self.ops)
        self.ops.append(op)
        return op

    def matmul(self, out, lhsT, rhs, start=True, stop=True, **kw):
        rd = [lhsT, rhs] + ([] if start else [out])
        return self.add('pe', lambda e: e.matmul(out, lhsT=lhsT, rhs=rhs, start=start, stop=stop, **kw), rd, [out])

    def transpose(self, out, in_, ident):
        return self.add('pe', lambda e: e.transpose(out, in_, ident), [in_, ident], [out])

    def act(self, out, in_, func, bias=None, scale=None, accum_out=None):
        rd = [in_]
        kw = {}
        if bias is not None:
            kw['bias'] = bias
            if not isinstance(bias, (int, float)):
                rd.append(bias)
        if scale is not None:
            kw['scale'] = scale
            if not isinstance(scale, (int, float)):
                rd.append(scale)
        wr = [out]
        if accum_out is not None:
            kw['accum_out'] = accum_out
            wr.append(accum_out)
        return self.add('act', lambda e: e.activation(out=out, in_=in_, func=func, **kw), rd, wr)

    def tt(self, eng, out, in0, in1, op):
        return self.add(eng, lambda e: e.tensor_tensor(out=out, in0=in0, in1=in1, op=op), [in0, in1], [out])

    def ts(self, eng, out, in0, s1, op0, s2=None, op1=None):
        rd = [in0]
        if not isinstance(s1, (int, float)):
            rd.append(s1)
        if s2 is not None and not isinstance(s2, (int, float)):
            rd.append(s2)
        if op1 is None:
            return self.add(eng, lambda e: e.tensor_scalar(out=out, in0=in0, scalar1=s1, scalar2=None, op0=op0),
                            rd, [out])
        return self.add(eng, lambda e: e.tensor_scalar(out=out, in0=in0, scalar1=s1, scalar2=s2, op0=op0, op1=op1),
                        rd, [out])

    def stt(self, out, in0, scalar, in1, op0, op1):
        rd = [in0, in1]
        if not isinstance(scalar, (int, float)):
            rd.append(scalar)
        return self.add('dve', lambda e: e.scalar_tensor_tensor(out=out, in0=in0, scalar=scalar, in1=in1,
                                                                  op0=op0, op1=op1), rd, [out])

    def copy(self, eng, out, in_):
        if eng == 'act':
            return self.add('act', lambda e: e.copy(out=out, in_=in_), [in_], [out])
        return self.add(eng, lambda e: e.tensor_copy(out=out, in_=in_), [in_], [out])

    def recip(self, out, in_):
        return self.add('dve', lambda e: e.reciprocal(out=out, in_=in_), [in_], [out])

    def memset(self, eng, out, val):
        return self.add(eng, lambda e: e.memset(out, val), [], [out])

    def dma(self, eng, out, in_, **kw):
        return self.add(eng, lambda e: e.dma_start(out=out, in_=in_, **kw), [in_], [out], dma=True)

    def _analyze(self):
        live = {}
        ndma = {e: 0 for e in self.ENGS}
        dma_ops = {e: [] for e in self.ENGS}
        ops = self.ops
        for op in ops:
            deps = {}
            for regs, is_write in ((op.reads, False), (op.writes, True)):
                for (nm, p0, p1, b0, b1) in regs:
                    if nm[0:2] == 'D:' and nm not in self.track_dram:
                        continue
                    lst = live.get(nm)
                    if not lst:
                        continue
                    psum = nm[0:2] == 'P:'
                    for (q0, q1, c0, c1, j, w) in lst:
                        if q0 < p1 and p0 < q1 and c0 < b1 and b0 < c1 and j != op.idx:
                            if is_write:
                                kind = 'WAW' if w else 'WAR'
                            else:
                                if not w:
                                    if psum and ops[j].eng != op.eng:
                                        kind = 'RAR'
                                    else:
                                        continue
                                else:
                                    kind = 'RAW'
                            if j not in deps or kind == 'RAW':
                                deps[j] = kind
            for (nm, p0, p1, b0, b1) in op.writes:
                if nm[0:2] == 'D:' and nm not in self.track_dram:
                    continue
                lst = live.setdefault(nm, [])
                lst[:] = [t for t in lst if not (p0 <= t[0] and t[1] <= p1 and b0 <= t[2] and t[3] <= b1)]
                lst.append((p0, p1, b0, b1, op.idx, True))
            for (nm, p0, p1, b0, b1) in op.reads:
                if nm[0:2] == 'D:' and nm not in self.track_dram:
                    continue
                live.setdefault(nm, []).append((p0, p1, b0, b1, op.idx, False))
            if op.dma:
                n = ndma[op.eng]
                if n >= self.NDSEM:
                    deps.setdefault(dma_ops[op.eng][n - self.NDSEM].idx, 'SEM')
                dma_ops[op.eng].append(op)
                ndma[op.eng] = n + 1
            need = []
            best = {}
            for j, kind in deps.items():
                p = ops[j]
                if p.dma:
                    need.append(j)
                    continue
                if p.eng == op.eng and not op.dma:
                    if op.eng == 'pe':
                        continue
                if p.eng not in best or best[p.eng] < j:
                    best[p.eng] = j
            need.extend(best.values())
            op.deps = need
            op.signal = False
        for op in ops:
            for j in op.deps:
                ops[j].signal = True
        self.dma_ops = dma_ops

    def emit(self, es):
        nc = self.nc
        self._analyze()
        csem = {e: es.enter_context(nc.semaphore('c_' + e)) for e in ('pe', 'act', 'dve', 'pool')}
        dsem = {e: [es.enter_context(nc.semaphore('d_%s%d' % (e, i))) for i in range(self.NDSEM)]
                for e in ('sp', 'pool', 'act')}
        cnt = {e: 0 for e in self.ENGS}
        nd = {e: 0 for e in self.ENGS}
        for op in self.ops:
            if op.dma:
                n = nd[op.eng]
                nd[op.eng] = n + 1
                op.sem = dsem[op.eng][n % self.NDSEM]
                op.val = 16 * (n // self.NDSEM + 1)
                op.signal = True
            elif op.signal:
                cnt[op.eng] += 1
                op.sem = csem[op.eng]
                op.val = cnt[op.eng]
        self.stats = {e: (len([o for o in self.ops if o.eng == e]), cnt[e], nd[e]) for e in self.ENGS}
        block = es.enter_context(nc.Block())
        per = {e: [op for op in self.ops if op.eng == e] for e in self.ENGS}
        ops = self.ops
        dma_ops = self.dma_ops

        def run(engname, e):
            waited = {}
            for op in per[engname]:
                for j in op.deps:
                    p = ops[j]
                    key = id(p.sem)
                    if waited.get(key, 0) < p.val:
                        e.wait_ge(p.sem, p.val)
                        waited[key] = p.val
                inst = op.fn(e)
                if op.signal:
                    inst.then_inc(op.sem, 16 if op.dma else 1)
            for op in dma_ops[engname]:
                key = id(op.sem)
                if waited.get(key, 0) < op.val:
                    e.wait_ge(op.sem, op.val)
                    waited[key] = op.val

        @block.tensor
        def _(e):
            run('pe', e)

        @block.scalar
        def _(e):
            run('act', e)

        @block.vector
        def _(e):
            run('dve', e)

        @block.gpsimd
        def _(e):
            run('pool', e)

        @block.sync
        def _(e):
            run('sp', e)


def _ap(base, off, pat):
    return bass.AP(base.tensor, int(base.offset) + off, [list(x) for x in pat])


class Prog:
    def __init__(self, cfg):
        self.cfg = cfg
        self.nc = bass.Bass("TRN2", target_bir_lowering=False)
        self.es = ExitStack()
        self.S = Sched(self.nc)
        self.ins = {}
        self.outs = {}
        self.wslot_i = 0
        self.bank_i = 0

    def din(self, name, shape, dt=F32):
        a = self.nc.dram_tensor(name, list(shape), dt, kind="ExternalInput").ap()
        self.ins[name] = a
        return a

    def dout(self, name, shape, dt=F32):
        a = self.nc.dram_tensor(name, list(shape), dt, kind="ExternalOutput").ap()
        self.outs[name] = a
        return a

    def sb(self, name, shape, dt):
        return self.es.enter_context(self.nc.sbuf_tensor(name, list(shape), dt))

    def wload(self, src3):
        kc, mw = src3.shape[1], src3.shape[2]
        assert kc * mw <= self.WELEMS, (kc, mw)
        slot = self.wslots[self.wslot_i % getattr(self, 'NW_eff', self.NW)]
        self.wslot_i += 1
        v = slot[:, 0:kc * mw].rearrange("p (k m) -> p k m", m=mw)
        self.S.dma('pool', v, src3)
        return v

    def bank(self, lo=0, hi=4):
        b = lo + (self.bank_i % (hi - lo))
        self.bank_i += 1
        return b


def build_program(cfg):
    P = Prog(cfg)
    nc, S, es = P.nc, P.S, P.es
    din, dout, sb = P.din, P.dout, P.sb

    xin = din("xin", [NT, D])
    condT_d = din("condT", [128, 8, 2])
    bmodT_d = din("bmodT", [128, 2, 48])
    gvec_d = din("gvec", [128, 5, 8])
    ident_d = din("ident", [128, 128])
    w_mod = din("w_mod", [2, D, 6 * D])
    wg_d = din("ffn_w_gate", [2, D, DFF])
    wu_d = din("ffn_w_up", [2, D, DFF])
    wd_d = din("ffn_w_down", [2, DFF, D])
    y_d = dout("y", [NT, D])
    w_in_odd = din("w_in_odd", [D, D])
    w_in_even = din("w_in_even", [D, 2048])
    w_out_even = din("w_out_even", [D, D])
    ctxkT_d = din("ctxkT", [512, 512])
    ctxv_d = din("ctxv", [512, 512])
    ctxbias_d = din("ctxbias", [128, 1])
    maskt_d = din("maskt", [128, 40, 128], BF16)
    rpbH_d = din("rpbH", [8, 19, 128])
    sA_lam_d = din("sA_lam", [128, 2, 4, 2, 64])
    sA_ls_d = din("sA_ls", [128, 4, 2])
    sA_b_d = din("sA_b", [128, 2, 4, 2, 64])
    parA_d = din("parA", [128, 2])
    sB_lam_d = din("sB_lam", [128, 2, 32])
    sB_ls_d = din("sB_ls", [128, 32])
    sB_c_d = din("sB_c", [128, 2, 32, 16])
    sB_b_d = din("sB_b", [128, 2, 32, 16])
    s0B_d = din("s0B", [128, 64])
    carry_d = din("carry", [128, 1])
    sdT_d = din("sdT", [128, 4])
    glubT_d = din("glubT", [128, 4])
    glu_w_d = din("s5_glu_w", [512, 512])
    so_d = dout("so", [5, 128, 64])
    ko_d = dout("ko", [NT, 512])
    vo_d = dout("vo", [NT, 512])
    w_out_odd = din("w_out_odd", [D, D])
    cs256_d = din("cs256", [256, 512], BF16)
    dftL_d = din("dftL", [2, 1024, 1024], BF16)
    dft4_d = din("dft4", [2, 256, 256], BF16)

    xT = sb("xT", [128, 8, NT], F32)
    rstd = sb("rstd", [128, NT], F32)
    ntmp = sb("ntmp", [128, 2, 512], F32)
    ident_bf = sb("ident_bf", [128, 128], BF16)
    ctxbias = sb("ctxbias_s", [128, 1], F32)
    sbm = sb("sbm", [128, 1280], F32)
    ident = sb("ident_s", [128, 128], F32)
    ones_bf = sb("ones_bf", [128, 128], BF16)
    condT = sb("condT_s", [128, 8, 2], F32)
    scT = sb("scT", [128, 8, 2], BF16)
    bmodT = sb("bmodT_s", [128, 2, 48], F32)
    gvec = sb("gvec_s", [128, 5, 8], F32)
    modT = sb("modT", [128, 2, 48, 2], F32)
    gs = sb("gs", [128, 4, 8, 2], F32)
    P.NW = 5
    P.WELEMS = 4096
    wsl = sb("wsl", [128, P.NW * P.WELEMS + 768], BF16)
    P.wslots = [wsl[:, i * P.WELEMS:(i + 1) * P.WELEMS] for i in range(P.NW)]
    ARENA_BYTES = 111104
    arena = sb("arena", [128, ARENA_BYTES // 2], BF16)
    PS = es.enter_context(nc.psum_tensor("PS", [128, 8, 512], F32))
    PSM = PS[:, 7, 0:256].rearrange("p (l c) -> p l c", l=2)

    def aview(off_bytes, shape, dt):
        esz = _DTSIZE[dt]
        n = int(np.prod(shape[1:]))
        assert off_bytes % 4 == 0 and off_bytes + n * esz <= ARENA_BYTES, (off_bytes, shape)
        a = arena[:, off_bytes // 2: off_bytes // 2 + n * esz // 2]
        if dt != BF16:
            a = a.bitcast(dt)
        if len(shape) == 2:
            return a
        names = " ".join("d%d" % i for i in range(len(shape) - 1))
        kw = {"d%d" % i: shape[i + 1] for i in range(len(shape) - 1)}
        return a.rearrange("p (%s) -> p %s" % (names, names), **kw)

    S.dma('sp', ident[:], ident_d)
    S.dma('sp', condT[:], condT_d)
    S.dma('sp', bmodT[:], bmodT_d)
    S.dma('sp', gvec[:], gvec_d)
    S.memset('dve', ones_bf[:], 1.0)
    S.copy('dve', ident_bf[:], ident[:])
    S.dma('sp', ctxbias[:], ctxbias_d)
    S.act(scT[:], condT[:], AF.Silu)

    def modulation_gen(layer):
        for u in range(12):
            W = P.wload(w_mod[layer][:, u * 512:(u + 1) * 512].rearrange("(k p) m -> p k m", p=128))
            for mt in range(4):
                col = (u * 4 + mt) * 2
                for kc in range(8):
                    S.matmul(PSM[:, layer, col:col + 2], W[:, kc, mt * 128:(mt + 1) * 128], scT[:, kc, :],
                             start=(kc == 0), stop=(kc == 7))
            yield u
        src = PSM[:, layer, 0:96].rearrange("p (m c) -> p m c", c=2)
        b_ = bmodT[:, layer, :]
        bb = _ap(b_, 0, [b_.ap[0], [1, 48], [0, 2]])
        S.tt('dve', modT[:, layer, :, :], src, bb, ALU.add)
        for which in range(2):
            sc = modT[:, layer, (1 + 3 * which) * 8:(2 + 3 * which) * 8, :]
            g_ = gvec[:, layer * 2 + which, :]
            gb = _ap(g_, 0, [g_.ap[0], [1, 8], [0, 2]])
            S.stt(gs[:, layer * 2 + which, :, :], sc, 1.0, gb, ALU.add, ALU.mult)

    def modulation(layer):
        for _ in modulation_gen(layer):
            pass

    def mod_vec(layer, j, c, ci):
        return modT[:, layer, j * 8 + c, ci:ci + 1]

    hT = aview(0, [128, 8, NT], BF16)
    sq = aview(20480, [128, 8, NT], BF16)

    def sumsq_rstd_tb(s, n):
        S.act(sq[:, :, s:s + n], xT[:, :, s:s + n], AF.Square)
        b = P.bank(0, 4)
        for c in range(8):
            S.matmul(PS[:, b, 0:n], ones_bf[:], sq[:, c, s:s + n], start=(c == 0), stop=(c == 7))
        S.act(rstd[:, s:s + n], PS[:, b, 0:n], AF.Sqrt, bias=epsb[:, 0:1], scale=1.0 / D)
        S.recip(rstd[:, s:s + n], rstd[:, s:s + n])

    def sumsq_rstd():
        for (s, n) in TB:
            sumsq_rstd_tb(s, n)

    def norm_mod(layer, which):
        k = 0
        for ti, (s, n) in enumerate(TB):
            sumsq_rstd_tb(s, n)
        for ti, (s, n) in enumerate(TB):
            ci = 0 if ti < 2 else 1
            for c in range(8):
                t = ntmp[:, k % 2, 0:n]
                k += 1
                S.tt('dve', t, xT[:, c, s:s + n], rstd[:, s:s + n], ALU.mult)
                S.act(hT[:, c, s:s + n], t, AF.Identity,
                      bias=mod_vec(layer, 3 * which, c, ci), scale=gs[:, layer * 2 + which, c, ci:ci + 1])

    epsb = sb("epsb", [128, 1], F32)
    S.memset('dve', epsb[:], 1e-6)

    aT = aview(20480, [128, 22, NT], BF16)
    sgt = [aview(76800, [128, 512], BF16), aview(77824, [128, 512], BF16)]

    def ffn(layer):
        groups = [(i * 512, 512) for i in range(5)] + [(2560, 256)]
        k = 0
        for (m0, mw) in groups:
            Wg = P.wload(wg_d[layer][:, m0:m0 + mw].rearrange("(k p) m -> p k m", p=128))
            Wu = P.wload(wu_d[layer][:, m0:m0 + mw].rearrange("(k p) m -> p k m", p=128))
            for mt in range(mw // 128):
                j = m0 // 128 + mt
                for (s, n) in TB:
                    bg = P.bank(0, 6)
                    bu = P.bank(0, 6)
                    for kc in range(8):
                        S.matmul(PS[:, bg, 0:n], Wg[:, kc, mt * 128:(mt + 1) * 128], hT[:, kc, s:s + n],
                                 start=(kc == 0), stop=(kc == 7))
                    for kc in range(8):
                        S.matmul(PS[:, bu, 0:n], Wu[:, kc, mt * 128:(mt + 1) * 128], hT[:, kc, s:s + n],
                                 start=(kc == 0), stop=(kc == 7))
                    t = sgt[k % 2][:, 0:n]
                    k += 1
                    S.act(t, PS[:, bg, 0:n], AF.Silu)
                    S.tt('dve', aT[:, j, s:s + n], t, PS[:, bu, 0:n], ALU.mult)
        for mg in range(4):
            Wd = [P.wload(wd_d[layer][kh * 1408:(kh + 1) * 1408, mg * 256:(mg + 1) * 256]
                          .rearrange("(k p) m -> p k m", p=128)) for kh in range(2)]
            for mt in range(2):
                m = mg * 2 + mt
                for ti, (s, n) in enumerate(TB):
                    b = P.bank(0, 6)
                    for kk in range(22):
                        S.matmul(PS[:, b, 0:n], Wd[kk // 11][:, kk % 11, mt * 128:(mt + 1) * 128],
                                 aT[:, kk, s:s + n], start=(kk == 0), stop=(kk == 21))
                    ci = 0 if ti < 2 else 1
                    S.stt(xT[:, m, s:s + n], PS[:, b, 0:n], mod_vec(layer, 5, m, ci), xT[:, m, s:s + n],
                          ALU.mult, ALU.add)

    def evac(i, out, in_):
        S.copy('act' if i % 2 == 0 else 'dve', out, in_)

    def fnet_mixer(layer):
        zT = aview(20480, [128, 8, NT], BF16)
        ZCS = aview(40960, [128, 10, 4, 512], BF16)
        fT = aview(0, [128, 8, NT], BF16)
        cs256 = aview(81920, [128, 2, 512], BF16)
        dft4 = aview(83968, [128, 2, 2, 256], BF16)
        S.dma('sp', cs256, cs256_d.rearrange("(c p) m -> p c m", p=128))
        S.dma('sp', dft4, dft4_d.rearrange("a (t p) k -> p a t k", p=128))
        norm_mod(layer, 0)
        k = 0
        for u in range(2):
            W = P.wload(w_in_odd[:, u * 512:(u + 1) * 512].rearrange("(k p) m -> p k m", p=128))
            for mt in range(4):
                m = u * 4 + mt
                for (s_, n) in TB:
                    b = P.bank(0, 6)
                    for kc in range(8):
                        S.matmul(PS[:, b, 0:n], W[:, kc, mt * 128:(mt + 1) * 128], hT[:, kc, s_:s_ + n],
                                 start=(kc == 0), stop=(kc == 7))
                    evac(k, zT[:, m, s_:s_ + n], PS[:, b, 0:n]); k += 1
        for tt in range(10):
            for gq in range(4):
                b = P.bank(0, 6)
                for cc in range(2):
                    S.matmul(PS[:, b, :], zT[:, 2 * gq + cc, tt * 128:(tt + 1) * 128], cs256[:, cc, :],
                             start=(cc == 0), stop=(cc == 1))
                evac(k, ZCS[:, tt, gq, :], PS[:, b, :]); k += 1
        for kb in range(2):
            CL = P.wload(dftL_d[0][:, kb * 512:(kb + 1) * 512].rearrange("(t p) k -> p t k", p=128))
            SL = P.wload(dftL_d[1][:, kb * 512:(kb + 1) * 512].rearrange("(t p) k -> p t k", p=128))
            for m in range(8):
                gq, half = m // 2, m % 2
                b = P.bank(0, 6)
                for tt in range(8):
                    S.matmul(PS[:, b, :], ZCS[:, tt, gq, half * 128:(half + 1) * 128], CL[:, tt, :],
                             start=(tt == 0), stop=False)
                for tt in range(8):
                    S.matmul(PS[:, b, :], ZCS[:, tt, gq, 256 + half * 128:256 + (half + 1) * 128], SL[:, tt, :],
                             start=False, stop=(tt == 7))
                evac(k, fT[:, m, kb * 512:(kb + 1) * 512], PS[:, b, :]); k += 1
        for m in range(8):
            gq, half = m // 2, m % 2
            b = P.bank(0, 6)
            i = 0
            for a in range(2):
                for t in range(2):
                    S.matmul(PS[:, b, 0:256], ZCS[:, 8 + t, gq, a * 256 + half * 128:a * 256 + (half + 1) * 128],
                             dft4[:, a, t, :], start=(i == 0), stop=(i == 3))
                    i += 1
            evac(k, fT[:, m, 1024:1280], PS[:, b, 0:256]); k += 1
        for u in range(2):
            W = P.wload(w_out_odd[:, u * 512:(u + 1) * 512].rearrange("(k p) m -> p k m", p=128))
            for mt in range(4):
                m = u * 4 + mt
                for ti, (s_, n) in enumerate(TB):
                    b = P.bank(0, 6)
                    for kc in range(8):
                        S.matmul(PS[:, b, 0:n], W[:, kc, mt * 128:(mt + 1) * 128], fT[:, kc, s_:s_ + n],
                                 start=(kc == 0), stop=(kc == 7))
                    ci = 0 if ti < 2 else 1
                    S.stt(xT[:, m, s_:s_ + n], PS[:, b, 0:n], mod_vec(layer, 2, m, ci), xT[:, m, s_:s_ + n],
                          ALU.mult, ALU.add)

    KB = [[0, 1, 2, 3], [0, 1, 2, 3], [0, 1, 2, 3, 4], [1, 2, 3, 4, 5], [2, 3, 4, 5, 6], [3, 4, 5, 6, 7],
          [4, 5, 6, 7], [4, 5, 6, 7]]
    qT = aview(40960, [128, 4, NT], BF16)
    kT = aview(51200, [128, 4, NT], BF16)
    uTp = aview(61440, [128, 4, 8, 160], BF16)
    vtok = aview(71680, [128, 10, 512], BF16)
    catT = aview(81920, [128, 8, NT], BF16)
    kvst = [aview(102400, [128, 512], F32), aview(104448, [128, 512], F32)]
    rec = aview(106496, [128, 2, 256], F32)
    rstage = [aview(108544, [128, 9, 128], BF16)]

    def pp(n, nb=None):
        for attr, cnt in (('s5prepA', n), ('s5prepB', n if nb is None else nb)):
            for _ in range(cnt):
                g_ = getattr(P, attr, None)
                if g_ is None:
                    break
                try:
                    next(g_)
                except StopIteration:
                    setattr(P, attr, None)

    def projections():
        norm_mod(0, 0)

        def wl(i):
            return P.wload(w_in_even[:, i * 512:(i + 1) * 512].rearrange("(k p) m -> p k m", p=128))
        k = 0
        Wq = wl(0)
        Wk = wl(1)
        for mt in range(4):
            for (s_, n) in TB:
                b = P.bank(0, 6)
                for kc in range(8):
                    S.matmul(PS[:, b, 0:n], Wq[:, kc, mt * 128:(mt + 1) * 128], hT[:, kc, s_:s_ + n],
                             start=(kc == 0), stop=(kc == 7))
                S.act(qT[:, mt, s_:s_ + n], PS[:, b, 0:n], AF.Identity, scale=0.125)
            pp(1)
        Wv = wl(2)
        for mt in range(4):
            for (s_, n) in TB:
                b = P.bank(0, 6)
                for kc in range(8):
                    S.matmul(PS[:, b, 0:n], Wk[:, kc, mt * 128:(mt + 1) * 128], hT[:, kc, s_:s_ + n],
                             start=(kc == 0), stop=(kc == 7))
                S.copy('act', kT[:, mt, s_:s_ + n], PS[:, b, 0:n])
            pp(1)
        for tt in range(10):
            b = P.bank(0, 6)
            for kc in range(8):
                S.matmul(PS[:, b, :], hT[:, kc, tt * 128:(tt + 1) * 128], Wk[:, kc, :],
                         start=(kc == 0), stop=(kc == 7))
            st = kvst[tt % 2]
            S.copy('act', st, PS[:, b, :])
            S.dma('sp', ko_d[tt * 128:(tt + 1) * 128, :], st)
            pp(1)
        Wu = wl(3)
        for tt in range(10):
            b = P.bank(0, 6)
            for kc in range(8):
                S.matmul(PS[:, b, :], hT[:, kc, tt * 128:(tt + 1) * 128], Wv[:, kc, :],
                         start=(kc == 0), stop=(kc == 7))
            st = kvst[tt % 2]
            S.copy('act', st, PS[:, b, :])
            S.copy('act', vtok[:, tt, :], PS[:, b, :])
            S.dma('sp', vo_d[tt * 128:(tt + 1) * 128, :], st)
            pp(1)
        for mt in range(4):
            for (s_, n) in TB:
                b = P.bank(0, 6)
                for kc in range(8):
                    S.matmul(PS[:, b, 0:n], Wu[:, kc, mt * 128:(mt + 1) * 128], hT[:, kc, s_:s_ + n],
                             start=(kc == 0), stop=(kc == 7))
                pv = PS[:, b, 0:n]
                src = _ap(pv, 0, [pv.ap[0], [1, 8], [8, n // 8]])
                dst = uTp[:, mt, :, s_ // 8:(s_ + n) // 8]
                S.copy('act', dst, src)
            pp(1)

    def attention():
        ctxkT = aview(0, [128, 4, 512], BF16)
        ctxv = aview(4096, [128, 4, 512], BF16)
        E = aview(8192, [128, 2, 10, 256], BF16)
        maskt = aview(18432, [128, 40, 128], BF16)
        rpbT = aview(28672, [128, 2, 9, 128], BF16)
        stgs = [rstage[0], aview(103680, [128, 9, 128], BF16)]
        rec2 = aview(106496, [128, 2, 256], F32)
        S.dma('pool', ctxkT, ctxkT_d.rearrange("(c p) k -> p c k", p=128))
        S.dma('pool', ctxv, ctxv_d.rearrange("(j p) f -> p j f", p=128))
        S.dma('sp', maskt, maskt_d)
        UN = [[0, 1, 2, 3], [0, 1, 2, 3, 4, 5], [2, 3, 4, 5, 6, 7], [4, 5, 6, 7]]
        ti0 = [0, 8, 20, 32]

        def build_bias(h):
            stg = stgs[h % 2]
            for krl in range(2):
                src = bass.AP(rpbH_d.tensor, h * 19 * 128 + krl * 128, [[1, 64], [256, 9], [128, 2], [1, 64]])
                dst = stg[krl * 64:(krl + 1) * 64, :, :].rearrange("p a (r c) -> p a r c", c=64)
                S.dma('pool', dst, src)

        its = []
        for j_ in range(4):
            for (hh, mp_) in ((0, 0), (0, 1), (1, 0), (1, 1), (0, 2), (0, 3), (1, 2), (1, 3)):
                its.append((2 * j_ + hh, mp_))

        def build_bm(it):
            h, mp = its[it]
            if mp == 0:
                a_ = stgs[h % 2][:, :, :]
                rev = _ap(a_, 127, [a_.ap[0], [128, 9], [-1, 128]])
                S.copy('act', rpbT[:, h % 2, :, :], rev)
            if mp == 3 and h + 2 < 8:
                build_bias(h + 2)

        def sreg(slot):
            b = slot // 2
            return PS[:, b, (slot % 2) * 256:(slot % 2) * 256 + 256]

        def scores(it):
            h, mp = its[it]
            hp, hc = h % 2, h // 2
            pr = slice(64 * hp, 64 * hp + 64)
            Ju = len(UN[mp])
            qsl = qT[pr, hc, mp * 256:(mp + 1) * 256]
            for jn, n in enumerate(UN[mp]):
                S.matmul(sreg(jn), kT[pr, hc, n * 128:(n + 1) * 128], qsl, start=(jn % 2 == 0), stop=False,
                         skip_group_check=True)
            for j in range(4):
                S.matmul(sreg(6 + j), ctxkT[pr, hc, j * 128:(j + 1) * 128], qsl, start=(j % 2 == 0), stop=True,
                         skip_group_check=True)
            rp = rpbT[:, h % 2, :, :]
            for jn, n in enumerate(UN[mp]):
                t0_ = ti0[mp] + 2 * jn
                S.matmul(sreg(jn), ident_bf[:], maskt[:, t0_:t0_ + 2, :].rearrange("p a c -> p (a c)"),
                         start=False, stop=False, skip_group_check=True)
                d0 = n - 2 * mp + 4
                rhs = _ap(rp, d0 * 128, [rp.ap[0], [-128, 2], [1, 128]])
                S.matmul(sreg(jn), ident_bf[:], rhs, start=False, stop=True, skip_group_check=True)

        def exps(it):
            h, mp = its[it]
            Ju = len(UN[mp])
            Et = E[:, it % 2, :, :].rearrange("p a c -> p (a c)")
            Sl = PS[:, 0:3, :].rearrange("p b c -> p (b c)")
            Sc = PS[:, 3:5, :].rearrange("p b c -> p (b c)")
            S.act(Et[:, 0:Ju * 256], Sl[:, 0:Ju * 256], AF.Exp)
            S.act(Et[:, 1536:2560], Sc, AF.Exp, bias=ctxbias[:, 0:1])

        def pv(it):
            h, mp = its[it]
            hp, hc = h % 2, h // 2
            pr = slice(64 * hp, 64 * hp + 64)
            Ju = len(UN[mp])
            bnk = 5 + mp % 2
            num = PS[pr, bnk, 0:256]
            den = PS[pr, bnk, 256:512]
            tot = Ju + 4
            for j in range(tot):
                lv = vtok[:, UN[mp][j], h * 64:(h + 1) * 64] if j < Ju else ctxv[:, j - Ju, h * 64:(h + 1) * 64]
                ev = E[:, it % 2, j if j < Ju else 6 + j - Ju, :]
                S.matmul(num, lv, ev, start=(j == 0), stop=(j == tot - 1))
            for j in range(tot):
                ev = E[:, it % 2, j if j < Ju else 6 + j - Ju, :]
                S.matmul(den, ones_bf[:, 0:64], ev, start=(j == 0), stop=(j == tot - 1))
            if hp == 1:
                rc = rec2[:, mp % 2, :]
                S.recip(rc, PS[:, bnk, 256:512])
                S.tt('dve', catT[:, hc, mp * 256:(mp + 1) * 256], PS[:, bnk, 0:256], rc, ALU.mult)

        def pump():
            if getattr(P, 'mod1_gen', None) is not None:
                try:
                    next(P.mod1_gen)
                except StopIteration:
                    P.mod1_gen = None

        def pump_prep(n):
            for _ in range(n):
                if getattr(P, 's5prep', None) is None:
                    return
                try:
                    next(P.s5prep)
                except StopIteration:
                    P.s5prep = None

        NI = len(its)
        build_bias(0)
        build_bias(1)
        build_bm(0)
        build_bm(1)
        scores(0)
        exps(0)
        for it in range(1, NI + 1):
            if it + 1 < NI:
                build_bm(it + 1)
            if it < NI:
                scores(it)
                exps(it)
            pv(it - 1)
            for _ in range(4):
                if getattr(P, 'scan_gen', None) is not None:
                    try:
                        next(P.scan_gen)
                    except StopIteration:
                        P.scan_gen = None
        it = 0
        for h in range(8):
            hp, hc = h % 2, h // 2
            pr = slice(64 * hp, 64 * hp + 64)
            Sreg = PS[:, it % 2, :]
            Et = E[:, it % 2, 0:2, :]
            for kb in range(2):
                S.matmul(Sreg[:, kb * 256:(kb + 1) * 256], kT[pr, hc, 1024 + kb * 128:1024 + (kb + 1) * 128],
                         qT[pr, hc, 1024:1280], start=(kb == 0), stop=True, skip_group_check=True)
            S.act(Et.rearrange("p a c -> p (a c)"), Sreg, AF.Exp)
            bnk = 5 + hc % 2
            num = PS[pr, bnk, 0:256]
            den = PS[pr, bnk, 256:512]
            for kb in range(2):
                S.matmul(num, vtok[:, 8 + kb, h * 64:(h + 1) * 64], Et[:, kb, :], start=(kb == 0), stop=(kb == 1))
            for kb in range(2):
                S.matmul(den, ones_bf[:, 0:64], Et[:, kb, :], start=(kb == 0), stop=(kb == 1))
            if hp == 1:
                rc = rec2[:, hc % 2, :]
                S.recip(rc, PS[:, bnk, 256:512])
                S.tt('dve', catT[:, hc, 1024:1280], PS[:, bnk, 0:256], rc, ALU.mult)
            it += 1

    def out_proj_even():
        for u in range(2):
            W = P.wload(w_out_even[:, u * 512:(u + 1) * 512].rearrange("(k p) m -> p k m", p=128))
            for mt in range(4):
                m = u * 4 + mt
                for ti, (s_, n) in enumerate(TB):
                    b = P.bank(0, 6)
                    for kc in range(8):
                        S.matmul(PS[:, b, 0:n], W[:, kc, mt * 128:(mt + 1) * 128], catT[:, kc, s_:s_ + n],
                                 start=(kc == 0), stop=(kc == 7))
                    ci = 0 if ti < 2 else 1
                    S.stt(xT[:, m, s_:s_ + n], PS[:, b, 0:n], mod_vec(0, 2, m, ci), xT[:, m, s_:s_ + n],
                          ALU.mult, ALU.add)

    TWO_PI = 6.283185

    def disc(eng, lam, ls_b, T, F):
        ar, ai, cr, ci, lr, t0, t1, t2, t3, mag = T[:10]
        ti = T[10].bitcast(I32)
        S.ts(eng, lr, lam[:, 0], -1e-4, ALU.min)
        S.tt(eng, t0, lr, ls_b, ALU.mult)
        S.act(mag, t0, AF.Exp)
        S.tt(eng, t0, lam[:, 1], ls_b, ALU.mult)
        S.ts(eng, t0, t0, 1.0 / (2 * np.pi), ALU.mult)
        S.copy(eng, ti, t0)
        S.copy(eng, t1, ti)
        S.tt(eng, t1, t0, t1, ALU.subtract)
        S.act(t2, t1, AF.Sin, scale=TWO_PI)
        S.tt(eng, ai, mag, t2, ALU.mult)
        S.ts(eng, t0, t0, 0.25, ALU.add)
        S.copy(eng, ti, t0)
        S.copy(eng, t1, ti)
        S.tt(eng, t1, t0, t1, ALU.subtract)
        S.act(t2, t1, AF.Sin, scale=TWO_PI)
        S.tt(eng, ar, mag, t2, ALU.mult)
        li = lam[:, 1]
        S.ts(eng, t0, ar, -1.0, ALU.add)
        S.tt(eng, t1, lr, lr, ALU.mult)
        S.tt(eng, t2, li, li, ALU.mult)
        S.tt(eng, t1, t1, t2, ALU.add)
        S.recip(t1, t1)
        S.tt(eng, t2, t0, lr, ALU.mult)
        S.tt(eng, t3, ai, li, ALU.mult)
        S.tt(eng, t2, t2, t3, ALU.add)
        S.tt(eng, cr, t2, t1, ALU.mult)
        S.tt(eng, t2, ai, lr, ALU.mult)
        S.tt(eng, t3, t0, li, ALU.mult)
        S.tt(eng, t2, t2, t3, ALU.subtract)
        S.tt(eng, ci, t2, t1, ALU.mult)
        return ar, ai, cr, ci

    def bc(ap2, shape, pat):
        return _ap(ap2, 0, [ap2.ap[0]] + pat)

    def tview(t2d, off_bytes, shape, dt):
        esz0 = _DTSIZE[t2d.dtype]
        esz = _DTSIZE[dt]
        n = int(np.prod(shape[1:]))
        a = t2d[:, off_bytes // esz0:(off_bytes + n * esz) // esz0]
        if dt != t2d.dtype:
            a = a.bitcast(dt)
        if len(shape) == 2:
            return a
        names = " ".join("d%d" % i for i in range(len(shape) - 1))
        kw = {"d%d" % i: shape[i + 1] for i in range(len(shape) - 1)}
        return a.rearrange("p (%s) -> p %s" % (names, names), **kw)

    lsA = sbm[:, 0:8]
    parA = sbm[:, 8:10]
    carry = sbm[:, 10:11]
    sdT = sbm[:, 12:16]
    glubT = sbm[:, 16:20]
    dtA = sbm[:, 20:28]
    lamB = sbm[:, 32:96].rearrange("p (a f) -> p a f", a=2)
    lsB = sbm[:, 96:128]
    dtB = sbm[:, 128:160]
    Fin = sbm[:, 160:480].rearrange("p (i t) -> p i t", i=5)
    PB = sbm[:, 480:1056].rearrange("p (a e f) -> p a e f", a=2, e=9)
    lam_rr = sbm[:, 1056:1120].rearrange("p (f r) -> p f r", r=2)
    lam_is = sbm[:, 1120:1184].rearrange("p (f r) -> p f r", r=2)
    crB = sbm[:, 1184:1216]
    ciB = sbm[:, 1216:1248]
    WinD = nc.dram_tensor("WinD", [4, 128, 4096], BF16, kind="Internal").ap()
    WoutD = nc.dram_tensor("WoutD", [4, 128, 4096], BF16, kind="Internal").ap()
    ToepD = nc.dram_tensor("ToepD", [4, 128, 2048], BF16, kind="Internal").ap()
    for nm_ in ("WinD", "WoutD", "ToepD"):
        S.track_dram.add('D:' + nm_)
    rs2 = rstd[:, :]
    nt2 = ntmp[:, :, :].rearrange("p a b -> p (a b)")

    def s5_small():
        eng = 'dve'
        S.dma('sp', lsA, sA_ls_d.rearrange("p q d -> p (q d)"))
        S.dma('sp', parA, parA_d)
        S.dma('sp', carry, carry_d)
        S.dma('sp', sdT, sdT_d)
        S.dma('sp', glubT, glubT_d)
        S.act(dtA, lsA, AF.Exp)
        S.dma('sp', lamB, sB_lam_d)
        S.dma('sp', lsB, sB_ls_d)
        S.act(dtB, lsB, AF.Exp)
        TB_ = [tview(rs2, 1536 + 128 * i, [128, 32], F32) for i in range(11)]
        arB, aiB, crB_, ciB_ = disc(eng, lamB, dtB, TB_, 32)
        S.copy(eng, crB, crB_)
        S.copy(eng, ciB, ciB_)
        S.memset(eng, PB[:, 0, 0, :], 1.0)
        S.memset(eng, PB[:, 1, 0, :], 0.0)
        S.copy(eng, PB[:, 0, 1, :], arB)
        S.copy(eng, PB[:, 1, 1, :], aiB)
        u1, u2 = TB_[4], TB_[5]
        for e in range(2, 9):
            S.tt(eng, u1, PB[:, 0, e - 1, :], arB, ALU.mult)
            S.tt(eng, u2, PB[:, 1, e - 1, :], aiB, ALU.mult)
            S.tt(eng, PB[:, 0, e, :], u1, u2, ALU.subtract)
            S.tt(eng, u1, PB[:, 0, e - 1, :], aiB, ALU.mult)
            S.tt(eng, u2, PB[:, 1, e - 1, :], arB, ALU.mult)
            S.tt(eng, PB[:, 1, e, :], u1, u2, ALU.add)
        S.copy(eng, lam_rr[:, :, 0], PB[:, 0, 8, :])
        S.copy(eng, lam_rr[:, :, 1], PB[:, 0, 8, :])
        S.copy(eng, lam_is[:, :, 0], PB[:, 1, 8, :])
        S.ts(eng, lam_is[:, :, 1], PB[:, 1, 8, :], -1.0, ALU.mult)

    def s5_prep_batched():
        eng = 'dve'
        Win = aview(8192, [128, 4, 2, 8, 2, 128], BF16)
        T_ = [aview(40960 + 2048 * i, [128, 4, 2, 64], F32) for i in range(11)]
        lamA = aview(63488, [128, 2, 4, 2, 64], F32)
        bA = aview(67584, [128, 2, 4, 2, 64], F32)
        S.dma('sp', lamA, sA_lam_d)
        S.dma('sp', bA, sA_b_d)
        dt_b = _ap(dtA, 0, [dtA.ap[0], [2, 4], [1, 2], [0, 64]])
        arA, aiA, crA, ciA = disc(eng, lamA, dt_b, T_, 512)
        bre = bA[:, 0]
        bim = bA[:, 1]
        bbr, bbi, ua, ub = T_[4], T_[5], T_[6], T_[7]
        S.tt(eng, ua, crA, bre, ALU.mult)
        S.tt(eng, ub, ciA, bim, ALU.mult)
        S.tt(eng, bbr, ua, ub, ALU.subtract)
        S.tt(eng, ua, crA, bim, ALU.mult)
        S.tt(eng, ub, ciA, bre, ALU.mult)
        S.tt(eng, bbi, ua, ub, ALU.add)
        Wr = [bbr, T_[8]]
        Wi = [bbi, T_[9]]
        for e in range(8):
            cr_, ci_ = Wr[e % 2], Wi[e % 2]
            if e > 0:
                pr_, pi_ = Wr[(e - 1) % 2], Wi[(e - 1) % 2]
                S.tt(eng, ua, pr_, arA, ALU.mult)
                S.tt(eng, ub, pi_, aiA, ALU.mult)
                S.tt(eng, cr_, ua, ub, ALU.subtract)
                S.tt(eng, ua, pr_, aiA, ALU.mult)
                S.tt(eng, ub, pi_, arA, ALU.mult)
                S.tt(eng, ci_, ua, ub, ALU.add)
            for gp in range(2):
                S.act(Win[:, :, 0, e, :, gp * 64:(gp + 1) * 64], cr_, AF.Copy, scale=parA[:, gp:gp + 1])
                S.act(Win[:, :, 1, e, :, gp * 64:(gp + 1) * 64], ci_, AF.Copy, scale=parA[:, gp:gp + 1])
        for q in range(4):
            S.dma('sp', WinD[q], Win[:, q].rearrange("p a b c d -> p (a b c d)"))
        Toep = aview(8192, [128, 4, 16, 128], BF16)
        Wo = aview(40960, [128, 32, 2, 8, 32], BF16)
        Wo0 = aview(73728, [128, 32, 2, 32], BF16)
        BbS = aview(77824, [128, 32, 2, 32], BF16)
        cB = aview(81920, [128, 2, 32, 16], F32)
        bB = aview(86016, [128, 2, 32, 16], F32)
        w1b = [aview(90112, [128, 32, 16], F32), aview(92160, [128, 32, 16], F32)]
        w2 = aview(94208, [128, 32, 16], F32)
        Dd = aview(96256, [128, 4, 128], F32)
        S.dma('sp', cB, sB_c_d)
        S.dma('sp', bB, sB_b_d)
        cr_b = _ap(crB, 0, [crB.ap[0], [1, 32], [0, 16]])
        ci_b = _ap(ciB, 0, [ciB.ap[0], [1, 32], [0, 16]])
        S.memset('pool', BbS[:, :, :, :].rearrange("p a b c -> p (a b c)"), 0.0)
        S.memset('pool', Wo[:, :, :, :, :].rearrange("p a b c d -> p (a b c d)"), 0.0)
        S.memset('pool', Wo0[:, :, :, :].rearrange("p a b c -> p (a b c)"), 0.0)
        for ri in range(2):
            wx = w1b[ri]
            if ri == 0:
                S.tt(eng, wx, cr_b, bB[:, 0], ALU.mult)
                S.tt(eng, w2, ci_b, bB[:, 1], ALU.mult)
                S.tt(eng, wx, wx, w2, ALU.subtract)
            else:
                S.tt(eng, wx, cr_b, bB[:, 1], ALU.mult)
                S.tt(eng, w2, ci_b, bB[:, 0], ALU.mult)
                S.tt(eng, wx, wx, w2, ALU.add)
            for gp in range(2):
                ps_ = slice(64 * gp, 64 * gp + 64)
                S.copy('act', BbS[ps_, :, ri, 16 * gp:16 * gp + 16], wx[ps_, :, :])
        for e in range(9):
            pr_b = _ap(PB, (0 * 9 + e) * 32, [PB.ap[0], [1, 32], [0, 16]])
            pi_b = _ap(PB, (1 * 9 + e) * 32, [PB.ap[0], [1, 32], [0, 16]])
            for ri in range(2):
                wx = w1b[ri]
                if ri == 0:
                    S.tt(eng, wx, cB[:, 0], pr_b, ALU.mult)
                    S.tt(eng, w2, cB[:, 1], pi_b, ALU.mult)
                    S.tt(eng, wx, wx, w2, ALU.subtract)
                    sgn = 1.0
                else:
                    S.tt(eng, wx, cB[:, 0], pi_b, ALU.mult)
                    S.tt(eng, w2, cB[:, 1], pr_b, ALU.mult)
                    S.tt(eng, wx, wx, w2, ALU.add)
                    sgn = -1.0
                for gp in range(2):
                    ps_ = slice(64 * gp, 64 * gp + 64)
                    if e == 0:
                        dst = Wo0[ps_, :, ri, 16 * gp:16 * gp + 16]
                    else:
                        dst = Wo[ps_, :, ri, e - 1, 16 * gp:16 * gp + 16]
                    S.act(dst, wx[ps_, :, :], AF.Copy, scale=sgn)
        for q in range(4):
            S.dma('sp', WoutD[q], Wo[:, q * 8:(q + 1) * 8].rearrange("p a b c d -> p (a b c d)"))

        def wo(f, ri, tau):
            return Wo0[:, f, ri, :] if tau == 0 else Wo[:, f, ri, tau - 1, :]

        kk = 0
        for q in range(4):
            S.ts(eng, Dd[:, q, :], ident[:], sdT[:, q:q + 1], ALU.mult)
            for bi in range(4):
                b = P.bank(0, 6)
                S.memset(eng, PS[:, b, :], 0.0)
                for s4 in range(4):
                    slot = bi * 4 + s4
                    if slot > 14:
                        continue
                    if slot < 7:
                        combos = [(0, slot + 1)]
                    elif slot < 14:
                        combos = [(1, slot - 6)]
                    else:
                        combos = [(0, 0), (1, 0)]
                    n_ = len(combos) * 2
                    i_ = 0
                    for (d, tau) in combos:
                        for ri in range(2):
                            for pr in range(4):
                                f = (q * 2 + d) * 4 + pr
                                o = PS[32 * pr:32 * pr + 32, b, s4 * 128 + 32 * pr:s4 * 128 + 32 * pr + 32]
                                S.matmul(o, BbS[:, f, ri, :], wo(f, ri, tau),
                                         start=(i_ == 0), stop=(i_ == n_ - 1), tile_position=(0, 32 * pr))
                            i_ += 1
                if bi < 3:
                    S.copy('act', Toep[:, q, bi * 4:bi * 4 + 4, :], PS[:, b, :].rearrange("p (s c) -> p s c", c=128))
                else:
                    S.copy('act', Toep[:, q, 12:14, :], PS[:, b, 0:256].rearrange("p (s c) -> p s c", c=128))
                    S.tt(eng, Toep[:, q, 14, :], PS[:, b, 256:384], Dd[:, q, :], ALU.add)
            S.dma('sp', ToepD[q], Toep[:, q].rearrange("p a c -> p (a c)"))

    def s5_defs():
        P.Pst = tview(wsl[:, :], 0, [128, 64, 162], F32)

    def s5_V():
        eng = 'dve'
        Pst = P.Pst
        Winb = [aview(8192 * i, [128, 2, 8, 2, 128], BF16) for i in range(2)]
        BT = 102400
        s0tmp = aview(BT + 1024, [128, 64], F32)
        S.dma('sp', s0tmp, s0B_d)
        S.copy(eng, Pst[:, :, 0], s0tmp)
        S.memset(eng, Pst[:, :, 129], 0.0)
        kk = 0
        for q in range(4):
            Win = Winb[q % 2]
            S.dma('sp', Win[:, :, :, :, :].rearrange("p a b c d -> p (a b c d)"), WinD[q])
            for d in range(2):
                for pr in range(4):
                    for ri in range(2):
                        t = q * 16 + d * 8 + pr * 2 + ri
                        b = P.bank(0, 6)
                        rows = slice(32 * pr, 32 * pr + 32)
                        for j in range(8):
                            e = 7 - j if d == 0 else j
                            S.matmul(PS[:, b, 0:160], Win[rows, ri, e, d, :], uTp[rows, q, j, :],
                                     start=(j == 0), stop=(j == 7), tile_position=(32 * pr, 0))
                        pt = Pst[:, t, :]
                        if d == 0:
                            evac(kk, pt[:, 1:129], PS[:, b, 0:128]); kk += 1
                            evac(kk, pt[:, 130:162], PS[:, b, 128:160]); kk += 1
                        else:
                            evac(kk, _ap(pt, 128, [pt.ap[0], [-1, 128]]), PS[:, b, 0:128]); kk += 1
                            evac(kk, _ap(pt, 161, [pt.ap[0], [-1, 32]]), PS[:, b, 128:160]); kk += 1


    def s5_scan_gen():
        eng = 'dve'
        Pst = P.Pst
        BT = 102400
        sc1 = aview(BT, [128, 32, 2, 2], F32)
        sc2 = aview(BT + 512, [128, 32, 2, 2], F32)
        pa = Pst[:, :, :]
        pstep = pa.ap[0]
        for k in range(128):
            if k % 1 == 0 and k > 0:
                yield
            ncol = 2 if k < 32 else 1
            src = _ap(pa, k, [pstep, [324, 32], [162, 2], [129, ncol]])
            dst = _ap(pa, k + 1, [pstep, [324, 32], [162, 2], [129, ncol]])
            if ncol == 1:
                src2 = _ap(pa, k, [pstep, [0, 2], [324, 32], [162, 2]])
                lam2 = _ap(lam_rr, 0, [lam_rr.ap[0], [64, 2], [2, 32], [1, 2]])
                out2 = _ap(sc1, 0, [sc1.ap[0], [128, 2], [4, 32], [2, 2]])
                S.tt(eng, out2, src2, lam2, ALU.mult)
                a1 = _ap(sc1, 0, [sc1.ap[0], [4, 32], [2, 2], [1, 1]])
                a2s = _ap(sc2, 2, [sc2.ap[0], [4, 32], [-2, 2], [1, 1]])
                S.tt(eng, dst, dst, a1, ALU.add)
                S.tt(eng, dst, dst, a2s, ALU.add)
            else:
                lr_b = _ap(lam_rr, 0, [lam_rr.ap[0], [2, 32], [1, 2], [0, ncol]])
                li_b = _ap(lam_is, 0, [lam_is.ap[0], [2, 32], [1, 2], [0, ncol]])
                a1 = sc1[:, :, :, 0:ncol]
                a2 = sc2[:, :, :, 0:ncol]
                a2s = _ap(sc2, 2, [sc2.ap[0], [4, 32], [-2, 2], [1, ncol]])
                S.tt(eng, a1, src, lr_b, ALU.mult)
                S.tt(eng, a2, src, li_b, ALU.mult)
                S.tt(eng, dst, dst, a1, ALU.add)
                S.tt(eng, dst, dst, a2s, ALU.add)
            if (k + 1) % 32 == 0:
                idx = (k + 1) // 32 - 1
                S.copy(eng, Fin[:, idx, :], Pst[:, :, k + 1])
                if k + 1 < 128:
                    S.ts(eng, Pst[:, :, k + 1], Pst[:, :, k + 1], carry, ALU.mult)
                if k == 31:
                    S.copy(eng, Fin[:, 4, :], Pst[:, :, 161])

    def s5_rest():
        eng = 'dve'
        Pst = P.Pst
        ygT = aview(71680, [128, 4, NT], BF16)
        SinA = aview(92160, [128, 32, 160], BF16)
        SinB = aview(51200, [128, 32, 160], BF16)
        Wob = [aview(8192 * i, [128, 2, 4, 2, 8, 32], BF16) for i in range(2)]
        Tpb = [aview(16384 + 4096 * i, [128, 16, 128], BF16) for i in range(2)]
        BT = 102400
        S.dma('sp', so_d.rearrange("i p t -> p i t"), Fin)

        for q in (2, 3, 0, 1):
            Sq = (SinA if q < 2 else SinB)[:, (q % 2) * 16:(q % 2) * 16 + 16, :]
            for d in range(2):
                pqd = Pst[:, q * 16 + d * 8:q * 16 + d * 8 + 8, :]
                sd_ = Sq[:, d * 8:(d + 1) * 8, :]
                if d == 0:
                    S.copy('act', sd_[:, :, 0:128], pqd[:, :, 0:128])
                    S.copy(eng, sd_[:, :, 128:160], pqd[:, :, 129:161])
                else:
                    S.copy('act', sd_[:, :, 0:128], _ap(pqd, 127, [pqd.ap[0], [162, 8], [-1, 128]]))
                    S.copy(eng, sd_[:, :, 128:160], _ap(pqd, 129 + 31, [pqd.ap[0], [162, 8], [-1, 32]]))
        gluW = P.wload(glu_w_d.rearrange("(k p) m -> p k m", p=128))
        for q in range(4):
            Wout = Wob[q % 2]
            Toep = Tpb[q % 2]
            S.dma('sp', Wout[:, :, :, :, :, :].rearrange("p a b c d e -> p (a b c d e)"), WoutD[q])
            S.dma('sp', Toep[:, :, :].rearrange("p a c -> p (a c)"), ToepD[q])
            Sin = (SinA if q < 2 else SinB)[:, (q % 2) * 16:(q % 2) * 16 + 16, :]
            for i in range(8):
                b = P.bank(0, 6)
                o = PS[:, b, 0:160]
                for j in range(8):
                    slot = (i - j - 1) if j < i else ((7 + j - i - 1) if j > i else 14)
                    S.matmul(o, Toep[:, slot, :], uTp[:, q, j, :], start=(j == 0), stop=False)
                i_ = 0
                for d in range(2):
                    e = i + 1 if d == 0 else 8 - i
                    for ri in range(2):
                        for pr in range(4):
                            i_ += 1
                            S.matmul(PS[32 * pr:32 * pr + 32, b, 0:160], Wout[:, d, pr, ri, e - 1, :],
                                     Sin[:, d * 8 + pr * 2 + ri, :], start=False, stop=(i_ > 12),
                                     tile_position=(0, 32 * pr))
                yq = ygT[:, q, :]
                S.act(_ap(yq, i, [yq.ap[0], [8, 160]]), o, AF.Gelu)
        mark('s5_y')
        sg = [aview(BT + 1024, [128, 512], BF16), aview(BT + 2048, [128, 512], BF16)]
        k2 = 0
        for m in range(4):
            for (s_, n) in TB:
                b = P.bank(0, 6)
                for kc in range(4):
                    S.matmul(PS[:, b, 0:n], gluW[:, kc, m * 128:(m + 1) * 128], ygT[:, kc, s_:s_ + n],
                             start=(kc == 0), stop=(kc == 3))
                t = sg[k2 % 2][:, 0:n]
                k2 += 1
                S.act(t, PS[:, b, 0:n], AF.Sigmoid, bias=glubT[:, m:m + 1])
                S.tt('dve', catT[:, 4 + m, s_:s_ + n], ygT[:, m, s_:s_ + n], t, ALU.mult)

    def mixer_even():
        projections()
        mark('proj')
        P.scan_gen = None
        if cfg['s5']:
            s5_defs()
            s5_V()
            mark('s5_V')
            P.scan_gen = s5_scan_gen()
        if cfg['attn']:
            attention()
        else:
            S.memset('dve', catT[:, 0:4, :], 0.0)
        if P.scan_gen is not None:
            for _ in P.scan_gen:
                pass
            P.scan_gen = None
        mark('attn')
        if cfg['s5']:
            s5_rest()
        else:
            S.memset('dve', catT[:, 4:8, :], 0.0)
        mark('s5')
        out_proj_even()
        mark('wout')

    def input_transposes():
        xstage = [aview(0, [128, D], F32), aview(4096, [128, D], F32),
                  aview(98304, [128, D], F32), aview(102400, [128, D], F32)]
        for tt in range(10):
            st = xstage[tt % 4]
            S.dma('sp', st, xin[tt * 128:(tt + 1) * 128, :])
            for half in range(2):
                b = P.bank(0, 4)
                for c4 in range(4):
                    c = half * 4 + c4
                    S.transpose(PS[:, b, c4 * 128:(c4 + 1) * 128], st[:, c * 128:(c + 1) * 128], ident[:])
                src = PS[:, b, :].rearrange("p (c t) -> p c t", t=128)
                dst = xT[:, half * 4:half * 4 + 4, tt * 128:(tt + 1) * 128]
                S.copy('act' if (tt + half) % 2 == 0 else 'dve', dst, src)


    P.marks = []

    def mark(name):
        P.marks.append((name, sum(1 for o in S.ops if o.eng == 'pe'), sum(1 for o in S.ops if o.eng == 'dve'),
                        sum(1 for o in S.ops if o.eng == 'act')))
    if cfg['s5']:
        s5_small()
    input_transposes()
    mark('xin')
    P.s5prepA = None
    P.s5prepB = None
    g0_ = modulation_gen(0)
    if cfg['s5']:
        for _ in range(12):
            next(g0_)
        P.mod1_gen = modulation_gen(1)
        for _ in range(12):
            next(P.mod1_gen)
        s5_prep_batched()
    for _ in g0_:
        pass
    if getattr(P, 'mod1_gen', None) is not None:
        for _ in P.mod1_gen:
            pass
        P.mod1_gen = 'done'
    mark('mod0')
    for layer in range(2):
        if layer == 1 and cfg['fnet']:
            fnet_mixer(1)
            mark('fnet')
        if layer == 0 and (cfg['attn'] or cfg['s5']):
            mixer_even()
        if layer == 0:
            g_ = getattr(P, 'mod1_gen', None)
            if g_ is None:
                g_ = modulation_gen(1)
            if g_ != 'done':
                for _ in g_:
                    pass
            mark('mod1')
        if cfg['ffn']:
            norm_mod(layer, 1)
            ffn(layer)
            mark('ffn%d' % layer)

    sumsq_rstd()
    for c in range(8):
        S.stt(xT[:, c, :], xT[:, c, :], gvec[:, 4, c:c + 1], rstd[:, :], ALU.mult, ALU.mult)
    ystage = [aview(0, [128, D], F32), aview(4096, [128, D], F32)]
    for tt in range(10):
        st = ystage[tt % 2]
        for half in range(2):
            b = P.bank(0, 4)
            for c4 in range(4):
                c = half * 4 + c4
                S.transpose(PS[:, b, c4 * 128:(c4 + 1) * 128], xT[:, c, tt * 128:(tt + 1) * 128], ident[:])
            S.copy('act' if (tt + half) % 2 == 0 else 'dve', st[:, half * 512:(half + 1) * 512], PS[:, b, :])
        S.dma('sp', y_d[tt * 128:(tt + 1) * 128, :], st)

    S.emit(es)
    P.es.close()
    return P


def _core_tokens(c, x_prompt, x_sample):
    if c < 2:
        return np.concatenate([x_sample[c], x_prompt[c]], 0)
    s = 2 + 5 * (c - 2)
    return x_prompt[s:s + 5].reshape(NT, D)


def _fm(v, nch):
    return np.ascontiguousarray(v.reshape(nch, 128).T)


def make_in_maps(inp, cores):
    f32 = np.float32
    shared = {}
    shared['bmodT'] = np.ascontiguousarray(np.stack([_fm(inp['b_mod'][l], 48) for l in range(2)], 1)).astype(f32)
    shared['gvec'] = np.ascontiguousarray(np.stack([
        _fm(inp['norm_mix_g'][0], 8), _fm(inp['norm_ffn_g'][0], 8),
        _fm(inp['norm_mix_g'][1], 8), _fm(inp['norm_ffn_g'][1], 8),
        _fm(inp['final_norm_g'], 8)], 1)).astype(f32)
    shared['ident'] = np.eye(128, dtype=f32)
    for k in ('w_mod', 'ffn_w_gate', 'ffn_w_up', 'ffn_w_down'):
        shared[k] = np.ascontiguousarray(inp[k])
    shared['w_in_odd'] = np.ascontiguousarray(inp['w_in_odd'][0])
    shared['w_out_odd'] = np.ascontiguousarray(inp['w_out_odd'][0])
    bf = ml_dtypes.bfloat16
    ang = 2 * np.pi * np.outer(np.arange(256), np.arange(256)) / 256.0
    shared['cs256'] = np.concatenate([np.cos(ang) / 16.0, -np.sin(ang) / 16.0], 1).astype(bf)
    shared['dft4'] = np.stack([np.cos(ang) / 16.0, np.sin(ang) / 16.0], 0).astype(bf)
    angL = 2 * np.pi * (np.outer(np.arange(1024), np.arange(1024)) % 1024) / 1024.0
    dft_sample = np.stack([np.cos(angL) / 32.0, np.sin(angL) / 32.0], 0).astype(bf)
    dft_prompt = np.zeros((2, 1024, 1024), np.float32)
    for i in range(4):
        dft_prompt[0, i * 256:(i + 1) * 256, i * 256:(i + 1) * 256] = np.cos(ang) / 16.0
        dft_prompt[1, i * 256:(i + 1) * 256, i * 256:(i + 1) * 256] = np.sin(ang) / 16.0
    dft_prompt = dft_prompt.astype(bf)
    KBh = [[0, 1, 2, 3], [0, 1, 2, 3], [0, 1, 2, 3, 4], [1, 2, 3, 4, 5], [2, 3, 4, 5, 6], [3, 4, 5, 6, 7],
           [4, 5, 6, 7], [4, 5, 6, 7]]
    kr_ = np.arange(2)[:, None].repeat(64, 1).reshape(128)
    kc_ = np.arange(64)[None, :].repeat(2, 0).reshape(128)
    mask_s = np.zeros((128, 40, 128), np.float32)
    mask_p = np.zeros((128, 40, 128), np.float32)
    UNh = [[0, 1, 2, 3], [0, 1, 2, 3, 4, 5], [2, 3, 4, 5, 6, 7], [4, 5, 6, 7]]
    mi = 0
    for mp in range(4):
        for n in UNh[mp]:
            for mm in range(2):
                m = 2 * mp + mm
                qrow = 2 * m + kr_[None, :]; qcol = kc_[None, :]
                krow = 2 * n + kr_[:, None]; kcol = kc_[:, None]
                rs = np.clip(qrow - 4, 0, 8); cs = np.clip(qcol - 8, 0, 48)
                ok = (krow >= rs) & (krow < rs + 8) & (kcol >= cs) & (kcol < cs + 16)
                mask_s[:, mi, :] = np.where(ok, 0.0, NEG)
                mask_p[:, mi, :] = 0.0 if (n // 2 == m // 2) else NEG
                mi += 1
    mask_s = mask_s.astype(bf); mask_p = mask_p.astype(bf)
    R_ = inp['na_rpb'][0]
    rpbH = np.zeros((8, 19, 128), f32)
    rpbH[:, 2:17, 48:79] = R_
    shared['w_in_even'] = np.ascontiguousarray(inp['w_in_even'][0])
    shared['w_out_even'] = np.ascontiguousarray(inp['w_out_even'][0])
    lam = np.stack([inp['s5_lam_re'][0], inp['s5_lam_im'][0]], 0)
    bb = np.stack([inp['s5_b_re'][0], inp['s5_b_im'][0]], 0)
    cc = np.stack([inp['s5_c_re'][0], inp['s5_c_im'][0]], 0)
    ls = inp['s5_log_step'][0]
    lam_q = lam.reshape(2, 2, 4, 8, 64)
    sA_lam = np.broadcast_to(lam_q.transpose(3, 0, 2, 1, 4)[:, None], (8, 16, 2, 4, 2, 64)).reshape(128, 2, 4, 2, 64)
    shared['sA_lam'] = np.ascontiguousarray(sA_lam).astype(f32)
    ls_q = ls.reshape(2, 4, 8)
    shared['sA_ls'] = np.ascontiguousarray(
        np.broadcast_to(ls_q.transpose(2, 1, 0)[:, None], (8, 16, 4, 2)).reshape(128, 4, 2)).astype(f32)
    bq = bb.reshape(2, 2, 4, 8, 64, 16)
    shared['sA_b'] = np.ascontiguousarray(bq.transpose(3, 5, 0, 2, 1, 4).reshape(128, 2, 4, 2, 64)).astype(f32)
    par = np.zeros((128, 2), f32)
    gpar = (np.arange(128) // 16) % 2
    par[gpar == 0, 0] = 1.0
    par[gpar == 1, 1] = 1.0
    shared['parA'] = par
    lam_s = lam.reshape(2, 2, 4, 4, 2, 64)
    shared['sB_lam'] = np.ascontiguousarray(lam_s.transpose(4, 5, 0, 2, 1, 3).reshape(128, 2, 32)).astype(f32)
    ls_s = ls.reshape(2, 4, 4, 2)
    shared['sB_ls'] = np.ascontiguousarray(
        np.broadcast_to(ls_s.transpose(3, 1, 0, 2)[:, None], (2, 64, 4, 2, 4)).reshape(128, 32)).astype(f32)
    c_s = cc.reshape(2, 2, 4, 4, 2, 16, 64)
    shared['sB_c'] = np.ascontiguousarray(c_s.transpose(4, 6, 0, 2, 1, 3, 5).reshape(128, 2, 32, 16)).astype(f32)
    b_s = bb.reshape(2, 2, 4, 4, 2, 64, 16)
    shared['sB_b'] = np.ascontiguousarray(b_s.transpose(4, 5, 0, 2, 1, 3, 6).reshape(128, 2, 32, 16)).astype(f32)
    shared['sdT'] = _fm(inp['s5_d'][0], 4).astype(f32)
    shared['glubT'] = _fm(inp['s5_glu_b'][0], 4).astype(f32)
    shared['s5_glu_w'] = np.ascontiguousarray(inp['s5_glu_w'][0])
    maps = []
    for c in cores:
        m = dict(shared)
        m['xin'] = np.ascontiguousarray(_core_tokens(c, inp['x_prompt'], inp['x_sample']))
        cond_long = inp['c'][c] if c < 2 else inp['c_ctx']
        cond = np.stack([cond_long, inp['c_ctx']], 0)
        m['condT'] = np.ascontiguousarray(cond.reshape(2, 8, 128).transpose(2, 1, 0)).astype(f32)
        m['dftL'] = dft_sample if c < 2 else dft_prompt
        if c < 2:
            m['ctxkT'] = np.ascontiguousarray(inp['cache_na_k'][c, 0].reshape(512, 512).T)
            m['ctxv'] = np.ascontiguousarray(inp['cache_na_v'][c, 0].reshape(512, 512))
            m['ctxbias'] = np.zeros((128, 1), f32)
            m['maskt'] = mask_s
            m['rpbH'] = rpbH
            st = inp['state_s5'][c, 0].reshape(2, 2, 4, 4, 2, 64)
            m['s0B'] = np.ascontiguousarray(st.transpose(4, 5, 2, 0, 3, 1).reshape(128, 64)).astype(f32)
            m['carry'] = np.ones((128, 1), f32)
        else:
            m['ctxkT'] = np.zeros((512, 512), f32)
            m['ctxv'] = np.zeros((512, 512), f32)
            m['ctxbias'] = np.full((128, 1), NEG, f32)
            m['maskt'] = mask_p
            m['rpbH'] = np.zeros((8, 19, 128), f32)
            m['s0B'] = np.zeros((128, 64), f32)
            m['carry'] = np.zeros((128, 1), f32)
        maps.append(m)
    return maps


_PROG = {}


def run_cores(inp, cores, cfg=None):
    cfg = dict(CFG) if cfg is None else cfg
    key = tuple(sorted(cfg.items()))
    if key not in _PROG:
        _PROG[key] = build_program(cfg)
    P = _PROG[key]
    maps = make_in_maps(inp, cores)
    maps = [{k: v for k, v in m.items() if k in P.ins} for m in maps]
    res = run_bass_kernel_spmd(P.nc, maps, core_ids=list(range(len(cores))))
    return res.results


def kernel(**inputs):
    inp = {k: np.asarray(v) for k, v in inputs.items()}
    res = run_cores(inp, list(range(8)))
    y_prompt = np.zeros((32, 256, D), np.float32)
    y_sample = np.zeros((2, 1024, D), np.float32)
    for c in range(8):
        y = res[c]['y']
        if c < 2:
            y_sample[c] = y[0:1024]
            y_prompt[c] = y[1024:1280]
        else:
            s = 2 + 5 * (c - 2)
            y_prompt[s:s + 5] = y.reshape(5, 256, D)
    nk = np.zeros((32, 1, 256, 8, 64), np.float32)
    nv = np.zeros((32, 1, 256, 8, 64), np.float32)
    for c in range(8):
        for nm, dst in (('ko', nk), ('vo', nv)):
            a = res[c][nm]
            if c < 2:
                dst[c, 0] = a[1024:1280].reshape(256, 8, 64)
            else:
                s0 = 2 + 5 * (c - 2)
                dst[s0:s0 + 5, 0] = a.reshape(5, 256, 8, 64)
    ns = np.zeros((32, 1, 2, 2, 32, 64), np.float32)
    for c in range(8):
        so = res[c]['so']
        so = so.reshape(5, 2, 64, 4, 2, 4, 2)
        st = so.transpose(0, 4, 6, 3, 5, 1, 2).reshape(5, 2, 2, 32, 64)
        if c < 2:
            ns[c, 0] = st[4]
        else:
            s0 = 2 + 5 * (c - 2)
            for j in range(4):
                ns[s0 + j, 0, 0] = st[j, 0]
                ns[s0 + j, 0, 1] = st[3 - j, 1]
            ns[s0 + 4, 0] = st[4]
    return (y_prompt, y_sample, nk, nv, ns)
```
